# Optimizing a Trainium2 kernel written in Bass

```python
import jax, jax.numpy as jnp
from jax import lax
import numpy as np

D_MODEL = 1024
BATCH = 4
SEQ = 8192
DEPTH = 4

N_EVEN = (DEPTH + 1) // 2
N_ODD = DEPTH // 2
EPS = 1e-6

GDN_HEADS = D_MODEL // 256
GDN_DK = 128
GDN_DV = 128
GDN_CONV = 4
GDN_CHUNK = 64
SB_HEADS = D_MODEL // 128
SB_DH = 64
SB_BLOCK = 128
RET_HEADS = D_MODEL // 256
RET_DK = 256
RET_DV = 512
RET_CHUNK = 128
ROPE_BASE = 10000.0
FFN_HIDDEN = ((8 * D_MODEL // 3 + 255) // 256) * 256

GDN_QK = GDN_HEADS * GDN_DK
GDN_VW = GDN_HEADS * GDN_DV
SB_W = SB_HEADS * SB_DH
HYB_SPLITS = (GDN_QK, GDN_QK, GDN_VW, GDN_VW, GDN_HEADS, GDN_HEADS, SB_W, SB_W, SB_W)
HYB_IN = sum(HYB_SPLITS)
CONV_CH = 2 * GDN_QK + GDN_VW
MIX_W = GDN_VW + SB_W
RET_QK = RET_HEADS * RET_DK
RET_VW = RET_HEADS * RET_DV
RET_SPLITS = (RET_QK, RET_QK, RET_VW, RET_VW)
RET_IN = sum(RET_SPLITS)

kernel_name = "hybrid_gdn_stickbreak_retention_adaln"


def _split(x, sizes):
    out, start = [], 0
    for s in sizes:
        out.append(x[..., start:start + s])
        start += s
    return out


def _heads(x, n_heads):
    b, t, _ = x.shape
    return x.reshape(b, t, n_heads, -1).transpose(0, 2, 1, 3)


def _merge(x):
    b, h, t, d = x.shape
    return x.transpose(0, 2, 1, 3).reshape(b, t, h * d)


def rms_norm(x, gain=None):
    xf = x.astype(jnp.float32)
    y = xf * lax.rsqrt(jnp.mean(xf * xf, axis=-1, keepdims=True) + EPS)
    if gain is not None:
        y = y * gain.astype(jnp.float32)
    return y.astype(x.dtype)


def l2_norm(x):
    return x * lax.rsqrt(jnp.sum(x * x, axis=-1, keepdims=True) + EPS)


def causal_short_conv(x, w):
    k_w = w.shape[0]
    t = x.shape[1]
    xp = jnp.pad(x, ((0, 0), (k_w - 1, 0), (0, 0)))
    y = sum(xp[:, j:j + t] * w[j] for j in range(k_w))
    return jax.nn.silu(y)


def rotary(x, pos):
    d = x.shape[-1]
    inv = 1.0 / (ROPE_BASE ** (jnp.arange(0, d, 2, dtype=jnp.float32) / d))
    ang = pos[:, None] * inv[None, :]
    cos, sin = jnp.cos(ang), jnp.sin(ang)
    x1, x2 = x[..., : d // 2], x[..., d // 2:]
    return jnp.concatenate([x1 * cos - x2 * sin, x1 * sin + x2 * cos], axis=-1)


def gated_delta_rule(q, k, v, g, beta):
    b_, h, t, dk = q.shape
    dv = v.shape[-1]
    c = GDN_CHUNK
    n = t // c
    q = (q * dk ** -0.5).reshape(b_, h, n, c, dk)
    k = k.reshape(b_, h, n, c, dk)
    v = v.reshape(b_, h, n, c, dv)
    beta = beta.reshape(b_, h, n, c)
    g = lax.cumsum(g.reshape(b_, h, n, c), axis=3)
    idx = jnp.arange(c)
    strict = idx[:, None] > idx[None, :]
    causal = idx[:, None] >= idx[None, :]
    gdiff = g[..., :, None] - g[..., None, :]
    dec_strict = jnp.exp(jnp.where(strict, gdiff, -jnp.inf))
    dec_causal = jnp.exp(jnp.where(causal, gdiff, -jnp.inf))
    kb = k * beta[..., None]
    lower = jnp.einsum('bhnid,bhnjd->bhnij', kb, k) * dec_strict
    tri = jnp.eye(c, dtype=lower.dtype) + lower
    rhs = jnp.concatenate([v * beta[..., None], kb * jnp.exp(g)[..., None]], axis=-1)
    sol = lax.linalg.triangular_solve(tri, rhs, left_side=True, lower=True, unit_diagonal=True)
    u, w = sol[..., :dv], sol[..., dv:]
    qk_intra = jnp.einsum('bhnid,bhnjd->bhnij', q, k) * dec_causal
    g_last = g[..., -1]
    q_dec = q * jnp.exp(g)[..., None]
    k_dec = k * jnp.exp(g_last[..., None] - g)[..., None]
    xs = tuple(jnp.moveaxis(a, 2, 0) for a in (u, w, qk_intra, q_dec, k_dec, g_last))

    def step(state, inp):
        u_n, w_n, a_n, q_n, k_n, gl_n = inp
        v_new = u_n - jnp.einsum('bhcd,bhde->bhce', w_n, state)
        o_n = jnp.einsum('bhcd,bhde->bhce', q_n, state) + jnp.einsum('bhij,bhje->bhie', a_n, v_new)
        state = state * jnp.exp(gl_n)[..., None, None] + jnp.einsum('bhcd,bhce->bhde', k_n, v_new)
        return state, o_n

    s0 = jnp.zeros((b_, h, dk, dv), q.dtype)
    _, o = lax.scan(step, s0, xs)
    return jnp.moveaxis(o, 0, 2).reshape(b_, h, t, dv)


def stick_breaking_attention(q, k, v):
    b_, h, t, d = q.shape
    nb = t // SB_BLOCK
    scale = d ** -0.5
    qb = jnp.moveaxis(q.reshape(b_, h, nb, SB_BLOCK, d), 2, 0)
    kpos = jnp.arange(t)

    def block(args):
        q_i, i = args
        z = jnp.einsum('bhqd,bhkd->bhqk', q_i, k) * scale
        qpos = i * SB_BLOCK + jnp.arange(SB_BLOCK)
        mask = kpos[None, :] < qpos[:, None]
        log_fail = jnp.where(mask, jax.nn.log_sigmoid(-z), 0.0)
        later = lax.cumsum(log_fail, axis=3, reverse=True) - log_fail
        a = jnp.where(mask, jnp.exp(jax.nn.log_sigmoid(z) + later), 0.0)
        return jnp.einsum('bhqk,bhkd->bhqd', a, v)

    o = lax.map(block, (qb, jnp.arange(nb)))
    return jnp.moveaxis(o, 0, 2).reshape(b_, h, t, d)


def retention_chunkwise(q, k, v):
    b_, h, t, dk = q.shape
    dv = v.shape[-1]
    c = RET_CHUNK
    n = t // c
    lg = jnp.log1p(-jnp.exp2(-5.0 - jnp.arange(h, dtype=jnp.float32)))
    idx = jnp.arange(c, dtype=jnp.float32)
    diff = idx[:, None] - idx[None, :]
    dmat = jnp.exp(jnp.where(diff >= 0, diff[None] * lg[:, None, None], -jnp.inf))
    qc = q.reshape(b_, h, n, c, dk)
    kc = k.reshape(b_, h, n, c, dk)
    vc = v.reshape(b_, h, n, c, dv)
    intra = jnp.einsum('bhnid,bhnjd->bhnij', qc, kc) * dmat[None, :, None]
    o_intra = jnp.einsum('bhnij,bhnje->bhnie', intra, vc)
    q_dec = qc * jnp.exp(lg[:, None] * (idx + 1.0)[None, :])[None, :, None, :, None]
    k_dec = kc * jnp.exp(lg[:, None] * (c - 1.0 - idx)[None, :])[None, :, None, :, None]
    chunk_decay = jnp.exp(lg * c)[None, :, None, None]
    xs = tuple(jnp.moveaxis(a, 2, 0) for a in (q_dec, k_dec, vc))

    def step(state, inp):
        q_n, k_n, v_n = inp
        o_n = jnp.einsum('bhcd,bhde->bhce', q_n, state)
        state = state * chunk_decay + jnp.einsum('bhcd,bhce->bhde', k_n, v_n)
        return state, o_n

    s0 = jnp.zeros((b_, h, dk, dv), q.dtype)
    _, o_cross = lax.scan(step, s0, xs)
    o = o_intra + jnp.moveaxis(o_cross, 0, 2)
    return o.reshape(b_, h, t, dv)


def hybrid_mixer(h, w_in, conv_w, a_log, dt_bias, gdn_norm, sb_q_norm, sb_k_norm, w_out):
    f32 = jnp.float32
    proj = h @ w_in
    gq, gk, gv, gate, ga, gb, sq, sk, sv = _split(proj, HYB_SPLITS)
    qkv = causal_short_conv(jnp.concatenate([gq, gk, gv], axis=-1), conv_w).astype(f32)
    cq, ck, cv = _split(qkv, (GDN_QK, GDN_QK, GDN_VW))
    qa = l2_norm(_heads(cq, GDN_HEADS))
    ka = l2_norm(_heads(ck, GDN_HEADS))
    va = _heads(cv, GDN_HEADS)
    g = (-jnp.exp(a_log.astype(f32)) * jax.nn.softplus(ga.astype(f32) + dt_bias.astype(f32))).transpose(0, 2, 1)
    beta = jax.nn.sigmoid(gb.astype(f32)).transpose(0, 2, 1)
    o_a = gated_delta_rule(qa, ka, va, g, beta)
    o_a = rms_norm(o_a, gdn_norm) * jax.nn.silu(_heads(gate, GDN_HEADS).astype(f32))
    qb = rms_norm(_heads(sq, SB_HEADS).astype(f32), sb_q_norm)
    kb = rms_norm(_heads(sk, SB_HEADS).astype(f32), sb_k_norm)
    vb = _heads(sv, SB_HEADS).astype(f32)
    o_b = stick_breaking_attention(qb, kb, vb)
    o = jnp.concatenate([_merge(o_a), _merge(o_b)], axis=-1).astype(h.dtype)
    return o @ w_out


def retention_mixer(h, w_in, w_out):
    f32 = jnp.float32
    proj = h @ w_in
    rq, rk, rv, rg = _split(proj, RET_SPLITS)
    pos = jnp.arange(h.shape[1], dtype=f32)
    q = rotary(_heads(rq, RET_HEADS).astype(f32), pos)
    k = rotary(_heads(rk, RET_HEADS).astype(f32), pos) * RET_DK ** -0.5
    v = _heads(rv, RET_HEADS).astype(f32)
    o = retention_chunkwise(q, k, v)
    o = rms_norm(o) * jax.nn.silu(_heads(rg, RET_HEADS).astype(f32))
    return _merge(o).astype(h.dtype) @ w_out


def swiglu(h, w_in, w_out):
    gu = h @ w_in
    g, u = gu[..., :FFN_HIDDEN], gu[..., FFN_HIDDEN:]
    return (jax.nn.silu(g) * u) @ w_out


def setup_inputs(seed: int = 0) -> dict:
    key = jax.random.key(seed)
    ks = jax.random.split(key, 20)
    f32 = jnp.float32

    def nrm(k, shape, scale):
        return jax.random.normal(k, shape, f32) * scale

    dt = jnp.exp(jax.random.uniform(ks[8], (N_EVEN, GDN_HEADS), f32, np.log(1e-3), np.log(1e-1)))
    return {
        "x": nrm(ks[0], (BATCH, SEQ, D_MODEL), 1.0),
        "c": nrm(ks[1], (BATCH, D_MODEL), 1.0),
        "ada_w": nrm(ks[2], (DEPTH, D_MODEL, 6 * D_MODEL), 0.5 * D_MODEL ** -0.5),
        "ada_b": nrm(ks[3], (DEPTH, 6 * D_MODEL), 0.02),
        "norm_mix": 1.0 + nrm(ks[4], (DEPTH, D_MODEL), 0.1),
        "norm_ffn": 1.0 + nrm(ks[5], (DEPTH, D_MODEL), 0.1),
        "hyb_w_in": nrm(ks[6], (N_EVEN, D_MODEL, HYB_IN), D_MODEL ** -0.5),
        "hyb_conv": nrm(ks[7], (N_EVEN, GDN_CONV, CONV_CH), GDN_CONV ** -0.5),
        "gdn_a_log": jnp.log(jax.random.uniform(ks[9], (N_EVEN, GDN_HEADS), f32, 1.0, 16.0)),
        "gdn_dt_bias": dt + jnp.log(-jnp.expm1(-dt)),
        "gdn_norm": 1.0 + nrm(ks[10], (N_EVEN, GDN_DV), 0.1),
        "sb_q_norm": 1.0 + nrm(ks[11], (N_EVEN, SB_DH), 0.1),
        "sb_k_norm": 1.0 + nrm(ks[12], (N_EVEN, SB_DH), 0.1),
        "hyb_w_out": nrm(ks[13], (N_EVEN, MIX_W, D_MODEL), MIX_W ** -0.5),
        "ret_w_in": nrm(ks[14], (N_ODD, D_MODEL, RET_IN), D_MODEL ** -0.5),
        "ret_w_out": nrm(ks[15], (N_ODD, RET_VW, D_MODEL), RET_VW ** -0.5),
        "ffn_w_in": nrm(ks[16], (DEPTH, D_MODEL, 2 * FFN_HIDDEN), D_MODEL ** -0.5),
        "ffn_w_out": nrm(ks[17], (DEPTH, FFN_HIDDEN, D_MODEL), FFN_HIDDEN ** -0.5),
    }


def reference(x, c, ada_w, ada_b, norm_mix, norm_ffn, hyb_w_in, hyb_conv, gdn_a_log,
              gdn_dt_bias, gdn_norm, sb_q_norm, sb_k_norm, hyb_w_out, ret_w_in, ret_w_out,
              ffn_w_in, ffn_w_out):
    c_act = jax.nn.silu(c)
    for l in range(DEPTH):
        mod = (c_act @ ada_w[l] + ada_b[l])[:, None, :]
        sh_m, sc_m, gt_m, sh_f, sc_f, gt_f = jnp.split(mod, 6, axis=-1)
        h = rms_norm(x, norm_mix[l]) * (1 + sc_m) + sh_m
        i = l // 2
        if l % 2 == 0:
            y = hybrid_mixer(h, hyb_w_in[i], hyb_conv[i], gdn_a_log[i], gdn_dt_bias[i],
                             gdn_norm[i], sb_q_norm[i], sb_k_norm[i], hyb_w_out[i])
        else:
            y = retention_mixer(h, ret_w_in[i], ret_w_out[i])
        x = x + gt_m * y
        h = rms_norm(x, norm_ffn[l]) * (1 + sc_f) + sh_f
        x = x + gt_f * swiglu(h, ffn_w_in[l], ffn_w_out[l])
    return x
```

```python
import numpy as np
import concourse.bass as bass
import concourse.mybir as mybir

F32 = mybir.dt.float32
BF16 = mybir.dt.bfloat16
AF = mybir.ActivationFunctionType
ALU = mybir.AluOpType
AX = mybir.AxisListType

EPOCH = 30000
COMPUTE = ('pe', 'act', 'dve', 'pool')
ALLQ = ('pe', 'act', 'dve', 'pool', 'sp')


class Buf:
    __slots__ = ('ap', 'name', 'last_w', 'readers', 'sem', 'cnt', 'space')

    def __init__(self, ap, name='', space='sb'):
        self.ap = ap
        self.name = name
        self.last_w = None
        self.readers = []
        self.sem = None
        self.cnt = 0
        self.space = space

    def __getitem__(self, idx):
        return View(self, self.ap[idx])

    @property
    def v(self):
        return View(self, self.ap)


class View:
    __slots__ = ('buf', 'ap')

    def __init__(self, buf, ap):
        self.buf = buf
        self.ap = ap

    def __getitem__(self, idx):
        return View(self.buf, self.ap[idx])

    def re(self, pat, **kw):
        return View(self.buf, self.ap.rearrange(pat, **kw))

    def bitcast(self, dt):
        return View(self.buf, self.ap.bitcast(dt))


class Op:
    __slots__ = ('eng', 'fn', 'deps', 'signal', 'n', 'is_dma', 'slot', 'val', 'ndma', 'tag', 'sem')

    def __init__(self, eng, fn, is_dma=False, slot=None, ndma=0, tag=''):
        self.eng = eng
        self.fn = fn
        self.deps = ()
        self.signal = False
        self.n = -1
        self.is_dma = is_dma
        self.slot = slot
        self.val = 0
        self.ndma = ndma
        self.tag = tag


def _bufs(xs):
    out = []
    for x in xs:
        if x is None:
            continue
        if isinstance(x, View):
            out.append(x.buf)
        elif isinstance(x, Buf):
            out.append(x)
        elif isinstance(x, (list, tuple)):
            out.extend(_bufs(x))
    return out


class Kern:
    def __init__(self, nc):
        self.nc = nc
        self.ops = {q: [] for q in ALLQ}
        self.all_ops = []
        self.sb_off = 0
        self.sb_names = 0
        self.last_dma_by_slot = {}
        import contextlib
        self.stacks = [contextlib.ExitStack()]
        self.free_sems = {}
        self.marks = []
        self.phase_slots = [[]]

    def sb(self, shape, dtype, name=None):
        self.sb_names += 1
        nm = f"sb{self.sb_names}_{name or ''}"
        cm = self.nc.sbuf_tensor(nm, list(shape), dtype)
        t = self.stacks[-1].enter_context(cm)
        return Buf(t.ap(), nm, 'sb')

    def phase_begin(self):
        import contextlib
        self.stacks.append(contextlib.ExitStack())
        self.phase_slots.append([])

    def phase_end(self):
        self.marks.append({q: sum(1 for o in self.ops[q] if o.fn is not None and not o.is_dma) for q in COMPUTE})
        self.barrier()
        self.stacks.pop().close()
        for slot, q in self.phase_slots.pop():
            self.free_sems.setdefault(q, []).append((slot.sem[q], slot.cnt[q]))

    def add(self, eng, fn, reads=(), writes=(), is_dma=False, slot=None, ndma=0, tag=''):
        op = Op(eng, fn, is_dma, slot, ndma, tag)
        R = _bufs(reads)
        W = _bufs(writes)
        deps = set()
        for b in R:
            if b.last_w is not None:
                deps.add(b.last_w)
            if b.space == 'ps':
                for r in b.readers:
                    if r.eng != eng:
                        deps.add(r)
        for b in W:
            if b.last_w is not None:
                deps.add(b.last_w)
            deps.update(b.readers)
        deps.discard(op)
        op.deps = tuple(deps)
        for b in W:
            b.last_w = op
            b.readers = []
        for b in R:
            if b in W:
                continue
            if not is_dma:
                b.readers = [r for r in b.readers if r.is_dma or r.eng != eng]
            b.readers.append(op)
        if is_dma:
            slot.cnt[eng] += ndma
            op.val = 16 * slot.cnt[eng]
            op.sem = slot.sem[eng]
            assert op.val < 60000, f"dma sem overflow on {slot.name}"
            self.last_dma_by_slot[(id(slot), eng)] = op
        self.ops[eng].append(op)
        self.all_ops.append(op)
        return op

    def barrier(self):
        lasts = []
        for q in COMPUTE:
            for o in reversed(self.ops[q]):
                if not o.is_dma and o.fn is not None:
                    lasts.append(o)
                    break
        lasts.extend(self.last_dma_by_slot.values())
        self.last_dma_by_slot = {}
        for q in ALLQ:
            op = Op(q, None)
            op.deps = tuple(lasts)
            self.ops[q].append(op)
            self.all_ops.append(op)

    def mm(self, out, lhsT, rhs, start=True, stop=True, **kw):
        o, l, r = out.ap, lhsT.ap, rhs.ap
        return self.add('pe', lambda e: e.matmul(o, l, r, start=start, stop=stop, **kw),
                        reads=[lhsT, rhs], writes=[out])

    def transpose(self, out, in_, ident):
        o, i, d = out.ap, in_.ap, ident.ap
        return self.add('pe', lambda e: e.transpose(o, i, d), reads=[in_, ident], writes=[out])

    def act(self, out, in_, func, bias=None, scale=None, accum_out=None, eng='act'):
        kw = {}
        reads = [in_]
        if bias is not None:
            if isinstance(bias, View):
                kw['bias'] = bias.ap
                reads.append(bias)
            else:
                kw['bias'] = bias
        if scale is not None:
            if isinstance(scale, View):
                kw['scale'] = scale.ap
                reads.append(scale)
            else:
                kw['scale'] = scale
        writes = [out]
        if accum_out is not None:
            kw['accum_out'] = accum_out.ap
            writes.append(accum_out)
        o, i = out.ap, in_.ap
        return self.add('act', lambda e: e.activation(o, i, func, **kw), reads=reads, writes=writes)

    def tt(self, out, in0, in1, op, eng='dve'):
        o, a, b = out.ap, in0.ap, in1.ap
        return self.add(eng, lambda e: e.tensor_tensor(o, a, b, op), reads=[in0, in1], writes=[out])

    def ts(self, out, in0, s1, op0, s2=None, op1=None, eng='dve', accum_out=None):
        reads = [in0]
        a1 = s1
        if isinstance(s1, View):
            a1 = s1.ap
            reads.append(s1)
        a2 = s2
        if isinstance(s2, View):
            a2 = s2.ap
            reads.append(s2)
        o, i = out.ap, in0.ap
        kw = {}
        writes = [out]
        if accum_out is not None:
            kw['accum_out'] = accum_out.ap
            writes.append(accum_out)
        if op1 is None:
            return self.add(eng, lambda e: e.tensor_scalar(o, i, a1, None, op0, **kw), reads=reads, writes=writes)
        return self.add(eng, lambda e: e.tensor_scalar(o, i, a1, a2, op0, op1, **kw), reads=reads, writes=writes)

    def stt(self, out, in0, scalar, in1, op0, op1, eng='dve'):
        reads = [in0, in1]
        sc = scalar
        if isinstance(scalar, View):
            sc = scalar.ap
            reads.append(scalar)
        o, a, b = out.ap, in0.ap, in1.ap
        return self.add(eng, lambda e: e.scalar_tensor_tensor(o, a, sc, b, op0, op1), reads=reads, writes=[out])

    def scan(self, out, d0, d1, initial, op0, op1):
        reads = [d0, d1]
        ini = initial
        if isinstance(initial, View):
            ini = initial.ap
            reads.append(initial)
        o, a, b = out.ap, d0.ap, d1.ap
        return self.add('dve', lambda e: e.tensor_tensor_scan(o, a, b, ini, op0, op1), reads=reads, writes=[out])

    def copy(self, out, in_, eng='dve'):
        o, i = out.ap, in_.ap
        if eng == 'act':
            return self.add('act', lambda e: e.copy(o, i), reads=[in_], writes=[out])
        return self.add(eng, lambda e: e.tensor_copy(o, i), reads=[in_], writes=[out])

    def memset(self, out, val, eng='pool'):
        o = out.ap
        return self.add(eng, lambda e: e.memset(o, val), reads=[], writes=[out])

    def dma(self, out, in_, q='sp', slot=None, reads=None, writes=None, **kw):
        outs = out if isinstance(out, (list, tuple)) else [out]
        ins = in_ if isinstance(in_, (list, tuple)) else [in_]
        pairs = [(o.ap, i.ap) for o, i in zip(outs, ins)]
        if slot is None:
            slot = outs[0].buf if outs[0].buf.space == 'sb' else ins[0].buf
        if slot.sem is None:
            slot.sem = {}
            slot.cnt = {}
        if q not in slot.sem:
            if self.free_sems.get(q):
                slot.sem[q], slot.cnt[q] = self.free_sems[q].pop()
            else:
                slot.sem[q] = self.new_sem()
                slot.cnt[q] = 0
            self.phase_slots[-1].append((slot, q))
        sem_ = slot.sem[q]

        def fn(e, pairs=pairs, sem=sem_, kw=kw):
            last = None
            for (o, i) in pairs:
                last = e.dma_start(out=o, in_=i, **kw).then_inc(sem, 16)
            return None
        op = self.add(q, fn, reads=ins if reads is None else reads,
                      writes=outs if writes is None else writes,
                      is_dma=True, slot=slot, ndma=len(pairs))
        return op

    def new_sem(self):
        self._nsem = getattr(self, '_nsem', 0) + 1
        cm = self.nc.semaphore(f"s{self._nsem}")
        sem = cm.__enter__()
        self._sem_cms = getattr(self, '_sem_cms', [])
        self._sem_cms.append(cm)
        return sem

    def finalize(self):
        nc = self.nc
        for op in self.all_ops:
            for d in op.deps:
                if d.is_dma:
                    continue
                if d.eng == 'pe' and op.eng == 'pe' and not op.is_dma:
                    continue
                d.signal = True
        eng_sems = {}
        for q in ALLQ:
            n = 0
            for op in self.ops[q]:
                if op.signal and not op.is_dma:
                    op.n = n
                    n += 1
            eng_sems[q] = [self.new_sem() for _ in range((n + EPOCH - 1) // EPOCH)]
        self.n_instr = {q: len(self.ops[q]) for q in ALLQ}

        def emit(q, e):
            waited_eng = {}
            waited_dma = {}
            for op in self.ops[q]:
                need_eng = {}
                need_dma = {}
                for d in op.deps:
                    if d.is_dma:
                        k = d.sem
                        if waited_dma.get(id(k), 0) < d.val:
                            if need_dma.get(id(k), (None, 0))[1] < d.val:
                                need_dma[id(k)] = (k, d.val)
                    else:
                        if d.eng == 'pe' and q == 'pe' and not op.is_dma:
                            continue
                        if waited_eng.get(d.eng, -1) < d.n:
                            if need_eng.get(d.eng, -1) < d.n:
                                need_eng[d.eng] = d.n
                for pe_, n in need_eng.items():
                    e.wait_ge(eng_sems[pe_][n // EPOCH], n % EPOCH + 1)
                    waited_eng[pe_] = n
                for _, (k, v) in need_dma.items():
                    e.wait_ge(k, v)
                    waited_dma[id(k)] = v
                if op.fn is None:
                    continue
                ins = op.fn(e)
                if op.signal and not op.is_dma:
                    ins.then_inc(eng_sems[q][op.n // EPOCH], 1)

        with nc.Block() as block:
            @block.tensor
            def _(e):
                emit('pe', e)

            @block.scalar
            def _(e):
                emit('act', e)

            @block.vector
            def _(e):
                emit('dve', e)

            @block.gpsimd
            def _(e):
                emit('pool', e)

            @block.sync
            def _(e):
                emit('sp', e)

from concourse.bass_utils import run_bass_kernel_spmd

D = 1024
EPS = 1e-6
FFN_H = 2816
HYB_IN = 3592
RET_IN = 6144
BIG = 30000.0
RET_LG = [float(np.log1p(-np.exp2(-5.0 - h))) for h in range(4)]

CST = {}
_off = 0
for _n, _w in [('ident', 128), ('tri2', 128), ('triu2', 128), ('cind0', 128), ('cind1', 128),
               ('mb1', 128), ('mb2', 128), ('mlow', 128), ('ones64', 128),
               ('DT0', 128), ('DT1', 128), ('DT2', 128), ('DT3', 128),
               ('QD0', 128), ('QD1', 128), ('QD2', 128), ('QD3', 128), ('kds', 4)]:
    CST[_n] = (_off, _w)
    _off += _w
NCST = _off


def make_consts():
    c = np.zeros((128, NCST), np.float64)
    i = np.arange(128)
    same = (i[:, None] // 64) == (i[None, :] // 64)

    def put(n, a):
        o, w = CST[n]
        c[:, o:o + w] = a
    put('ident', np.eye(128))
    put('tri2', (same & (i[:, None] <= i[None, :])) * 1.0)
    put('triu2', (same & (i[:, None] > i[None, :])) * 1.0)
    put('cind0', np.broadcast_to((i < 64)[:, None] * 1.0, (128, 128)))
    put('cind1', np.broadcast_to((i >= 64)[:, None] * 1.0, (128, 128)))
    put('mb1', np.where(same & (i[:, None] > i[None, :]), 0.0, BIG))
    put('mb2', np.where(same & (i[None, :] >= i[:, None]), 0.0, -BIG))
    put('mlow', (i[None, :] < i[:, None]) * 1.0)
    put('ones64', same * 1.0)
    for h in range(4):
        lg = RET_LG[h]
        dif = i[None, :] - i[:, None]
        put(f'DT{h}', np.where(dif >= 0, np.exp(lg * np.maximum(dif, 0)), 0.0))
        put(f'QD{h}', np.broadcast_to(np.exp(lg * (i + 1.0))[None, :], (128, 128)))
        o, w = CST['kds']
        c[:, o + h] = np.exp(lg * (127.0 - i))
    return c.astype(np.float32)


def make_rope(T):
    inv = 1.0 / (10000.0 ** (np.arange(0, 256, 2, dtype=np.float64) / 256.0))
    ang = inv[:, None] * np.arange(T, dtype=np.float64)[None, :]
    return np.stack([np.cos(ang), np.sin(ang)]).astype(np.float32)


class Rot:
    def __init__(self, mk, n):
        self.bufs = [mk(i) for i in range(n)]
        self.i = 0

    def next(self):
        b = self.bufs[self.i % len(self.bufs)]
        self.i += 1
        return b


class DT_:
    def __init__(self, nc, name, rows, T, dt, fm=True, kind="Internal"):
        shape = [rows, T] if fm else [T, rows]
        self.full = nc.dram_tensor(name, shape, dt, kind=kind).ap()
        self.fm = fm
        self.tiles = []
        for i in range(T // 512):
            ap = self.full[:, i * 512:(i + 1) * 512] if fm else self.full[i * 512:(i + 1) * 512, :]
            self.tiles.append(Buf(ap, f"{name}_t{i}", 'dram'))

    def tile(self, i):
        return self.tiles[i]

    def whole(self, ap):
        return View(self.tiles[0], ap)


GDN_STAGE = 99
SB_W = 1024


def build(T, debug=False, phases=None, n_layers=4):
    nc = bass.Bass("TRN2", target_bir_lowering=False)
    k = Kern(nc)
    NT = T // 128
    NTT = T // 512
    skind = "ExternalOutput" if debug else "Internal"

    def want(p):
        return phases is None or p in phases

    def din(name, shape, dt=F32, used=True):
        return Buf(nc.dram_tensor(name, list(shape), dt, kind="ExternalInput" if used else "Internal").ap(), name, 'dram')

    XT_in = DT_(nc, "xT", D, T, F32, True, "ExternalInput")
    C_in = din("c_col", [128, 8])
    ADAW = din("ada_w", [4, D, 6 * D], used=want("ada"))
    ADAB = din("ada_b_col", [4, 128, 48])
    NMIX = din("norm_mix_col", [4, 128, 8])
    NFFN = din("norm_ffn_col", [4, 128, 8])
    HWI = din("hyb_w_in", [2, D, HYB_IN], used=want("cast"))
    CONV = din("conv_col", [2, 128, 12, 4])
    ALOG = din("a_log_b", [2, 128, 4])
    DTB = din("dt_bias_b", [2, 128, 4])
    GNORM = din("gdn_norm_col", [2, 128, 1])
    QN = din("sb_q_norm_col", [2, 128, 1])
    KN = din("sb_k_norm_col", [2, 128, 1])
    HWO = din("hyb_w_out", [2, D, D], used=want("cast"))
    RWI = din("ret_w_in", [2, D, RET_IN], used=want("cast"))
    RWO = din("ret_w_out", [2, 2048, D], used=want("cast"))
    FWI = din("ffn_w_in", [4, D, 2 * FFN_H], used=want("cast"))
    FWO = din("ffn_w_out", [4, FFN_H, D], used=want("cast"))
    CSTD = din("cst", [128, NCST])
    ROPE = DT_(nc, "rope", 256, T, F32, True, "ExternalInput")
    OUT = DT_(nc, "outT", D, T, F32, True, "ExternalOutput")

    def dscr(name, shape, dt):
        return Buf(nc.dram_tensor(name, list(shape), dt, kind=skind).ap(), name, 'dram')
    HWIb = [dscr(f"hwib{i}", [D, HYB_IN], BF16) for i in range(2)]
    HWOb = [dscr(f"hwob{i}", [D, D], BF16) for i in range(2)]
    RWIb = [dscr(f"rwib{i}", [D, RET_IN], BF16) for i in range(2)]
    RWOb = [dscr(f"rwob{i}", [2048, D], BF16) for i in range(2)]
    FWIb = [dscr(f"fwib{i}", [D, 2 * FFN_H], BF16) for i in range(4)]
    FWOb = [dscr(f"fwob{i}", [8, 128, 22, 128], BF16) for i in range(4)]
    XT = DT_(nc, "xres", D, T, F32, True, skind)
    GQ = DT_(nc, "gq", 512, T, BF16, True, skind)
    GK = DT_(nc, "gk", 512, T, BF16, True, skind)
    GV = DT_(nc, "gv", 512, T, BF16, True, skind)
    GS = DT_(nc, "gs", 512, T, BF16, True, skind)
    GB = DT_(nc, "gbeta", 8, T, F32, False, skind)
    SQ = DT_(nc, "sq", 512, T, BF16, True, skind)
    SK = DT_(nc, "sk", 512, T, BF16, True, skind)
    SV = DT_(nc, "sv", 512, T, BF16, False, skind)
    OMIX = DT_(nc, "omix", 1024, T, BF16, True, skind)
    RQ = DT_(nc, "rq", 1024, T, BF16, True, skind)
    RK = DT_(nc, "rk", 1024, T, BF16, True, skind)
    RV = DT_(nc, "rv", 2048, T, BF16, False, skind)
    RG = DT_(nc, "rg", 2048, T, BF16, True, skind)
    RO = DT_(nc, "ro", 2048, T, BF16, True, skind)

    cst = k.sb([128, NCST], F32, 'cst')
    k.dma(cst.v, CSTD.v)

    def C(n):
        o, w = CST[n]
        return cst[:, o:o + w]
    ident = C('ident')
    ones_f = k.sb([128, 1024], F32, 'ones_f')
    k.memset(ones_f.v, 1.0)
    ones_bf = k.sb([128, 128], BF16, 'ones_bf')
    k.memset(ones_bf.v, 1.0)
    ident_bf = k.sb([128, 128], BF16, 'ident_bf')
    k.copy(ident_bf.v, ident, eng='dve')
    ones64_bf = k.sb([128, 128], BF16, 'ones64_bf')
    k.copy(ones64_bf.v, C('ones64'), eng='dve')
    modc = [k.sb([128, 48], F32, f'modc{l}') for l in range(4)]
    A_m = [k.sb([128, 8], F32, f'Am{l}') for l in range(4)]
    A_f = [k.sb([128, 8], F32, f'Af{l}') for l in range(4)]

    def psum_pool(n, shape=(128, 512), dt=F32, name='ps'):
        def mk(i):
            cm = nc.psum_tensor(f"{name}{i}_{k.sb_names}", list(shape), dt)
            k.sb_names += 1
            t = k.stacks[-1].enter_context(cm)
            return Buf(t.ap(), f"{name}{i}", 'ps')
        return Rot(mk, n)

    def sb_pool(n, shape, dt, name):
        return Rot(lambda i: k.sb(shape, dt, f"{name}{i}"), n)

    act_rr = [0]

    def phase_cast():
        k.phase_begin()
        CW = 2048
        fp = sb_pool(3, [128, CW], F32, 'cf')
        bp = sb_pool(3, [128, CW], BF16, 'cb')
        engs = ['pool', 'act', 'dve']
        cnt = 0
        jobs = []
        for i in range(2):
            if n_layers > 2 * i:
                jobs += [(HWI[i], HWIb[i]), (HWO[i], HWOb[i])]
            if n_layers > 2 * i + 1:
                jobs += [(RWI[i], RWIb[i]), (RWO[i], RWOb[i])]
        for l in range(n_layers):
            jobs += [(FWI[l], FWIb[l]), (FWO[l], FWOb[l])]
        for src, dst in jobs:
            if len(dst.ap.shape) == 4:
                for r in range(22):
                    f = fp.next()
                    b = bp.next()
                    k.dma(f[:, 0:D], src[r * 128:(r + 1) * 128, :])
                    k.copy(b[:, 0:D], f[:, 0:D], eng=engs[cnt % 3])
                    cnt += 1
                    k.dma(View(dst, dst.ap[:, :, r, :].rearrange("c p n -> p c n")),
                          View(b, b.ap[:, 0:D].rearrange("p (c n) -> p c n", c=8)), q='pool')
                continue
            K_, N_ = dst.ap.shape
            for r in range(K_ // 128):
                for c0 in range(0, N_, CW):
                    w = min(CW, N_ - c0)
                    f = fp.next()
                    b = bp.next()
                    k.dma(f[:, 0:w], src[r * 128:(r + 1) * 128, c0:c0 + w])
                    k.copy(b[:, 0:w], f[:, 0:w], eng=engs[cnt % 3])
                    cnt += 1
                    k.dma(dst[r * 128:(r + 1) * 128, c0:c0 + w], b[:, 0:w], q='pool')
        k.phase_end()

    def phase_ada():
        k.phase_begin()
        pp = psum_pool(2)
        ccol = k.sb([128, 8], F32, 'ccol')
        k.dma(ccol.v, C_in.v)
        cact = k.sb([128, 8], F32, 'cact')
        k.act(cact.v, ccol.v, AF.Silu)
        wp = sb_pool(2, [128, 8, 512], F32, 'adaw')
        for l in range(n_layers):
            ps = pp.next()
            for nch in range(12):
                w = wp.next()
                k.dma([w[:, kc, :] for kc in range(8)],
                      [ADAW[l, kc * 128:(kc + 1) * 128, nch * 512:(nch + 1) * 512] for kc in range(8)])
                for jj in range(4):
                    j = nch * 4 + jj
                    for kc in range(8):
                        k.mm(ps[:, j:j + 1], w[:, kc, jj * 128:(jj + 1) * 128], cact[:, kc:kc + 1],
                             start=(kc == 0), stop=(kc == 7))
            bcol = k.sb([128, 48], F32, 'bcol')
            k.dma(bcol.v, ADAB[l])
            k.tt(modc[l].v, ps[:, 0:48], bcol.v, ALU.add)
            for (A, NG, c0) in ((A_m[l], NMIX, 8), (A_f[l], NFFN, 32)):
                g = k.sb([128, 8], F32, 'g')
                k.dma(g.v, NG[l])
                t = k.sb([128, 8], F32, 't')
                k.ts(t.v, modc[l][:, c0:c0 + 8], 1.0, ALU.add)
                k.tt(A.v, t.v, g.v, ALU.mult)
        k.phase_end()

    def make_norm(pp):
        sqp = sb_pool(2, [128, 512], BF16, 'nsq')
        rp = sb_pool(2, [128, 512], F32, 'nr')
        tp = sb_pool(2, [128, 512], F32, 'nt')

        def norm_tile(xt, A, sh, hT):
            ps = pp.next()
            for kc in range(8):
                sq = sqp.next()
                k.act(sq.v, xt[:, kc, :], AF.Square)
                k.mm(ps.v, ones_bf.v, sq.v, start=(kc == 0), stop=(kc == 7))
            r1 = rp.next()
            k.act(r1.v, ps.v, AF.Ln, scale=1.0 / D, bias=EPS)
            rstd = rp.next()
            k.act(rstd.v, r1.v, AF.Exp, scale=-0.5)
            for kc in range(8):
                t = tp.next()
                k.stt(t.v, xt[:, kc, :], A[:, kc:kc + 1], rstd.v, ALU.mult, ALU.mult)
                k.act(hT[:, kc, :], t.v, AF.Identity, bias=sh[:, kc:kc + 1])
        return norm_tile

    def fm_view(dt, ti, r0, nch, p=128):
        b = dt.tile(ti)
        return View(b, b.ap[r0:r0 + nch * p, :].rearrange("(c p) t -> p c t", p=p))

    def tm_view(dt, ti, c0, c1):
        b = dt.tile(ti)
        return View(b, b.ap[:, c0:c1].rearrange("(t p) c -> p t c", p=128))

    def phase_p1_hyb(l, i, xsrc):
        k.phase_begin()
        pp = psum_pool(8)
        W = k.sb([128, 8, HYB_IN], BF16, 'Whyb')
        for kc in range(8):
            k.dma(W[:, kc, :], HWIb[i][kc * 128:(kc + 1) * 128, :])
        conv = k.sb([128, 12, 4], F32, 'conv')
        k.dma(conv.v, CONV[i])
        dtb = k.sb([128, 4], F32, 'dtb')
        k.dma(dtb.v, DTB[i])
        alog = k.sb([128, 4], F32, 'alog')
        k.dma(alog.v, ALOG[i])
        negA = k.sb([128, 4], F32, 'negA')
        k.act(negA.v, alog.v, AF.Exp)
        k.ts(negA.v, negA.v, -1.0, ALU.mult, eng='pool')
        qg = k.sb([128, 1], F32, 'qg')
        k.dma(qg.v, QN[i])
        k.ts(qg.v, qg.v, 0.125, ALU.mult, eng='pool')
        kg = k.sb([128, 1], F32, 'kg')
        k.dma(kg.v, KN[i])
        halo = k.sb([128, 12, 3], F32, 'halo')
        k.memset(halo.v, 0.0)
        xp = sb_pool(2, [128, 8, 512], F32, 'xt')
        hp = sb_pool(2, [128, 8, 512], BF16, 'hT')
        rawp = sb_pool(5, [128, 515], F32, 'raw')
        yp = sb_pool(5, [128, 512], F32, 'y')
        sp_ = sb_pool(5, [128, 512], F32, 's')
        sqp = sb_pool(5, [128, 512], BF16, 'sq1')
        rp = sb_pool(9, [128, 512], F32, 'r')
        obp = sb_pool(8, [128, 512], BF16, 'ob')
        gbp = sb_pool(2, [128, 4, 8], F32, 'gbo')
        smp = sb_pool(6, [128, 4, 4], F32, 'sm')
        norm_tile = make_norm(pp)
        for ti in range(NTT):
            xt = xp.next()
            k.dma(xt.v, fm_view(xsrc, ti, 0, 8))
            hT = hp.next()
            norm_tile(xt, A_m[l], modc[l][:, 0:8], hT)
            for grp in range(3):
                ccs = list(range(4 * grp, 4 * grp + 4))
                raws, ys, ss, obs = {}, {}, {}, {}
                for cc in ccs:
                    ps = pp.next()
                    for kc in range(8):
                        k.mm(ps.v, W[:, kc, cc * 128:(cc + 1) * 128], hT[:, kc, :], start=(kc == 0), stop=(kc == 7))
                    raw = rawp.next()
                    k.copy(raw[:, 0:3], halo[:, cc, :], eng='pool')
                    k.copy(raw[:, 3:515], ps.v, eng='act')
                    k.copy(halo[:, cc, :], raw[:, 512:515], eng='pool')
                    raws[cc] = raw
                for cc in ccs:
                    raw = raws[cc]
                    y = yp.next()
                    k.ts(y.v, raw[:, 0:512], conv[:, cc, 0:1], ALU.mult)
                    for j in range(1, 4):
                        k.stt(y.v, raw[:, j:j + 512], conv[:, cc, j:j + 1], y.v, ALU.mult, ALU.add)
                    ys[cc] = y
                for cc in ccs:
                    s_ = sp_.next()
                    k.act(s_.v, ys[cc].v, AF.Silu)
                    ss[cc] = s_
                if grp < 2:
                    sqs, ps2s, r2s = {}, {}, {}
                    for cc in ccs:
                        sq = sqp.next()
                        k.act(sq.v, ss[cc].v, AF.Square)
                        sqs[cc] = sq
                    for cc in ccs:
                        ps2 = pp.next()
                        k.mm(ps2.v, ones_bf.v, sqs[cc].v)
                        ps2s[cc] = ps2
                    for cc in ccs:
                        r1 = rp.next()
                        k.act(r1.v, ps2s[cc].v, AF.Ln, bias=EPS)
                        r2 = rp.next()
                        k.act(r2.v, r1.v, AF.Exp, scale=-0.5)
                        r2s[cc] = r2
                for cc in ccs:
                    ob = obp.next()
                    hh = cc % 4
                    if grp == 0:
                        k.stt(ob.v, ss[cc].v, 128.0 ** -0.5, r2s[cc].v, ALU.mult, ALU.mult)
                        dst = GQ
                    elif grp == 1:
                        k.tt(ob.v, ss[cc].v, r2s[cc].v, ALU.mult, eng='pool')
                        dst = GK
                    else:
                        k.copy(ob.v, ss[cc].v, eng='pool')
                        dst = GV
                    k.dma(dst.tile(ti)[hh * 128:(hh + 1) * 128, :], ob.v, q='pool')
            pss = []
            for hh in range(4):
                ps = pp.next()
                c0 = 1536 + hh * 128
                for kc in range(8):
                    k.mm(ps.v, W[:, kc, c0:c0 + 128], hT[:, kc, :], start=(kc == 0), stop=(kc == 7))
                pss.append(ps)
            for hh in range(4):
                ob = obp.next()
                k.act(ob.v, pss[hh].v, AF.Silu)
                k.dma(GS.tile(ti)[hh * 128:(hh + 1) * 128, :], ob.v, q='pool')
            for grp in range(2):
                cs4 = list(range(4 * grp, 4 * grp + 4))
                pss, sqs, ps2s, r2s = {}, {}, {}, {}
                for c in cs4:
                    ps = pp.next()
                    c0 = 2056 + c * 128
                    for kc in range(8):
                        k.mm(ps.v, W[:, kc, c0:c0 + 128], hT[:, kc, :], start=(kc == 0), stop=(kc == 7))
                    pss[c] = ps
                for c in cs4:
                    sq = sqp.next()
                    k.act(sq.v, pss[c].v, AF.Square)
                    sqs[c] = sq
                for c in cs4:
                    ps2 = pp.next()
                    k.mm(ps2.v, ones64_bf.v, sqs[c].v)
                    ps2s[c] = ps2
                for c in cs4:
                    r1 = rp.next()
                    k.act(r1.v, ps2s[c].v, AF.Ln, scale=1.0 / 64, bias=EPS)
                    r2 = rp.next()
                    k.act(r2.v, r1.v, AF.Exp, scale=-0.5)
                    r2s[c] = r2
                for c in cs4:
                    ob = obp.next()
                    k.stt(ob.v, pss[c].v, (qg if c < 4 else kg)[:, 0:1], r2s[c].v, ALU.mult, ALU.mult)
                    dst = SQ if c < 4 else SK
                    cc = c % 4
                    k.dma(dst.tile(ti)[cc * 128:(cc + 1) * 128, :], ob.v, q='pool')
            for tb in range(4):
                ps = pp.next()
                for kc in range(8):
                    k.mm(ps.v, hT[:, kc, tb * 128:(tb + 1) * 128], W[:, kc, 3080:3592], start=(kc == 0), stop=(kc == 7))
                ob = obp.next()
                k.copy(ob.v, ps.v, eng='act' if tb % 2 else 'dve')
                k.dma(SV.tile(ti)[tb * 128:(tb + 1) * 128, :], ob.v, q='pool')
            ps = pp.next()
            for tb in range(4):
                for kc in range(8):
                    k.mm(ps[:, tb * 8:(tb + 1) * 8], hT[:, kc, tb * 128:(tb + 1) * 128], W[:, kc, 2048:2056],
                         start=(kc == 0), stop=(kc == 7))
            pv = ps[:, 0:32].re("p (t e) -> p t e", e=8)
            dtb_b = View(dtb, dtb.ap.unsqueeze(1).broadcast_to([128, 4, 4]))
            negA_b = View(negA, negA.ap.unsqueeze(1).broadcast_to([128, 4, 4]))
            z = smp.next()
            k.tt(z.v, pv[:, :, 0:4], dtb_b, ALU.add)
            e1 = smp.next()
            k.act(e1.v, z.v, AF.Exp)
            s1 = smp.next()
            k.act(s1.v, e1.v, AF.Ln, bias=1.0)
            gbo = gbp.next()
            k.tt(gbo[:, :, 0:4], s1.v, negA_b, ALU.mult)
            e2 = smp.next()
            k.act(e2.v, pv[:, :, 4:8], AF.Exp, scale=-1.0)
            d2 = smp.next()
            k.ts(d2.v, e2.v, 1.0, ALU.add)
            k.add('dve', lambda e, o=gbo.ap[:, :, 4:8], i_=d2.ap: e.reciprocal(o, i_), reads=[d2], writes=[gbo])
            k.dma(tm_view(GB, ti, 0, 8), gbo.v, q='pool')
        k.phase_end()

    def phase_gdn(l, i):
        k.phase_begin()
        pp = psum_pool(6)
        ppo = psum_pool(2, name='pso')
        gn = k.sb([128, 1], F32, 'gn')
        k.dma(gn.v, GNORM[i])
        ident4 = k.sb([128, 4, 128], F32, 'ident4')
        for h in range(4):
            k.copy(ident4[:, h, :], ident, eng='pool')
        S = k.sb([128, 4, 128], F32, 'S')
        k.memset(S.v, 0.0)
        ldp = {n: sb_pool(2, [128, 4, 512], BF16, n) for n in ('kT', 'qT', 'vT', 'gsT')}
        gbp = sb_pool(2, [128, 4, 8], F32, 'gbt')
        osp = sb_pool(2, [128, 4, 512], BF16, 'ost')
        f4 = {n: sb_pool(2, [128, 4, 128], F32, n) for n in
              ('kdec', 'kw', 'vb', 'gbc', 'E', 'ET', 'EQ', 't1', 'A', 'Aqk', 'u', 'wT', 'qd', 'osb', 'o2', 'r1', 'r2')}
        f4r = {n: sb_pool(3, [128, 4, 128], F32, n) for n in ('Sx', 'STx', 'PT')}
        sqp = sb_pool(2, [128, 4, 128], BF16, 'gsq')
        smp = {n: sb_pool(2, [128, w], F32, n) for n, w in (('cs', 16), ('ex', 16), ('ngc', 4), ('kws', 4), ('nb4', 4))}
        vnzp = [sb_pool(2, [128, 4, 128], F32, f'vnz{c}') for c in range(2)]
        for c in range(2):
            for b in vnzp[c].bufs:
                k.memset(b.v, 0.0)

        def f2(b):
            return b.v.re("p h d -> p (h d)")
        for ti in range(NTT):
            kT4 = ldp['kT'].next()
            k.dma(kT4.v, fm_view(GK, ti, 0, 4))
            qT4 = ldp['qT'].next()
            k.dma(qT4.v, fm_view(GQ, ti, 0, 4))
            vT4 = ldp['vT'].next()
            k.dma(vT4.v, fm_view(GV, ti, 0, 4))
            gs4 = ldp['gsT'].next()
            k.dma(gs4.v, fm_view(GS, ti, 0, 4))
            gbt = gbp.next()
            k.dma(gbt.v, tm_view(GB, ti, 0, 8))
            ost = osp.next()
            for tb in range(4):
                blk = slice(tb * 128, (tb + 1) * 128)
                g4 = gbt[:, tb, 0:4]
                b4 = gbt[:, tb, 4:8]
                cps = pp.next()
                k.mm(cps[:, 0:4], C('tri2'), g4)
                k.mm(cps[:, 4:8], C('triu2'), g4)
                k.mm(cps[:, 8:12], C('cind0'), g4)
                k.mm(cps[:, 12:16], C('cind1'), g4)
                cs = smp['cs'].next()
                k.copy(cs.v, cps[:, 0:16], eng='dve')
                ex = smp['ex'].next()
                k.act(ex.v, cps[:, 0:16], AF.Exp)
                ngc = smp['ngc'].next()
                k.ts(ngc.v, cs[:, 0:4], -1.0, ALU.mult, eng='pool')
                kws = smp['kws'].next()
                k.tt(kws.v, b4, ex[:, 0:4], ALU.mult, eng='pool')
                nb4 = smp['nb4'].next()
                k.ts(nb4.v, b4, -1.0, ALU.mult, eng='pool')
                if GDN_STAGE <= 1:
                    continue
                trp = pp.next()
                tv = trp.v.bitcast(BF16)
                for h in range(4):
                    k.transpose(tv[:, h * 128:(h + 1) * 128], kT4[:, h, blk], ident_bf.v)
                for h in range(4):
                    k.transpose(tv[:, 512 + h * 128:512 + (h + 1) * 128], vT4[:, h, blk], ident_bf.v)
                kdec = f4['kdec'].next()
                kw = f4['kw'].next()
                vb = f4['vb'].next()
                for h in range(4):
                    k.ts(kdec[:, h, :], tv[:, h * 128:(h + 1) * 128], ex[:, 4 + h:5 + h], ALU.mult)
                    k.act(kw[:, h, :], tv[:, h * 128:(h + 1) * 128], AF.Identity, scale=kws[:, h:h + 1])
                    k.act(vb[:, h, :], tv[:, 512 + h * 128:512 + (h + 1) * 128], AF.Identity, scale=b4[:, h:h + 1])
                if GDN_STAGE <= 2:
                    continue
                gbc = f4['gbc'].next()
                for h in range(4):
                    k.ts(gbc[:, h, :], ones_f[:, 0:128], g4[:, h:h + 1], ALU.mult, eng='pool')
                RA = pp.next()
                RB = pp.next()
                RC = pp.next()
                for h in range(4):
                    hs = slice(h * 128, (h + 1) * 128)
                    k.mm(RA[:, hs], gbc[:, h, :], C('tri2'), start=True, stop=False)
                    k.mm(RA[:, hs], ident, C('mb1'), start=False, stop=True)
                    k.mm(RB[:, hs], gbc[:, h, :], C('tri2'), start=True, stop=False)
                    k.mm(RB[:, hs], ident, C('mb2'), start=False, stop=True)
                    k.mm(RC[:, hs], gbc[:, h, :], C('tri2'))
                E = f4['E'].next()
                ET = f4['ET'].next()
                EQ = f4['EQ'].next()
                for h in range(4):
                    hs = slice(h * 128, (h + 1) * 128)
                    k.act(E[:, h, :], RA[:, hs], AF.Exp, scale=-1.0, bias=cs[:, h:h + 1])
                    k.act(ET[:, h, :], RB[:, hs], AF.Exp, bias=ngc[:, h:h + 1])
                k.act(f2(EQ), RC.v, AF.Exp)
                if GDN_STAGE <= 3:
                    continue
                KK = pp.next()
                KQ = pp.next()
                for h in range(4):
                    hs = slice(h * 128, (h + 1) * 128)
                    k.mm(KK[:, hs], kT4[:, h, blk], kT4[:, h, blk])
                    k.mm(KQ[:, hs], kT4[:, h, blk], qT4[:, h, blk])
                t1 = f4['t1'].next()
                k.tt(f2(t1), KK.v, f2(E), ALU.mult)
                A = f4['A'].next()
                for h in range(4):
                    k.ts(A[:, h, :], t1[:, h, :], nb4[:, h:h + 1], ALU.mult, eng='pool')
                Aqk = f4['Aqk'].next()
                k.tt(f2(Aqk), KQ.v, f2(ET), ALU.mult)
                if GDN_STAGE <= 4:
                    continue
                ATp = pp.next()
                for h in range(4):
                    k.transpose(ATp[:, h * 128:(h + 1) * 128], A[:, h, :], ident)
                ST_ = f4r['STx'].next()
                k.copy(f2(ST_), ATp.v, eng='act')
                PT = f4r['PT'].next()
                k.tt(f2(PT), ATp.v, f2(ident4), ALU.add)
                S_ = A
                for lev in range(1, 6):
                    Sp = pp.next()
                    for h in range(4):
                        k.mm(Sp[:, h * 128:(h + 1) * 128], ST_[:, h, :], S_[:, h, :])
                    Sn = f4r['Sx'].next()
                    k.copy(f2(Sn), Sp.v, eng='act')
                    if lev < 5:
                        STp = pp.next()
                        for h in range(4):
                            k.mm(STp[:, h * 128:(h + 1) * 128], S_[:, h, :], ST_[:, h, :])
                        STn = f4r['STx'].next()
                        k.copy(f2(STn), STp.v, eng='dve')
                    Pp = pp.next()
                    for h in range(4):
                        k.mm(Pp[:, h * 128:(h + 1) * 128], Sn[:, h, :], PT[:, h, :])
                    PTn = f4r['PT'].next()
                    k.tt(f2(PTn), Pp.v, f2(PT), ALU.add)
                    PT = PTn
                    S_ = Sn
                    if lev < 5:
                        ST_ = STn
                if GDN_STAGE <= 5:
                    continue
                up = pp.next()
                wp_ = pp.next()
                for h in range(4):
                    hs = slice(h * 128, (h + 1) * 128)
                    k.mm(up[:, hs], PT[:, h, :], vb[:, h, :])
                    k.mm(wp_[:, hs], kw[:, h, :], PT[:, h, :])
                u = f4['u'].next()
                k.copy(f2(u), up.v, eng='act')
                wT = f4['wT'].next()
                k.copy(f2(wT), wp_.v, eng='dve')
                qd = f4['qd'].next()
                k.tt(qd.v, qT4[:, :, blk], EQ.v, ALU.mult, eng='pool')
                if GDN_STAGE <= 6:
                    continue
                oTp = ppo.next()
                for c in range(2):
                    cs_ = slice(c * 64, (c + 1) * 64)
                    vnz = vnzp[c].next()
                    vnp = pp.next()
                    for h in range(4):
                        k.mm(vnp[:, h * 128:(h + 1) * 128], wT[:, h, :], S[:, h, :])
                    k.tt(vnz[cs_, :, :].re("p h d -> p (h d)"), u[cs_, :, :].re("p h d -> p (h d)"), vnp[cs_, :], ALU.subtract)
                    for h in range(4):
                        o_ = oTp[:, h * 128 + c * 64:h * 128 + (c + 1) * 64]
                        k.mm(o_, S[:, h, :], qd[:, h, cs_], start=True, stop=False)
                        k.mm(o_, vnz[:, h, :], Aqk[:, h, cs_], start=False, stop=True)
                    dSp = pp.next()
                    for h in range(4):
                        k.mm(dSp[:, h * 128:(h + 1) * 128], kdec[:, h, :], vnz[:, h, :])
                    for h in range(4):
                        k.stt(S[:, h, :], S[:, h, :], ex[:, 8 + 4 * c + h:9 + 4 * c + h], dSp[:, h * 128:(h + 1) * 128],
                              ALU.mult, ALU.add)
                if GDN_STAGE <= 7:
                    continue
                osb = f4['osb'].next()
                k.copy(f2(osb), oTp.v, eng='dve')
                sq = sqp.next()
                k.act(f2(sq), oTp.v, AF.Square)
                ssp = pp.next()
                for h in range(4):
                    k.mm(ssp[:, h * 128:(h + 1) * 128], ones_bf.v, sq[:, h, :])
                r1 = f4['r1'].next()
                k.act(f2(r1), ssp.v, AF.Ln, scale=1.0 / 128, bias=EPS)
                r2 = f4['r2'].next()
                k.act(f2(r2), f2(r1), AF.Exp, scale=-0.5)
                o2 = f4['o2'].next()
                k.tt(f2(o2), f2(osb), f2(r2), ALU.mult)
                k.stt(ost[:, :, blk], o2.v, gn[:, 0:1], gs4[:, :, blk], ALU.mult, ALU.mult)
            k.dma(fm_view(OMIX, ti, 0, 4), ost.v, q='pool')
        k.phase_end()

    def phase_sb(l, i):
        k.phase_begin()
        Wk = SB_W
        zp = psum_pool(2, (128, Wk), F32, 'z')
        atp = psum_pool(2, (128, Wk), BF16, 'aT')
        op_ = psum_pool(2, (128, 512), F32, 'oT')
        qp = sb_pool(2, [64, T], BF16, 'qh')
        kp = sb_pool(2, [64, T], BF16, 'kh')
        vp = sb_pool(2, [128, NT, 64], BF16, 'vh')
        ohp = sb_pool(2, [64, T], BF16, 'oh')
        ep = sb_pool(4, [128, Wk], F32, 'e')
        spp = sb_pool(3, [128, Wk], F32, 'sp')
        gp = sb_pool(3, [128, Wk + 1], F32, 'G')
        for b_ in gp.bufs:
            k.memset(b_[:, 0:1], 0.0)
        pp_ = sb_pool(2, [128, Wk], F32, 'p')
        ap_ = sb_pool(4, [128, Wk], BF16, 'a')
        atsp = sb_pool(3, [128, Wk], BF16, 'aTs')
        bp = sb_pool(10, [128, 1], F32, 'bias')
        mlow = C('mlow')
        tiles = []
        for h in range(8):
            for qb in range(NT):
                t0 = qb * 128
                nkt = (t0 + 128 + Wk - 1) // Wk
                for kt in reversed(range(nkt)):
                    tiles.append(dict(h=h, qb=qb, t0=t0, k0=kt * Wk, w=min(Wk, t0 + 128 - kt * Wk),
                                      diag=(kt == nkt - 1), lastq=(kt == 0), idx=len(tiles)))
        heads = {}

        def load_head(h):
            if h in heads or h >= 8:
                return
            qh = qp.next()
            k.dma(qh.v, SQ.whole(SQ.full[h * 64:(h + 1) * 64, :]), reads=SQ.tiles)
            kh = kp.next()
            k.dma(kh.v, SK.whole(SK.full[h * 64:(h + 1) * 64, :]), reads=SK.tiles)
            vh = vp.next()
            vstep = min(8, NT)
            for n0 in range(0, NT, vstep):
                k.dma(vh[:, n0:n0 + vstep, :],
                      SV.whole(SV.full[n0 * 128:(n0 + vstep) * 128, h * 64:(h + 1) * 64].rearrange("(n p) d -> p n d", p=128)),
                      reads=SV.tiles)
            heads[h] = dict(qh=qh, kh=kh, vh=vh, oh=ohp.next())

        def S1(t):
            h = t['h']
            if h not in heads:
                load_head(h)
            hd = heads[h]
            w, k0, t0 = t['w'], t['k0'], t['t0']
            z = zp.next()
            for c0 in range(0, w, 512):
                cw = min(512, w - c0)
                k.mm(z[:, c0:c0 + cw], hd['qh'][:, t0:t0 + 128], hd['kh'][:, k0 + c0:k0 + c0 + cw])
            e = ep.next()
            k.act(e[:, 0:w], z[:, 0:w], AF.Exp)
            sp = spp.next()
            k.act(sp[:, 0:w], e[:, 0:w], AF.Ln, bias=1.0)
            if t['diag']:
                k.tt(sp[:, w - 128:w], sp[:, w - 128:w], mlow, ALU.mult, eng='dve')
                k.tt(e[:, w - 128:w], e[:, w - 128:w], mlow, ALU.mult, eng='dve')
            t['e'] = e
            t['sp'] = sp

        def S2(t):
            w = t['w']
            G = gp.next()
            k.scan(G[:, 1:w + 1], ones_f[:, 0:w], t['sp'][:, 0:w], 0.0, ALU.mult, ALU.add)
            t['G'] = G

        def S3(t):
            w = t['w']
            G = t['G']
            bias = bp.next()
            if t['diag']:
                k.act(bias.v, G[:, w:w + 1], AF.Identity, scale=-1.0)
            else:
                k.act(bias.v, G[:, w:w + 1], AF.Identity, scale=-1.0, bias=tiles[t['idx'] - 1]['bias'][:, 0:1])
            t['bias'] = bias
            p = pp_.next()
            k.act(p[:, 0:w], G[:, 0:w], AF.Exp, bias=bias[:, 0:1])
            a = ap_.next()
            k.tt(a[:, 0:w], t['e'][:, 0:w], p[:, 0:w], ALU.mult, eng='pool')
            t['a'] = a

        def S4(t):
            w = t['w']
            aT = atp.next()
            for sb in range(w // 128):
                k.transpose(aT[:, sb * 128:(sb + 1) * 128], t['a'][:, sb * 128:(sb + 1) * 128], ident_bf.v)
            aTs = atsp.next()
            k.copy(aTs[:, 0:w], aT[:, 0:w], eng='act' if t['idx'] % 4 == 0 else 'dve')
            t['aTs'] = aTs

        def S5(t):
            w, k0, t0 = t['w'], t['k0'], t['t0']
            hd = heads[t['h']]
            if t['diag'] and t['qb'] == 0:
                load_head(t['h'] + 1)
            if t['diag']:
                t['oT'] = op_.next()
            else:
                t['oT'] = tiles[t['idx'] - 1]['oT']
            oT = t['oT']
            nsb = w // 128
            for sb in range(nsb):
                k.mm(oT[0:64, 0:128], hd['vh'][:, k0 // 128 + sb, :], t['aTs'][:, sb * 128:(sb + 1) * 128],
                     start=(t['diag'] and sb == 0), stop=(t['lastq'] and sb == nsb - 1))
            if t['lastq']:
                k.copy(hd['oh'][:, t0:t0 + 128], oT[0:64, 0:128], eng='act')
                if t['qb'] == NT - 1:
                    h = t['h']
                    k.dma(OMIX.whole(OMIX.full[512 + h * 64:512 + (h + 1) * 64, :]), hd['oh'].v, q='sp', writes=OMIX.tiles)
            for key in ('e', 'sp', 'G', 'a', 'aTs'):
                t.pop(key, None)

        n = len(tiles)
        for s_ in range(n + 6):
            if 0 <= s_ - 4 < n:
                S4(tiles[s_ - 4])
            if 0 <= s_ - 2 < n:
                S3(tiles[s_ - 2])
            if 0 <= s_ - 1 < n:
                S2(tiles[s_ - 1])
            if s_ < n:
                S1(tiles[s_])
            if 0 <= s_ - 5 < n:
                S5(tiles[s_ - 5])
        k.phase_end()

    def phase_p1_ret(l, i, xsrc):
        k.phase_begin()
        pp = psum_pool(8)
        W = k.sb([128, 8, RET_IN], BF16, 'Wret')
        for kc in range(8):
            k.dma(W[:, kc, :], RWIb[i][kc * 128:(kc + 1) * 128, :])
        xp = sb_pool(1, [128, 8, 512], F32, 'xt')
        hp = sb_pool(1, [128, 8, 512], BF16, 'hT')
        csp = sb_pool(2, [128, 2, 512], F32, 'cs')
        tp = sb_pool(4, [128, 512], F32, 'rt')
        qst = sb_pool(2, [128, 8, 512], BF16, 'qst')
        kst = qst
        gst = sb_pool(1, [128, 8, 512], BF16, 'gst')
        vst = sb_pool(1, [128, 2, 2048], BF16, 'vst')
        norm_tile = make_norm(pp)
        for ti in range(NTT):
            xt = xp.next()
            k.dma(xt.v, fm_view(xsrc, ti, 0, 8))
            cs = csp.next()
            k.dma(cs.v, fm_view(ROPE, ti, 0, 2))
            hT = hp.next()
            norm_tile(xt, A_m[l], modc[l][:, 0:8], hT)
            cos = cs[:, 0, :]
            sin = cs[:, 1, :]
            for which in range(2):
                st = (qst if which == 0 else kst).next()
                for h in range(4):
                    c0 = which * 1024 + h * 256
                    p1 = pp.next()
                    p2 = pp.next()
                    for kc in range(8):
                        k.mm(p1.v, W[:, kc, c0:c0 + 128], hT[:, kc, :], start=(kc == 0), stop=(kc == 7))
                    for kc in range(8):
                        k.mm(p2.v, W[:, kc, c0 + 128:c0 + 256], hT[:, kc, :], start=(kc == 0), stop=(kc == 7))
                    sc = 1.0 if which == 0 else 1.0 / 16.0
                    t1 = tp.next()
                    t2 = tp.next()
                    t3 = tp.next()
                    t4 = tp.next()
                    k.stt(t1.v, p1.v, sc, cos, ALU.mult, ALU.mult)
                    k.stt(t2.v, p2.v, sc, sin, ALU.mult, ALU.mult)
                    k.stt(t3.v, p1.v, sc, sin, ALU.mult, ALU.mult)
                    k.stt(t4.v, p2.v, sc, cos, ALU.mult, ALU.mult)
                    k.tt(st[:, 2 * h, :], t1.v, t2.v, ALU.subtract, eng='pool')
                    k.tt(st[:, 2 * h + 1, :], t3.v, t4.v, ALU.add, eng='pool')
                k.dma(fm_view(RQ if which == 0 else RK, ti, 0, 8), st.v, q='pool')
            for half in range(2):
                g_ = gst.next()
                for cl in range(8):
                    c = half * 8 + cl
                    ps = pp.next()
                    c0 = 4096 + c * 128
                    for kc in range(8):
                        k.mm(ps.v, W[:, kc, c0:c0 + 128], hT[:, kc, :], start=(kc == 0), stop=(kc == 7))
                    k.act(g_[:, cl, :], ps.v, AF.Silu)
                k.dma(fm_view(RG, ti, half * 1024, 8), g_.v, q='pool')
            for half in range(2):
                v_ = vst.next()
                for tl in range(2):
                    tb = half * 2 + tl
                    for nb in range(4):
                        ps = pp.next()
                        c0 = 2048 + nb * 512
                        for kc in range(8):
                            k.mm(ps.v, hT[:, kc, tb * 128:(tb + 1) * 128], W[:, kc, c0:c0 + 512], start=(kc == 0), stop=(kc == 7))
                        k.copy(v_[:, tl, nb * 512:(nb + 1) * 512], ps.v, eng='dve' if nb % 2 else 'act')
                bt = RV.tile(ti)
                k.dma(View(bt, bt.ap[half * 256:(half + 1) * 256, :].rearrange("(t p) c -> p t c", p=128)), v_.v, q='pool')
        k.phase_end()

    def phase_ret(l, i):
        k.phase_begin()
        pt4p = psum_pool(1, (128, 512), F32, 'pt4')
        trpp = psum_pool(1, (128, 512), F32, 'trp')
        otp = psum_pool(4, (128, 512), F32, 'oTp')
        pp2 = psum_pool(1, (128, 1024), F32, 'dS')
        Sst = [k.sb([128, 1024], F32, f'S{h}') for h in range(4)]
        Sb = [k.sb([128, 2, 512], BF16, f'Sb{h}') for h in range(4)]
        for h in range(4):
            k.memset(Sst[h].v, 0.0)
            k.memset(Sb[h].v, 0.0, eng='dve')
        qp = sb_pool(2, [128, 8, 512], BF16, 'q8')
        kp = sb_pool(2, [128, 8, 512], BF16, 'k8')
        vp = sb_pool(2, [128, 4, 2048], BF16, 'v4')
        gp = sb_pool(1, [128, 16, 512], BF16, 'g16')
        osp = sb_pool(2, [128, 16, 512], BF16, 'o16')
        P4p = sb_pool(2, [128, 4, 128], BF16, 'P4')
        kd4p = sb_pool(2, [128, 4, 256], BF16, 'kd4')
        qd4p = sb_pool(2, [128, 8, 128], BF16, 'qd4')
        sqp = sb_pool(2, [128, 4, 512], BF16, 'rsq')
        rp = sb_pool(4, [128, 512], F32, 'rr')
        o2p = sb_pool(4, [128, 4, 128], F32, 'o2')
        kds = C('kds')
        for ti in range(NTT):
            q8 = qp.next()
            k.dma(q8.v, fm_view(RQ, ti, 0, 8))
            k8 = kp.next()
            k.dma(k8.v, fm_view(RK, ti, 0, 8))
            v4 = vp.next()
            k.dma(v4.v, tm_view(RV, ti, 0, 2048))
            g16 = gp.next()
            k.dma(g16.v, fm_view(RG, ti, 0, 16))
            ost = osp.next()
            for tb in range(4):
                blk = slice(tb * 128, (tb + 1) * 128)
                PT4 = pt4p.next()
                trp = trpp.next()
                tv = trp.v.bitcast(BF16)
                for h in range(4):
                    hs = slice(h * 128, (h + 1) * 128)
                    k.mm(PT4[:, hs], k8[:, 2 * h, blk], q8[:, 2 * h, blk], start=True, stop=False)
                    k.mm(PT4[:, hs], k8[:, 2 * h + 1, blk], q8[:, 2 * h + 1, blk], start=False, stop=True)
                for h in range(4):
                    for d in range(2):
                        k.transpose(tv[:, h * 256 + d * 128:h * 256 + (d + 1) * 128], k8[:, 2 * h + d, blk], ident_bf.v)
                P4 = P4p.next()
                kd4 = kd4p.next()
                qd4 = qd4p.next()
                for h in range(4):
                    k.tt(P4[:, h, :], PT4[:, h * 128:(h + 1) * 128], C(f'DT{h}'), ALU.mult)
                for h in range(4):
                    k.ts(kd4[:, h, :], tv[:, h * 256:(h + 1) * 256], kds[:, h:h + 1], ALU.mult)
                for h in range(4):
                    for d in range(2):
                        k.tt(qd4[:, 2 * h + d, :], q8[:, 2 * h + d, blk], C(f'QD{h}'), ALU.mult, eng='pool')
                oTps = []
                for h in range(4):
                    oTp = otp.next()
                    for dvc in range(4):
                        o_ = oTp[:, dvc * 128:(dvc + 1) * 128]
                        k.mm(o_, v4[:, tb, h * 512 + dvc * 128:h * 512 + (dvc + 1) * 128], P4[:, h, :], start=True, stop=False)
                        k.mm(o_, Sb[h][:, 0, dvc * 128:(dvc + 1) * 128], qd4[:, 2 * h, :], start=False, stop=False)
                        k.mm(o_, Sb[h][:, 1, dvc * 128:(dvc + 1) * 128], qd4[:, 2 * h + 1, :], start=False, stop=True)
                    oTps.append(oTp)
                for h in range(4):
                    dSp = pp2.next()
                    for d in range(2):
                        k.mm(dSp[:, d * 512:(d + 1) * 512], kd4[:, h, d * 128:(d + 1) * 128], v4[:, tb, h * 512:(h + 1) * 512])
                    k.stt(Sst[h].v, Sst[h].v, float(np.exp(RET_LG[h] * 128.0)), dSp.v, ALU.mult, ALU.add)
                    k.copy(Sb[h].v.re("p a b -> p (a b)"), Sst[h].v, eng='act')
                sq = sqp.next()
                for h in range(4):
                    k.act(sq[:, h, :], oTps[h].v, AF.Square)
                ssp = pt4p.next()
                for h in range(4):
                    for dvc in range(4):
                        k.mm(ssp[:, h * 128:(h + 1) * 128], ones_bf.v, sq[:, h, dvc * 128:(dvc + 1) * 128], start=(dvc == 0), stop=(dvc == 3))
                r1 = rp.next()
                k.act(r1.v, ssp.v, AF.Ln, scale=1.0 / 512, bias=EPS)
                r2 = rp.next()
                k.act(r2.v, r1.v, AF.Exp, scale=-0.5)
                for h in range(4):
                    o2 = o2p.next()
                    r2b = View(r2, r2.ap[:, h * 128:(h + 1) * 128].unsqueeze(1).broadcast_to([128, 4, 128]))
                    k.tt(o2.v, oTps[h].v.re("p (c t) -> p c t", c=4), r2b, ALU.mult)
                    k.tt(ost[:, 4 * h:4 * h + 4, blk], o2.v, g16[:, 4 * h:4 * h + 4, blk], ALU.mult, eng='pool')
            k.dma(fm_view(RO, ti, 0, 16), ost.v, q='pool')
        k.phase_end()

    def phase_out(l, i, hyb, xsrc):
        k.phase_begin()
        pp = psum_pool(4)
        if hyb:
            wA = k.sb([128, 8, D], BF16, 'wA')
            for c in range(0, 8, 4):
                k.dma(wA[:, c:c + 4, :], View(HWOb[i], HWOb[i].ap[c * 128:(c + 4) * 128, :].rearrange("(c p) n -> p c n", p=128)))
            oap = sb_pool(2, [128, 8, 512], BF16, 'oa')
        else:
            wR = k.sb([128, 16, D], BF16, 'wR')
            for c in range(0, 16, 4):
                k.dma(wR[:, c:c + 4, :], View(RWOb[i], RWOb[i].ap[c * 128:(c + 4) * 128, :].rearrange("(c p) n -> p c n", p=128)))
            oap = sb_pool(2, [128, 16, 512], BF16, 'oa')
        xp = sb_pool(2, [128, 8, 512], F32, 'xt')
        gt = modc[l][:, 16:24]
        for ti in range(NTT):
            xt = xp.next()
            k.dma(xt.v, fm_view(xsrc, ti, 0, 8))
            oa = oap.next()
            if hyb:
                k.dma(oa.v, fm_view(OMIX, ti, 0, 8))
            else:
                k.dma(oa.v, fm_view(RO, ti, 0, 16))
            for dc in range(8):
                ps = pp.next()
                ds_ = slice(dc * 128, (dc + 1) * 128)
                if hyb:
                    for c in range(8):
                        k.mm(ps.v, wA[:, c, ds_], oa[:, c, :], start=(c == 0), stop=(c == 7))
                else:
                    for c in range(16):
                        k.mm(ps.v, wR[:, c, ds_], oa[:, c, :], start=(c == 0), stop=(c == 15))
                k.stt(xt[:, dc, :], ps.v, gt[:, dc:dc + 1], xt[:, dc, :], ALU.mult, ALU.add)
            k.dma(fm_view(XT, ti, 0, 8), xt.v, q='pool')
        k.phase_end()

    def phase_ffn(l, xdst):
        k.phase_begin()
        pp = psum_pool(8)
        W1 = k.sb([128, 8, 2 * FFN_H], BF16, 'W1')
        for kc in range(8):
            k.dma(W1[:, kc, :], FWIb[l][kc * 128:(kc + 1) * 128, :])
        w2p = sb_pool(2, [128, 22, 128], BF16, 'w2')
        xp = sb_pool(2, [128, 8, 512], F32, 'xt')
        hp = sb_pool(2, [128, 8, 512], BF16, 'hT')
        ap_ = sb_pool(1, [128, 22, 512], BF16, 'actT')
        sp_ = sb_pool(3, [128, 512], F32, 'sl')
        norm_tile = make_norm(pp)
        gt = modc[l][:, 40:48]
        for ti in range(NTT):
            xt = xp.next()
            k.dma(xt.v, fm_view(XT, ti, 0, 8))
            hT = hp.next()
            norm_tile(xt, A_f[l], modc[l][:, 24:32], hT)
            aT = ap_.next()
            for j in range(22):
                pg = pp.next()
                pu = pp.next()
                for kc in range(8):
                    k.mm(pg.v, W1[:, kc, j * 128:(j + 1) * 128], hT[:, kc, :], start=(kc == 0), stop=(kc == 7))
                for kc in range(8):
                    k.mm(pu.v, W1[:, kc, FFN_H + j * 128:FFN_H + (j + 1) * 128], hT[:, kc, :], start=(kc == 0), stop=(kc == 7))
                s = sp_.next()
                k.act(s.v, pg.v, AF.Silu)
                k.tt(aT[:, j, :], s.v, pu.v, ALU.mult)
            for dc in range(8):
                w2 = w2p.next()
                k.dma(w2.v, View(FWOb[l], FWOb[l].ap[dc]))
                ps = pp.next()
                for j in range(22):
                    k.mm(ps.v, w2[:, j, :], aT[:, j, :], start=(j == 0), stop=(j == 21))
                k.stt(xt[:, dc, :], ps.v, gt[:, dc:dc + 1], xt[:, dc, :], ALU.mult, ALU.add)
            k.dma(fm_view(xdst, ti, 0, 8), xt.v, q='pool')
        k.phase_end()

    if want('cast'):
        phase_cast()
    if want('ada'):
        phase_ada()
    for l in range(n_layers):
        i = l // 2
        xsrc = XT_in if l == 0 else XT
        xdst = OUT if l == n_layers - 1 else XT
        if l % 2 == 0:
            if want(f'p1_{l}'):
                phase_p1_hyb(l, i, xsrc)
            if want(f'gdn_{l}'):
                phase_gdn(l, i)
            if want(f'sb_{l}'):
                phase_sb(l, i)
            if want(f'out_{l}'):
                phase_out(l, i, True, xsrc)
        else:
            if want(f'p1_{l}'):
                phase_p1_ret(l, i, xsrc)
            if want(f'ret_{l}'):
                phase_ret(l, i)
            if want(f'out_{l}'):
                phase_out(l, i, False, xsrc)
        if want(f'ffn_{l}'):
            phase_ffn(l, xdst)
    k.barrier()
    k.finalize()
    return nc, k


def host_inputs(b, T, x, c, ada_w, ada_b, norm_mix, norm_ffn, hyb_w_in, hyb_conv, gdn_a_log, gdn_dt_bias,
                gdn_norm, sb_q_norm, sb_k_norm, hyb_w_out, ret_w_in, ret_w_out, ffn_w_in, ffn_w_out, shared):
    f = np.float32
    m = dict(shared)
    m["xT"] = np.ascontiguousarray(x[b, :T].T)
    m["c_col"] = np.ascontiguousarray(c[b].reshape(8, 128).T)
    return m


def host_shared(T, ada_w, ada_b, norm_mix, norm_ffn, hyb_w_in, hyb_conv, gdn_a_log, gdn_dt_bias,
                gdn_norm, sb_q_norm, sb_k_norm, hyb_w_out, ret_w_in, ret_w_out, ffn_w_in, ffn_w_out):
    ca = np.ascontiguousarray
    m = {}
    m["ada_w"] = ca(ada_w)
    m["ada_b_col"] = ca(ada_b.reshape(4, 48, 128).transpose(0, 2, 1))
    m["norm_mix_col"] = ca(norm_mix.reshape(4, 8, 128).transpose(0, 2, 1))
    m["norm_ffn_col"] = ca(norm_ffn.reshape(4, 8, 128).transpose(0, 2, 1))
    m["hyb_w_in"] = ca(hyb_w_in)
    m["conv_col"] = ca(hyb_conv.reshape(2, 4, 12, 128).transpose(0, 3, 2, 1))
    m["a_log_b"] = ca(np.broadcast_to(gdn_a_log[:, None, :], (2, 128, 4)))
    m["dt_bias_b"] = ca(np.broadcast_to(gdn_dt_bias[:, None, :], (2, 128, 4)))
    m["gdn_norm_col"] = ca(gdn_norm.reshape(2, 128, 1))
    m["sb_q_norm_col"] = ca(np.concatenate([sb_q_norm, sb_q_norm], axis=1).reshape(2, 128, 1))
    m["sb_k_norm_col"] = ca(np.concatenate([sb_k_norm, sb_k_norm], axis=1).reshape(2, 128, 1))
    m["hyb_w_out"] = ca(hyb_w_out)
    m["ret_w_in"] = ca(ret_w_in)
    m["ret_w_out"] = ca(ret_w_out)
    m["ffn_w_in"] = ca(ffn_w_in)
    m["ffn_w_out"] = ca(ffn_w_out)
    m["cst"] = make_consts()
    m["rope"] = ca(make_rope(T).reshape(256, T))
    return m


_CACHE = {}


def kernel(x, c, ada_w, ada_b, norm_mix, norm_ffn, hyb_w_in, hyb_conv, gdn_a_log, gdn_dt_bias,
           gdn_norm, sb_q_norm, sb_k_norm, hyb_w_out, ret_w_in, ret_w_out, ffn_w_in, ffn_w_out):
    args = [np.asarray(a, dtype=np.float32) for a in
            (x, c, ada_w, ada_b, norm_mix, norm_ffn, hyb_w_in, hyb_conv, gdn_a_log, gdn_dt_bias,
             gdn_norm, sb_q_norm, sb_k_norm, hyb_w_out, ret_w_in, ret_w_out, ffn_w_in, ffn_w_out)]
    x, c = args[0], args[1]
    B, T, _ = x.shape
    shared = host_shared(T, *args[2:])
    in_maps = [host_inputs(b, T, *args, shared) for b in range(B)]
    if T not in _CACHE:
        _CACHE[T] = build(T)[0]
    nc = _CACHE[T]
    res = run_bass_kernel_spmd(nc, in_maps, core_ids=list(range(B)))
    out = np.stack([np.asarray(r["outT"]).T for r in res.results], axis=0)
    return np.ascontiguousarray(out.astype(np.float32))
```

```python
import numpy as np
import concourse.bass as bass
import concourse.mybir as mybir

F32 = mybir.dt.float32
BF16 = mybir.dt.bfloat16
AF = mybir.ActivationFunctionType
ALU = mybir.AluOpType
AX = mybir.AxisListType

EPOCH = 30000
COMPUTE = ('pe', 'act', 'dve', 'pool')
ALLQ = ('pe', 'act', 'dve', 'pool', 'sp')


class Buf:
    __slots__ = ('ap', 'name', 'last_w', 'readers', 'sem', 'cnt', 'space')

    def __init__(self, ap, name='', space='sb'):
        self.ap = ap
        self.name = name
        self.last_w = None
        self.readers = []
        self.sem = None
        self.cnt = 0
        self.space = space

    def __getitem__(self, idx):
        return View(self, self.ap[idx])

    @property
    def v(self):
        return View(self, self.ap)


class View:
    __slots__ = ('buf', 'ap')

    def __init__(self, buf, ap):
        self.buf = buf
        self.ap = ap

    def __getitem__(self, idx):
        return View(self.buf, self.ap[idx])

    def re(self, pat, **kw):
        return View(self.buf, self.ap.rearrange(pat, **kw))

    def bitcast(self, dt):
        return View(self.buf, self.ap.bitcast(dt))


class Op:
    __slots__ = ('eng', 'fn', 'deps', 'signal', 'n', 'is_dma', 'slot', 'val', 'ndma', 'tag', 'sem')

    def __init__(self, eng, fn, is_dma=False, slot=None, ndma=0, tag=''):
        self.eng = eng
        self.fn = fn
        self.deps = ()
        self.signal = False
        self.n = -1
        self.is_dma = is_dma
        self.slot = slot
        self.val = 0
        self.ndma = ndma
        self.tag = tag


def _bufs(xs):
    out = []
    for x in xs:
        if x is None:
            continue
        if isinstance(x, View):
            out.append(x.buf)
        elif isinstance(x, Buf):
            out.append(x)
        elif isinstance(x, (list, tuple)):
            out.extend(_bufs(x))
    return out


class Kern:
    def __init__(self, nc):
        self.nc = nc
        self.ops = {q: [] for q in ALLQ}
        self.all_ops = []
        self.sb_off = 0
        self.sb_names = 0
        self.last_dma_by_slot = {}
        import contextlib
        self.stacks = [contextlib.ExitStack()]
        self.free_sems = {}
        self.marks = []
        self.phase_slots = [[]]

    def sb(self, shape, dtype, name=None):
        self.sb_names += 1
        nm = f"sb{self.sb_names}_{name or ''}"
        cm = self.nc.sbuf_tensor(nm, list(shape), dtype)
        t = self.stacks[-1].enter_context(cm)
        return Buf(t.ap(), nm, 'sb')

    def phase_begin(self):
        import contextlib
        self.stacks.append(contextlib.ExitStack())
        self.phase_slots.append([])

    def phase_end(self):
        self.marks.append({q: sum(1 for o in self.ops[q] if o.fn is not None and not o.is_dma) for q in COMPUTE})
        self.barrier()
        self.stacks.pop().close()
        for slot, q in self.phase_slots.pop():
            self.free_sems.setdefault(q, []).append((slot.sem[q], slot.cnt[q]))

    def add(self, eng, fn, reads=(), writes=(), is_dma=False, slot=None, ndma=0, tag=''):
        op = Op(eng, fn, is_dma, slot, ndma, tag)
        R = _bufs(reads)
        W = _bufs(writes)
        deps = set()
        for b in R:
            if b.last_w is not None:
                deps.add(b.last_w)
            if b.space == 'ps':
                for r in b.readers:
                    if r.eng != eng:
                        deps.add(r)
        for b in W:
            if b.last_w is not None:
                deps.add(b.last_w)
            deps.update(b.readers)
        deps.discard(op)
        op.deps = tuple(deps)
        for b in W:
            b.last_w = op
            b.readers = []
        for b in R:
            if b in W:
                continue
            if not is_dma:
                b.readers = [r for r in b.readers if r.is_dma or r.eng != eng]
            b.readers.append(op)
        if is_dma:
            slot.cnt[eng] += ndma
            op.val = 16 * slot.cnt[eng]
            op.sem = slot.sem[eng]
            assert op.val < 60000, f"dma sem overflow on {slot.name}"
            self.last_dma_by_slot[(id(slot), eng)] = op
        self.ops[eng].append(op)
        self.all_ops.append(op)
        return op

    def barrier(self):
        lasts = []
        for q in COMPUTE:
            for o in reversed(self.ops[q]):
                if not o.is_dma and o.fn is not None:
                    lasts.append(o)
                    break
        lasts.extend(self.last_dma_by_slot.values())
        self.last_dma_by_slot = {}
        for q in ALLQ:
            op = Op(q, None)
            op.deps = tuple(lasts)
            self.ops[q].append(op)
            self.all_ops.append(op)

    def mm(self, out, lhsT, rhs, start=True, stop=True, **kw):
        o, l, r = out.ap, lhsT.ap, rhs.ap
        return self.add('pe', lambda e: e.matmul(o, l, r, start=start, stop=stop, **kw),
                        reads=[lhsT, rhs], writes=[out])

    def transpose(self, out, in_, ident):
        o, i, d = out.ap, in_.ap, ident.ap
        return self.add('pe', lambda e: e.transpose(o, i, d), reads=[in_, ident], writes=[out])

    def act(self, out, in_, func, bias=None, scale=None, accum_out=None, eng='act'):
        kw = {}
        reads = [in_]
        if bias is not None:
            if isinstance(bias, View):
                kw['bias'] = bias.ap
                reads.append(bias)
            else:
                kw['bias'] = bias
        if scale is not None:
            if isinstance(scale, View):
                kw['scale'] = scale.ap
                reads.append(scale)
            else:
                kw['scale'] = scale
        writes = [out]
        if accum_out is not None:
            kw['accum_out'] = accum_out.ap
            writes.append(accum_out)
        o, i = out.ap, in_.ap
        return self.add('act', lambda e: e.activation(o, i, func, **kw), reads=reads, writes=writes)

    def tt(self, out, in0, in1, op, eng='dve'):
        o, a, b = out.ap, in0.ap, in1.ap
        return self.add(eng, lambda e: e.tensor_tensor(o, a, b, op), reads=[in0, in1], writes=[out])

    def ts(self, out, in0, s1, op0, s2=None, op1=None, eng='dve', accum_out=None):
        reads = [in0]
        a1 = s1
        if isinstance(s1, View):
            a1 = s1.ap
            reads.append(s1)
        a2 = s2
        if isinstance(s2, View):
            a2 = s2.ap
            reads.append(s2)
        o, i = out.ap, in0.ap
        kw = {}
        writes = [out]
        if accum_out is not None:
            kw['accum_out'] = accum_out.ap
            writes.append(accum_out)
        if op1 is None:
            return self.add(eng, lambda e: e.tensor_scalar(o, i, a1, None, op0, **kw), reads=reads, writes=writes)
        return self.add(eng, lambda e: e.tensor_scalar(o, i, a1, a2, op0, op1, **kw), reads=reads, writes=writes)

    def stt(self, out, in0, scalar, in1, op0, op1, eng='dve'):
        reads = [in0, in1]
        sc = scalar
        if isinstance(scalar, View):
            sc = scalar.ap
            reads.append(scalar)
        o, a, b = out.ap, in0.ap, in1.ap
        return self.add(eng, lambda e: e.scalar_tensor_tensor(o, a, sc, b, op0, op1), reads=reads, writes=[out])

    def scan(self, out, d0, d1, initial, op0, op1):
        reads = [d0, d1]
        ini = initial
        if isinstance(initial, View):
            ini = initial.ap
            reads.append(initial)
        o, a, b = out.ap, d0.ap, d1.ap
        return self.add('dve', lambda e: e.tensor_tensor_scan(o, a, b, ini, op0, op1), reads=reads, writes=[out])

    def copy(self, out, in_, eng='dve'):
        o, i = out.ap, in_.ap
        if eng == 'act':
            return self.add('act', lambda e: e.copy(o, i), reads=[in_], writes=[out])
        return self.add(eng, lambda e: e.tensor_copy(o, i), reads=[in_], writes=[out])

    def memset(self, out, val, eng='pool'):
        o = out.ap
        return self.add(eng, lambda e: e.memset(o, val), reads=[], writes=[out])

    def dma(self, out, in_, q='sp', slot=None, reads=None, writes=None, **kw):
        outs = out if isinstance(out, (list, tuple)) else [out]
        ins = in_ if isinstance(in_, (list, tuple)) else [in_]
        pairs = [(o.ap, i.ap) for o, i in zip(outs, ins)]
        if slot is None:
            slot = outs[0].buf if outs[0].buf.space == 'sb' else ins[0].buf
        if slot.sem is None:
            slot.sem = {}
            slot.cnt = {}
        if q not in slot.sem:
            if self.free_sems.get(q):
                slot.sem[q], slot.cnt[q] = self.free_sems[q].pop()
            else:
                slot.sem[q] = self.new_sem()
                slot.cnt[q] = 0
            self.phase_slots[-1].append((slot, q))
        sem_ = slot.sem[q]

        def fn(e, pairs=pairs, sem=sem_, kw=kw):
            last = None
            for (o, i) in pairs:
                last = e.dma_start(out=o, in_=i, **kw).then_inc(sem, 16)
            return None
        op = self.add(q, fn, reads=ins if reads is None else reads,
                      writes=outs if writes is None else writes,
                      is_dma=True, slot=slot, ndma=len(pairs))
        return op

    def new_sem(self):
        self._nsem = getattr(self, '_nsem', 0) + 1
        cm = self.nc.semaphore(f"s{self._nsem}")
        sem = cm.__enter__()
        self._sem_cms = getattr(self, '_sem_cms', [])
        self._sem_cms.append(cm)
        return sem

    def finalize(self):
        nc = self.nc
        for op in self.all_ops:
            for d in op.deps:
                if d.is_dma:
                    continue
                if d.eng == 'pe' and op.eng == 'pe' and not op.is_dma:
                    continue
                d.signal = True
        eng_sems = {}
        for q in ALLQ:
            n = 0
            for op in self.ops[q]:
                if op.signal and not op.is_dma:
                    op.n = n
                    n += 1
            eng_sems[q] = [self.new_sem() for _ in range((n + EPOCH - 1) // EPOCH)]
        self.n_instr = {q: len(self.ops[q]) for q in ALLQ}

        def emit(q, e):
            waited_eng = {}
            waited_dma = {}
            for op in self.ops[q]:
                need_eng = {}
                need_dma = {}
                for d in op.deps:
                    if d.is_dma:
                        k = d.sem
                        if waited_dma.get(id(k), 0) < d.val:
                            if need_dma.get(id(k), (None, 0))[1] < d.val:
                                need_dma[id(k)] = (k, d.val)
                    else:
                        if d.eng == 'pe' and q == 'pe' and not op.is_dma:
                            continue
                        if waited_eng.get(d.eng, -1) < d.n:
                            if need_eng.get(d.eng, -1) < d.n:
                                need_eng[d.eng] = d.n
                for pe_, n in need_eng.items():
                    e.wait_ge(eng_sems[pe_][n // EPOCH], n % EPOCH + 1)
                    waited_eng[pe_] = n
                for _, (k, v) in need_dma.items():
                    e.wait_ge(k, v)
                    waited_dma[id(k)] = v
                if op.fn is None:
                    continue
                ins = op.fn(e)
                if op.signal and not op.is_dma:
                    ins.then_inc(eng_sems[q][op.n // EPOCH], 1)

        with nc.Block() as block:
            @block.tensor
            def _(e):
                emit('pe', e)

            @block.scalar
            def _(e):
                emit('act', e)

            @block.vector
            def _(e):
                emit('dve', e)

            @block.gpsimd
            def _(e):
                emit('pool', e)

            @block.sync
            def _(e):
                emit('sp', e)

from concourse.bass_utils import run_bass_kernel_spmd

D = 1024
EPS = 1e-6
FFN_H = 2816
HYB_IN = 3592
RET_IN = 6144
BIG = 30000.0
RET_LG = [float(np.log1p(-np.exp2(-5.0 - h))) for h in range(4)]

CST = {}
_off = 0
for _n, _w in [('ident', 128), ('tri2', 128), ('triu2', 128), ('cind0', 128), ('cind1', 128),
               ('mb1', 128), ('mb2', 128), ('mlow', 128), ('ones64', 128),
               ('DT0', 128), ('DT1', 128), ('DT2', 128), ('DT3', 128),
               ('QD0', 128), ('QD1', 128), ('QD2', 128), ('QD3', 128), ('kds', 4),
               ('identS', 128), ('mk1', 128), ('mk2', 128)]:
    CST[_n] = (_off, _w)
    _off += _w
NCST = _off


def make_consts():
    c = np.zeros((128, NCST), np.float64)
    i = np.arange(128)
    same = (i[:, None] // 64) == (i[None, :] // 64)

    def put(n, a):
        o, w = CST[n]
        c[:, o:o + w] = a
    put('ident', np.eye(128))
    put('tri2', (same & (i[:, None] <= i[None, :])) * 1.0)
    put('triu2', (same & (i[:, None] > i[None, :])) * 1.0)
    put('cind0', np.broadcast_to((i < 64)[:, None] * 1.0, (128, 128)))
    put('cind1', np.broadcast_to((i >= 64)[:, None] * 1.0, (128, 128)))
    put('mb1', np.where(same & (i[:, None] > i[None, :]), 0.0, BIG))
    put('mb2', np.where(same & (i[None, :] >= i[:, None]), 0.0, -BIG))
    put('mlow', (i[None, :] < i[:, None]) * 1.0)
    put('ones64', same * 1.0)
    put('identS', (i[None, :] == i[:, None] + 64) * 1.0)
    put('mk1', np.where(i[None, :] >= i[:, None], -BIG, 0.0))
    put('mk2', np.where(i[None, :] >= i[:, None] + 64, -BIG, 0.0))
    for h in range(4):
        lg = RET_LG[h]
        dif = i[None, :] - i[:, None]
        put(f'DT{h}', np.where(dif >= 0, np.exp(lg * np.maximum(dif, 0)), 0.0))
        put(f'QD{h}', np.broadcast_to(np.exp(lg * (i + 1.0))[None, :], (128, 128)))
        o, w = CST['kds']
        c[:, o + h] = np.exp(lg * (127.0 - i))
    return c.astype(np.float32)


def make_rope(T):
    inv = 1.0 / (10000.0 ** (np.arange(0, 256, 2, dtype=np.float64) / 256.0))
    ang = inv[:, None] * np.arange(T, dtype=np.float64)[None, :]
    return np.stack([np.cos(ang), np.sin(ang)]).astype(np.float32)


class Rot:
    def __init__(self, mk, n):
        self.bufs = [mk(i) for i in range(n)]
        self.i = 0

    def next(self):
        b = self.bufs[self.i % len(self.bufs)]
        self.i += 1
        return b


class DT_:
    def __init__(self, nc, name, rows, T, dt, fm=True, kind="Internal"):
        shape = [rows, T] if fm else [T, rows]
        self.full = nc.dram_tensor(name, shape, dt, kind=kind).ap()
        self.fm = fm
        self.tiles = []
        for i in range(T // 512):
            ap = self.full[:, i * 512:(i + 1) * 512] if fm else self.full[i * 512:(i + 1) * 512, :]
            self.tiles.append(Buf(ap, f"{name}_t{i}", 'dram'))

    def tile(self, i):
        return self.tiles[i]

    def whole(self, ap):
        return View(self.tiles[0], ap)


GDN_STAGE = 99
SB_W = 1024


def build(T, debug=False, phases=None, n_layers=4):
    nc = bass.Bass("TRN2", target_bir_lowering=False)
    k = Kern(nc)
    NT = T // 128
    NTT = T // 512
    skind = "ExternalOutput" if debug else "Internal"

    def want(p):
        return phases is None or p in phases

    def din(name, shape, dt=F32, used=True):
        return Buf(nc.dram_tensor(name, list(shape), dt, kind="ExternalInput" if used else "Internal").ap(), name, 'dram')

    XT_in = DT_(nc, "xT", D, T, F32, True, "ExternalInput")
    C_in = din("c_col", [128, 8])
    ADAW = din("ada_w", [4, D, 6 * D], used=want("ada"))
    ADAB = din("ada_b_col", [4, 128, 48])
    NMIX = din("norm_mix_col", [4, 128, 8])
    NFFN = din("norm_ffn_col", [4, 128, 8])
    HWI = din("hyb_w_in", [2, D, HYB_IN], used=want("cast"))
    CONV = din("conv_col", [2, 128, 12, 4])
    ALOG = din("a_log_b", [2, 128, 4])
    DTB = din("dt_bias_b", [2, 128, 4])
    GNORM = din("gdn_norm_col", [2, 128, 1])
    QN = din("sb_q_norm_col", [2, 128, 1])
    KN = din("sb_k_norm_col", [2, 128, 1])
    HWO = din("hyb_w_out", [2, D, D], used=want("cast"))
    RWI = din("ret_w_in", [2, D, RET_IN], used=want("cast"))
    RWO = din("ret_w_out", [2, 2048, D], used=want("cast"))
    FWI = din("ffn_w_in", [4, D, 2 * FFN_H], used=want("cast"))
    FWO = din("ffn_w_out", [4, FFN_H, D], used=want("cast"))
    CSTD = din("cst", [128, NCST])
    ROPE = DT_(nc, "rope", 256, T, F32, True, "ExternalInput")
    OUT = DT_(nc, "outT", D, T, F32, True, "ExternalOutput")

    def dscr(name, shape, dt):
        return Buf(nc.dram_tensor(name, list(shape), dt, kind=skind).ap(), name, 'dram')
    HWIb = [dscr(f"hwib{i}", [D, HYB_IN], BF16) for i in range(2)]
    HWOb = [dscr(f"hwob{i}", [D, D], BF16) for i in range(2)]
    RWIb = [dscr(f"rwib{i}", [D, RET_IN], BF16) for i in range(2)]
    RWOb = [dscr(f"rwob{i}", [2048, D], BF16) for i in range(2)]
    FWIb = [dscr(f"fwib{i}", [D, 2 * FFN_H], BF16) for i in range(4)]
    FWOb = [dscr(f"fwob{i}", [8, 128, 22, 128], BF16) for i in range(4)]
    XT = DT_(nc, "xres", D, T, F32, True, skind)
    GQ = DT_(nc, "gq", 512, T, BF16, True, skind)
    GK = DT_(nc, "gk", 512, T, BF16, True, skind)
    GV = DT_(nc, "gv", 512, T, BF16, True, skind)
    GS = DT_(nc, "gs", 512, T, BF16, True, skind)
    GB = DT_(nc, "gbeta", 8, T, F32, False, skind)
    SQ = DT_(nc, "sq", 512, T, BF16, True, skind)
    SK = DT_(nc, "sk", 512, T, BF16, True, skind)
    SV = DT_(nc, "sv", 512, T, BF16, False, skind)
    OMIX = DT_(nc, "omix", 1024, T, BF16, True, skind)
    RQ = DT_(nc, "rq", 1024, T, BF16, True, skind)
    RK = DT_(nc, "rk", 1024, T, BF16, True, skind)
    RV = DT_(nc, "rv", 2048, T, BF16, False, skind)
    RG = DT_(nc, "rg", 2048, T, BF16, True, skind)
    RO = DT_(nc, "ro", 2048, T, BF16, True, skind)

    cst = k.sb([128, NCST], F32, 'cst')
    k.dma(cst.v, CSTD.v)

    def C(n):
        o, w = CST[n]
        return cst[:, o:o + w]
    ident = C('ident')
    ones_f = k.sb([128, 1024], F32, 'ones_f')
    k.memset(ones_f.v, 1.0)
    ones_bf = k.sb([128, 128], BF16, 'ones_bf')
    k.memset(ones_bf.v, 1.0)
    ident_bf = k.sb([128, 128], BF16, 'ident_bf')
    k.copy(ident_bf.v, ident, eng='dve')
    ones64_bf = k.sb([128, 128], BF16, 'ones64_bf')
    k.copy(ones64_bf.v, C('ones64'), eng='dve')
    modc = [k.sb([128, 48], F32, f'modc{l}') for l in range(4)]
    A_m = [k.sb([128, 8], F32, f'Am{l}') for l in range(4)]
    A_f = [k.sb([128, 8], F32, f'Af{l}') for l in range(4)]

    def psum_pool(n, shape=(128, 512), dt=F32, name='ps'):
        def mk(i):
            cm = nc.psum_tensor(f"{name}{i}_{k.sb_names}", list(shape), dt)
            k.sb_names += 1
            t = k.stacks[-1].enter_context(cm)
            return Buf(t.ap(), f"{name}{i}", 'ps')
        return Rot(mk, n)

    def sb_pool(n, shape, dt, name):
        return Rot(lambda i: k.sb(shape, dt, f"{name}{i}"), n)

    act_rr = [0]

    def phase_cast():
        k.phase_begin()
        CW = 2048
        fp = sb_pool(3, [128, CW], F32, 'cf')
        bp = sb_pool(3, [128, CW], BF16, 'cb')
        engs = ['pool', 'act', 'dve']
        cnt = 0
        jobs = []
        for i in range(2):
            if n_layers > 2 * i:
                jobs += [(HWI[i], HWIb[i]), (HWO[i], HWOb[i])]
            if n_layers > 2 * i + 1:
                jobs += [(RWI[i], RWIb[i]), (RWO[i], RWOb[i])]
        for l in range(n_layers):
            jobs += [(FWI[l], FWIb[l]), (FWO[l], FWOb[l])]
        for src, dst in jobs:
            if len(dst.ap.shape) == 4:
                for r in range(22):
                    f = fp.next()
                    b = bp.next()
                    k.dma(f[:, 0:D], src[r * 128:(r + 1) * 128, :])
                    k.copy(b[:, 0:D], f[:, 0:D], eng=engs[cnt % 3])
                    cnt += 1
                    k.dma(View(dst, dst.ap[:, :, r, :].rearrange("c p n -> p c n")),
                          View(b, b.ap[:, 0:D].rearrange("p (c n) -> p c n", c=8)), q='pool')
                continue
            K_, N_ = dst.ap.shape
            for r in range(K_ // 128):
                for c0 in range(0, N_, CW):
                    w = min(CW, N_ - c0)
                    f = fp.next()
                    b = bp.next()
                    k.dma(f[:, 0:w], src[r * 128:(r + 1) * 128, c0:c0 + w])
                    k.copy(b[:, 0:w], f[:, 0:w], eng=engs[cnt % 3])
                    cnt += 1
                    k.dma(dst[r * 128:(r + 1) * 128, c0:c0 + w], b[:, 0:w], q='pool')
        k.phase_end()

    def phase_ada():
        k.phase_begin()
        pp = psum_pool(2)
        ccol = k.sb([128, 8], F32, 'ccol')
        k.dma(ccol.v, C_in.v)
        cact = k.sb([128, 8], F32, 'cact')
        k.act(cact.v, ccol.v, AF.Silu)
        wp = sb_pool(2, [128, 8, 512], F32, 'adaw')
        for l in range(n_layers):
            ps = pp.next()
            for nch in range(12):
                w = wp.next()
                k.dma([w[:, kc, :] for kc in range(8)],
                      [ADAW[l, kc * 128:(kc + 1) * 128, nch * 512:(nch + 1) * 512] for kc in range(8)])
                for jj in range(4):
                    j = nch * 4 + jj
                    for kc in range(8):
                        k.mm(ps[:, j:j + 1], w[:, kc, jj * 128:(jj + 1) * 128], cact[:, kc:kc + 1],
                             start=(kc == 0), stop=(kc == 7))
            bcol = k.sb([128, 48], F32, 'bcol')
            k.dma(bcol.v, ADAB[l])
            k.tt(modc[l].v, ps[:, 0:48], bcol.v, ALU.add)
            for (A, NG, c0) in ((A_m[l], NMIX, 8), (A_f[l], NFFN, 32)):
                g = k.sb([128, 8], F32, 'g')
                k.dma(g.v, NG[l])
                t = k.sb([128, 8], F32, 't')
                k.ts(t.v, modc[l][:, c0:c0 + 8], 1.0, ALU.add)
                k.tt(A.v, t.v, g.v, ALU.mult)
        k.phase_end()

    def make_norm(pp):
        sqp = sb_pool(2, [128, 512], BF16, 'nsq')
        rp = sb_pool(2, [128, 512], F32, 'nr')
        tp = sb_pool(2, [128, 512], F32, 'nt')

        def norm_tile(xt, A, sh, hT):
            ps = pp.next()
            for kc in range(8):
                sq = sqp.next()
                k.act(sq.v, xt[:, kc, :], AF.Square)
                k.mm(ps.v, ones_bf.v, sq.v, start=(kc == 0), stop=(kc == 7))
            r1 = rp.next()
            k.act(r1.v, ps.v, AF.Ln, scale=1.0 / D, bias=EPS)
            rstd = rp.next()
            k.act(rstd.v, r1.v, AF.Exp, scale=-0.5)
            for kc in range(8):
                t = tp.next()
                k.stt(t.v, xt[:, kc, :], A[:, kc:kc + 1], rstd.v, ALU.mult, ALU.mult)
                k.act(hT[:, kc, :], t.v, AF.Identity, bias=sh[:, kc:kc + 1])
        return norm_tile

    def fm_view(dt, ti, r0, nch, p=128):
        b = dt.tile(ti)
        return View(b, b.ap[r0:r0 + nch * p, :].rearrange("(c p) t -> p c t", p=p))

    def tm_view(dt, ti, c0, c1):
        b = dt.tile(ti)
        return View(b, b.ap[:, c0:c1].rearrange("(t p) c -> p t c", p=128))

    def phase_p1_hyb(l, i, xsrc):
        k.phase_begin()
        pp = psum_pool(8)
        W = k.sb([128, 8, HYB_IN], BF16, 'Whyb')
        for kc in range(8):
            k.dma(W[:, kc, :], HWIb[i][kc * 128:(kc + 1) * 128, :])
        conv = k.sb([128, 12, 4], F32, 'conv')
        k.dma(conv.v, CONV[i])
        dtb = k.sb([128, 4], F32, 'dtb')
        k.dma(dtb.v, DTB[i])
        alog = k.sb([128, 4], F32, 'alog')
        k.dma(alog.v, ALOG[i])
        negA = k.sb([128, 4], F32, 'negA')
        k.act(negA.v, alog.v, AF.Exp)
        k.ts(negA.v, negA.v, -1.0, ALU.mult, eng='pool')
        qg = k.sb([128, 1], F32, 'qg')
        k.dma(qg.v, QN[i])
        k.ts(qg.v, qg.v, 0.125, ALU.mult, eng='pool')
        kg = k.sb([128, 1], F32, 'kg')
        k.dma(kg.v, KN[i])
        halo = k.sb([128, 12, 3], F32, 'halo')
        k.memset(halo.v, 0.0)
        xp = sb_pool(2, [128, 8, 512], F32, 'xt')
        hp = sb_pool(2, [128, 8, 512], BF16, 'hT')
        rawp = sb_pool(5, [128, 515], F32, 'raw')
        yp = sb_pool(5, [128, 512], F32, 'y')
        sp_ = sb_pool(5, [128, 512], F32, 's')
        sqp = sb_pool(5, [128, 512], BF16, 'sq1')
        rp = sb_pool(9, [128, 512], F32, 'r')
        obp = sb_pool(8, [128, 512], BF16, 'ob')
        gbp = sb_pool(2, [128, 4, 8], F32, 'gbo')
        smp = sb_pool(6, [128, 4, 4], F32, 'sm')
        norm_tile = make_norm(pp)
        for ti in range(NTT):
            xt = xp.next()
            k.dma(xt.v, fm_view(xsrc, ti, 0, 8))
            hT = hp.next()
            norm_tile(xt, A_m[l], modc[l][:, 0:8], hT)
            for grp in range(3):
                ccs = list(range(4 * grp, 4 * grp + 4))
                raws, ys, ss, obs = {}, {}, {}, {}
                for cc in ccs:
                    ps = pp.next()
                    for kc in range(8):
                        k.mm(ps.v, W[:, kc, cc * 128:(cc + 1) * 128], hT[:, kc, :], start=(kc == 0), stop=(kc == 7))
                    raw = rawp.next()
                    k.copy(raw[:, 0:3], halo[:, cc, :], eng='pool')
                    k.copy(raw[:, 3:515], ps.v, eng='act')
                    k.copy(halo[:, cc, :], raw[:, 512:515], eng='pool')
                    raws[cc] = raw
                for cc in ccs:
                    raw = raws[cc]
                    y = yp.next()
                    k.ts(y.v, raw[:, 0:512], conv[:, cc, 0:1], ALU.mult)
                    for j in range(1, 4):
                        k.stt(y.v, raw[:, j:j + 512], conv[:, cc, j:j + 1], y.v, ALU.mult, ALU.add)
                    ys[cc] = y
                for cc in ccs:
                    s_ = sp_.next()
                    k.act(s_.v, ys[cc].v, AF.Silu)
                    ss[cc] = s_
                if grp < 2:
                    sqs, ps2s, r2s = {}, {}, {}
                    for cc in ccs:
                        sq = sqp.next()
                        k.act(sq.v, ss[cc].v, AF.Square)
                        sqs[cc] = sq
                    for cc in ccs:
                        ps2 = pp.next()
                        k.mm(ps2.v, ones_bf.v, sqs[cc].v)
                        ps2s[cc] = ps2
                    for cc in ccs:
                        r1 = rp.next()
                        k.act(r1.v, ps2s[cc].v, AF.Ln, bias=EPS)
                        r2 = rp.next()
                        k.act(r2.v, r1.v, AF.Exp, scale=-0.5)
                        r2s[cc] = r2
                for cc in ccs:
                    ob = obp.next()
                    hh = cc % 4
                    if grp == 0:
                        k.stt(ob.v, ss[cc].v, 128.0 ** -0.5, r2s[cc].v, ALU.mult, ALU.mult)
                        dst = GQ
                    elif grp == 1:
                        k.tt(ob.v, ss[cc].v, r2s[cc].v, ALU.mult, eng='pool')
                        dst = GK
                    else:
                        k.copy(ob.v, ss[cc].v, eng='pool')
                        dst = GV
                    k.dma(dst.tile(ti)[hh * 128:(hh + 1) * 128, :], ob.v, q='pool')
            pss = []
            for hh in range(4):
                ps = pp.next()
                c0 = 1536 + hh * 128
                for kc in range(8):
                    k.mm(ps.v, W[:, kc, c0:c0 + 128], hT[:, kc, :], start=(kc == 0), stop=(kc == 7))
                pss.append(ps)
            for hh in range(4):
                ob = obp.next()
                k.act(ob.v, pss[hh].v, AF.Silu)
                k.dma(GS.tile(ti)[hh * 128:(hh + 1) * 128, :], ob.v, q='pool')
            for grp in range(2):
                cs4 = list(range(4 * grp, 4 * grp + 4))
                pss, sqs, ps2s, r2s = {}, {}, {}, {}
                for c in cs4:
                    ps = pp.next()
                    c0 = 2056 + c * 128
                    for kc in range(8):
                        k.mm(ps.v, W[:, kc, c0:c0 + 128], hT[:, kc, :], start=(kc == 0), stop=(kc == 7))
                    pss[c] = ps
                for c in cs4:
                    sq = sqp.next()
                    k.act(sq.v, pss[c].v, AF.Square)
                    sqs[c] = sq
                for c in cs4:
                    ps2 = pp.next()
                    k.mm(ps2.v, ones64_bf.v, sqs[c].v)
                    ps2s[c] = ps2
                for c in cs4:
                    r1 = rp.next()
                    k.act(r1.v, ps2s[c].v, AF.Ln, scale=1.0 / 64, bias=EPS)
                    r2 = rp.next()
                    k.act(r2.v, r1.v, AF.Exp, scale=-0.5)
                    r2s[c] = r2
                for c in cs4:
                    ob = obp.next()
                    k.stt(ob.v, pss[c].v, (qg if c < 4 else kg)[:, 0:1], r2s[c].v, ALU.mult, ALU.mult)
                    dst = SQ if c < 4 else SK
                    cc = c % 4
                    k.dma(dst.tile(ti)[cc * 128:(cc + 1) * 128, :], ob.v, q='pool')
            for tb in range(4):
                ps = pp.next()
                for kc in range(8):
                    k.mm(ps.v, hT[:, kc, tb * 128:(tb + 1) * 128], W[:, kc, 3080:3592], start=(kc == 0), stop=(kc == 7))
                ob = obp.next()
                k.copy(ob.v, ps.v, eng='act' if tb % 2 else 'dve')
                k.dma(SV.tile(ti)[tb * 128:(tb + 1) * 128, :], ob.v, q='pool')
            ps = pp.next()
            for tb in range(4):
                for kc in range(8):
                    k.mm(ps[:, tb * 8:(tb + 1) * 8], hT[:, kc, tb * 128:(tb + 1) * 128], W[:, kc, 2048:2056],
                         start=(kc == 0), stop=(kc == 7))
            pv = ps[:, 0:32].re("p (t e) -> p t e", e=8)
            dtb_b = View(dtb, dtb.ap.unsqueeze(1).broadcast_to([128, 4, 4]))
            negA_b = View(negA, negA.ap.unsqueeze(1).broadcast_to([128, 4, 4]))
            z = smp.next()
            k.tt(z.v, pv[:, :, 0:4], dtb_b, ALU.add)
            e1 = smp.next()
            k.act(e1.v, z.v, AF.Exp)
            s1 = smp.next()
            k.act(s1.v, e1.v, AF.Ln, bias=1.0)
            gbo = gbp.next()
            k.tt(gbo[:, :, 0:4], s1.v, negA_b, ALU.mult)
            e2 = smp.next()
            k.act(e2.v, pv[:, :, 4:8], AF.Exp, scale=-1.0)
            d2 = smp.next()
            k.ts(d2.v, e2.v, 1.0, ALU.add)
            k.add('dve', lambda e, o=gbo.ap[:, :, 4:8], i_=d2.ap: e.reciprocal(o, i_), reads=[d2], writes=[gbo])
            k.dma(tm_view(GB, ti, 0, 8), gbo.v, q='pool')
        k.phase_end()

    def phase_gdn(l, i):
        k.phase_begin()
        pp = psum_pool(4)
        pc = psum_pool(2, name='psc')
        ppo = psum_pool(2, name='pso')
        gn = k.sb([128, 1], F32, 'gn')
        k.dma(gn.v, GNORM[i])
        ident4 = k.sb([128, 4, 128], F32, 'ident4')
        for h in range(4):
            k.copy(ident4[:, h, :], ident, eng='pool')
        S = k.sb([128, 4, 128], F32, 'S')
        k.memset(S.v, 0.0)
        ldp = {n: sb_pool(2, [128, 4, 512], BF16, n) for n in ('kT', 'qT', 'vT', 'gsT')}
        gbp = sb_pool(2, [128, 4, 8], F32, 'gbt')
        osp = sb_pool(2, [128, 4, 512], BF16, 'ost')
        f4 = {n: sb_pool(2, [128, 4, 128], F32, n) for n in
              ('kdec', 'kw', 'vb', 'gbc', 'E', 'ET', 'EQ', 't1', 'A', 'Aqk', 'u', 'wT', 'qd', 'osb', 'o2', 'r1', 'r2')}
        f4r = {n: sb_pool(3, [128, 4, 128], F32, n) for n in ('Sx', 'STx', 'PT')}
        sqp = sb_pool(2, [128, 4, 128], BF16, 'gsq')
        smp = {n: sb_pool(2, [128, w], F32, n) for n, w in (('cs', 16), ('ex', 16), ('ngc', 4), ('kws', 4), ('nb4', 4))}
        vnzp = [sb_pool(2, [128, 4, 128], F32, f'vnz{c}') for c in range(2)]
        for c in range(2):
            for b in vnzp[c].bufs:
                k.memset(b.v, 0.0)

        def f2(b):
            return b.v.re("p h d -> p (h d)")
        tctx = {}

        def tile_ctx(ti):
            if ti not in tctx:
                kT4 = ldp['kT'].next()
                k.dma(kT4.v, fm_view(GK, ti, 0, 4))
                qT4 = ldp['qT'].next()
                k.dma(qT4.v, fm_view(GQ, ti, 0, 4))
                vT4 = ldp['vT'].next()
                k.dma(vT4.v, fm_view(GV, ti, 0, 4))
                gs4 = ldp['gsT'].next()
                k.dma(gs4.v, fm_view(GS, ti, 0, 4))
                gbt = gbp.next()
                k.dma(gbt.v, tm_view(GB, ti, 0, 8))
                tctx[ti] = dict(kT4=kT4, qT4=qT4, vT4=vT4, gs4=gs4, gbt=gbt, ost=osp.next())
            return tctx[ti]

        def prep(ti, tb, out):
            c_ = tile_ctx(ti)
            kT4, qT4, vT4, gs4, gbt, ost = c_['kT4'], c_['qT4'], c_['vT4'], c_['gs4'], c_['gbt'], c_['ost']
            blk = slice(tb * 128, (tb + 1) * 128)
            g4 = gbt[:, tb, 0:4]
            b4 = gbt[:, tb, 4:8]
            cps = pp.next()
            k.mm(cps[:, 0:4], C('tri2'), g4)
            k.mm(cps[:, 4:8], C('triu2'), g4)
            k.mm(cps[:, 8:12], C('cind0'), g4)
            k.mm(cps[:, 12:16], C('cind1'), g4)
            cs = smp['cs'].next()
            k.copy(cs.v, cps[:, 0:16], eng='dve')
            ex = smp['ex'].next()
            k.act(ex.v, cps[:, 0:16], AF.Exp)
            ngc = smp['ngc'].next()
            k.ts(ngc.v, cs[:, 0:4], -1.0, ALU.mult, eng='pool')
            kws = smp['kws'].next()
            k.tt(kws.v, b4, ex[:, 0:4], ALU.mult, eng='pool')
            nb4 = smp['nb4'].next()
            k.ts(nb4.v, b4, -1.0, ALU.mult, eng='pool')
            yield
            trp = pp.next()
            tv = trp.v.bitcast(BF16)
            for h in range(4):
                k.transpose(tv[:, h * 128:(h + 1) * 128], kT4[:, h, blk], ident_bf.v)
            for h in range(4):
                k.transpose(tv[:, 512 + h * 128:512 + (h + 1) * 128], vT4[:, h, blk], ident_bf.v)
            kdec = f4['kdec'].next()
            kw = f4['kw'].next()
            vb = f4['vb'].next()
            for h in range(4):
                k.ts(kdec[:, h, :], tv[:, h * 128:(h + 1) * 128], ex[:, 4 + h:5 + h], ALU.mult)
                k.act(kw[:, h, :], tv[:, h * 128:(h + 1) * 128], AF.Identity, scale=kws[:, h:h + 1])
                k.act(vb[:, h, :], tv[:, 512 + h * 128:512 + (h + 1) * 128], AF.Identity, scale=b4[:, h:h + 1])
            yield
            gbc = f4['gbc'].next()
            for h in range(4):
                k.ts(gbc[:, h, :], ones_f[:, 0:128], g4[:, h:h + 1], ALU.mult, eng='pool')
            RA = pp.next()
            RB = pp.next()
            RC = pp.next()
            for h in range(4):
                hs = slice(h * 128, (h + 1) * 128)
                k.mm(RA[:, hs], gbc[:, h, :], C('tri2'), start=True, stop=False)
                k.mm(RA[:, hs], ident, C('mb1'), start=False, stop=True)
                k.mm(RB[:, hs], gbc[:, h, :], C('tri2'), start=True, stop=False)
                k.mm(RB[:, hs], ident, C('mb2'), start=False, stop=True)
                k.mm(RC[:, hs], gbc[:, h, :], C('tri2'))
            E = f4['E'].next()
            ET = f4['ET'].next()
            EQ = f4['EQ'].next()
            for h in range(4):
                hs = slice(h * 128, (h + 1) * 128)
                k.act(E[:, h, :], RA[:, hs], AF.Exp, scale=-1.0, bias=cs[:, h:h + 1])
                k.act(ET[:, h, :], RB[:, hs], AF.Exp, bias=ngc[:, h:h + 1])
            k.act(f2(EQ), RC.v, AF.Exp)
            yield
            KK = pp.next()
            KQ = pp.next()
            for h in range(4):
                hs = slice(h * 128, (h + 1) * 128)
                k.mm(KK[:, hs], kT4[:, h, blk], kT4[:, h, blk])
                k.mm(KQ[:, hs], kT4[:, h, blk], qT4[:, h, blk])
            t1 = f4['t1'].next()
            k.tt(f2(t1), KK.v, f2(E), ALU.mult)
            A = f4['A'].next()
            for h in range(4):
                k.ts(A[:, h, :], t1[:, h, :], nb4[:, h:h + 1], ALU.mult, eng='pool')
            Aqk = f4['Aqk'].next()
            k.tt(f2(Aqk), KQ.v, f2(ET), ALU.mult)
            yield
            ATp = pp.next()
            for h in range(4):
                k.transpose(ATp[:, h * 128:(h + 1) * 128], A[:, h, :], ident)
            ST_ = f4r['STx'].next()
            k.copy(f2(ST_), ATp.v, eng='act')
            PT = f4r['PT'].next()
            k.tt(f2(PT), ATp.v, f2(ident4), ALU.add)
            S_ = A
            for lev in range(1, 6):
                Sp = pp.next()
                for h in range(4):
                    k.mm(Sp[:, h * 128:(h + 1) * 128], ST_[:, h, :], S_[:, h, :])
                Sn = f4r['Sx'].next()
                k.copy(f2(Sn), Sp.v, eng='act')
                if lev < 5:
                    STp = pp.next()
                    for h in range(4):
                        k.mm(STp[:, h * 128:(h + 1) * 128], S_[:, h, :], ST_[:, h, :])
                    STn = f4r['STx'].next()
                    k.copy(f2(STn), STp.v, eng='dve')
                Pp = pp.next()
                for h in range(4):
                    k.mm(Pp[:, h * 128:(h + 1) * 128], Sn[:, h, :], PT[:, h, :])
                PTn = f4r['PT'].next()
                k.tt(f2(PTn), Pp.v, f2(PT), ALU.add)
                PT = PTn
                S_ = Sn
                yield
                if lev < 5:
                    ST_ = STn
            yield
            up = pp.next()
            wp_ = pp.next()
            for h in range(4):
                hs = slice(h * 128, (h + 1) * 128)
                k.mm(up[:, hs], PT[:, h, :], vb[:, h, :])
                k.mm(wp_[:, hs], kw[:, h, :], PT[:, h, :])
            u = f4['u'].next()
            k.copy(f2(u), up.v, eng='act')
            wT = f4['wT'].next()
            k.copy(f2(wT), wp_.v, eng='dve')
            qd = f4['qd'].next()
            k.tt(qd.v, qT4[:, :, blk], EQ.v, ALU.mult, eng='pool')
            out.update(dict(kdec=kdec, wT=wT, u=u, qd=qd, Aqk=Aqk, ex=ex, blk=blk, gs4=gs4, ost=ost))
            yield

        def chain(ti, tb, o, pump):
            kdec, wT, u, qd, Aqk, ex, blk, gs4, ost = (o[x] for x in ('kdec', 'wT', 'u', 'qd', 'Aqk', 'ex', 'blk', 'gs4', 'ost'))
            oTp = ppo.next()
            for c in range(2):
                cs_ = slice(c * 64, (c + 1) * 64)
                vnz = vnzp[c].next()
                vnp = pc.next()
                for h in range(4):
                    k.mm(vnp[:, h * 128:(h + 1) * 128], wT[:, h, :], S[:, h, :])
                k.tt(vnz[cs_, :, :].re("p h d -> p (h d)"), u[cs_, :, :].re("p h d -> p (h d)"), vnp[cs_, :], ALU.subtract)
                pump(2)
                for h in range(4):
                    o_ = oTp[:, h * 128 + c * 64:h * 128 + (c + 1) * 64]
                    k.mm(o_, S[:, h, :], qd[:, h, cs_], start=True, stop=False)
                    k.mm(o_, vnz[:, h, :], Aqk[:, h, cs_], start=False, stop=True)
                dSp = pc.next()
                for h in range(4):
                    k.mm(dSp[:, h * 128:(h + 1) * 128], kdec[:, h, :], vnz[:, h, :])
                for h in range(4):
                    k.stt(S[:, h, :], S[:, h, :], ex[:, 8 + 4 * c + h:9 + 4 * c + h], dSp[:, h * 128:(h + 1) * 128],
                          ALU.mult, ALU.add)
                pump(3)
            osb = f4['osb'].next()
            k.copy(f2(osb), oTp.v, eng='dve')
            sq = sqp.next()
            k.act(f2(sq), oTp.v, AF.Square)
            ssp = pc.next()
            for h in range(4):
                k.mm(ssp[:, h * 128:(h + 1) * 128], ones_bf.v, sq[:, h, :])
            r1 = f4['r1'].next()
            k.act(f2(r1), ssp.v, AF.Ln, scale=1.0 / 128, bias=EPS)
            r2 = f4['r2'].next()
            k.act(f2(r2), f2(r1), AF.Exp, scale=-0.5)
            o2 = f4['o2'].next()
            k.tt(f2(o2), f2(osb), f2(r2), ALU.mult)
            k.stt(ost[:, :, blk], o2.v, gn[:, 0:1], gs4[:, :, blk], ALU.mult, ALU.mult)
            if tb == 3:
                k.dma(fm_view(OMIX, ti, 0, 4), ost.v, q='pool')

        blocks = [(ti, tb) for ti in range(NTT) for tb in range(4)]
        outs = [dict() for _ in blocks]
        g0 = prep(blocks[0][0], blocks[0][1], outs[0])
        for _ in g0:
            pass
        for bi, (ti, tb) in enumerate(blocks):
            nxt = prep(blocks[bi + 1][0], blocks[bi + 1][1], outs[bi + 1]) if bi + 1 < len(blocks) else None

            def pump(n, nxt=nxt):
                if nxt is None:
                    return
                for _ in range(n):
                    try:
                        next(nxt)
                    except StopIteration:
                        return
            chain(ti, tb, outs[bi], pump)
            if nxt is not None:
                for _ in nxt:
                    pass
            outs[bi].clear()
        k.phase_end()

    def phase_sb(l, i):
        k.phase_begin()
        Wk = SB_W
        zp = psum_pool(2, (128, Wk), F32, 'z')
        atp = psum_pool(2, (128, Wk), BF16, 'aT')
        op_ = psum_pool(2, (128, 512), F32, 'oT')
        qp = sb_pool(2, [64, T], BF16, 'qh')
        kp = sb_pool(2, [64, T], BF16, 'kh')
        vp = sb_pool(2, [128, NT, 64], BF16, 'vh')
        ohp = sb_pool(2, [64, T], BF16, 'oh')
        ep = sb_pool(4, [128, Wk], F32, 'e')
        spp = sb_pool(3, [128, Wk], F32, 'sp')
        gp = sb_pool(3, [128, Wk + 1], F32, 'G')
        for b_ in gp.bufs:
            k.memset(b_[:, 0:1], 0.0)
        pp_ = sb_pool(2, [128, Wk], BF16, 'p')
        ap_ = sb_pool(4, [128, Wk], BF16, 'a')
        atsp = sb_pool(3, [128, Wk], BF16, 'aTs')
        bp = sb_pool(10, [128, 1], F32, 'bias')
        mkc = k.sb([64, 3, 128], BF16, 'mkc')
        k.copy(mkc[:, 0, :], C('identS')[0:64, :], eng='dve')
        k.copy(mkc[:, 1, :], C('mk1')[0:64, :], eng='dve')
        k.copy(mkc[:, 2, :], C('mk2')[0:64, :], eng='dve')
        tiles = []
        for h in range(8):
            for qb in range(NT):
                t0 = qb * 128
                nkt = (t0 + 128 + Wk - 1) // Wk
                for kt in reversed(range(nkt)):
                    tiles.append(dict(h=h, qb=qb, t0=t0, k0=kt * Wk, w=min(Wk, t0 + 128 - kt * Wk),
                                      diag=(kt == nkt - 1), lastq=(kt == 0), idx=len(tiles)))
        heads = {}

        def load_head(h):
            if h in heads or h >= 8:
                return
            qh = qp.next()
            k.dma(qh.v, SQ.whole(SQ.full[h * 64:(h + 1) * 64, :]), reads=SQ.tiles)
            kh = kp.next()
            k.dma(kh.v, SK.whole(SK.full[h * 64:(h + 1) * 64, :]), reads=SK.tiles)
            vh = vp.next()
            vstep = min(8, NT)
            for n0 in range(0, NT, vstep):
                k.dma(vh[:, n0:n0 + vstep, :],
                      SV.whole(SV.full[n0 * 128:(n0 + vstep) * 128, h * 64:(h + 1) * 64].rearrange("(n p) d -> p n d", p=128)),
                      reads=SV.tiles)
            heads[h] = dict(qh=qh, kh=kh, vh=vh, oh=ohp.next())

        def S1(t):
            h = t['h']
            if h not in heads:
                load_head(h)
            hd = heads[h]
            w, k0, t0 = t['w'], t['k0'], t['t0']
            z = zp.next()
            wm = w - 128 if t['diag'] else w
            for c0 in range(0, wm, 512):
                cw = min(512, wm - c0)
                k.mm(z[:, c0:c0 + cw], hd['qh'][:, t0:t0 + 128], hd['kh'][:, k0 + c0:k0 + c0 + cw])
            if t['diag']:
                k.mm(z[:, wm:w], hd['qh'][:, t0:t0 + 128], hd['kh'][:, k0 + wm:k0 + w], start=True, stop=False)
                k.mm(z[:, wm:w], ident_bf[0:64, :], mkc[:, 1, :], start=False, stop=False)
                k.mm(z[:, wm:w], mkc[:, 0, :], mkc[:, 2, :], start=False, stop=True)
            e = ep.next()
            k.act(e[:, 0:w], z[:, 0:w], AF.Exp)
            sp = spp.next()
            k.act(sp[:, 0:w], e[:, 0:w], AF.Ln, bias=1.0)
            t['e'] = e
            t['sp'] = sp

        def S2(t):
            w = t['w']
            G = gp.next()
            k.scan(G[:, 1:w + 1], ones_f[:, 0:w], t['sp'][:, 0:w], 0.0, ALU.mult, ALU.add)
            t['G'] = G

        def S3(t):
            w = t['w']
            G = t['G']
            bias = bp.next()
            if t['diag']:
                k.act(bias.v, G[:, w:w + 1], AF.Identity, scale=-1.0)
            else:
                k.act(bias.v, G[:, w:w + 1], AF.Identity, scale=-1.0, bias=tiles[t['idx'] - 1]['bias'][:, 0:1])
            t['bias'] = bias
            p = pp_.next()
            k.act(p[:, 0:w], G[:, 0:w], AF.Exp, bias=bias[:, 0:1])
            a = ap_.next()
            k.tt(a[:, 0:w], t['e'][:, 0:w], p[:, 0:w], ALU.mult, eng='pool')
            t['a'] = a

        def S4(t):
            w = t['w']
            aT = atp.next()
            for sb in range(w // 128):
                k.transpose(aT[:, sb * 128:(sb + 1) * 128], t['a'][:, sb * 128:(sb + 1) * 128], ident_bf.v)
            aTs = atsp.next()
            k.copy(aTs[:, 0:w], aT[:, 0:w], eng='dve')
            t['aTs'] = aTs

        def S5(t):
            w, k0, t0 = t['w'], t['k0'], t['t0']
            hd = heads[t['h']]
            if t['diag'] and t['qb'] == 0:
                load_head(t['h'] + 1)
            if t['diag']:
                t['oT'] = op_.next()
            else:
                t['oT'] = tiles[t['idx'] - 1]['oT']
            oT = t['oT']
            nsb = w // 128
            for sb in range(nsb):
                k.mm(oT[0:64, 0:128], hd['vh'][:, k0 // 128 + sb, :], t['aTs'][:, sb * 128:(sb + 1) * 128],
                     start=(t['diag'] and sb == 0), stop=(t['lastq'] and sb == nsb - 1))
            if t['lastq']:
                k.copy(hd['oh'][:, t0:t0 + 128], oT[0:64, 0:128], eng='act')
                if t['qb'] == NT - 1:
                    h = t['h']
                    k.dma(OMIX.whole(OMIX.full[512 + h * 64:512 + (h + 1) * 64, :]), hd['oh'].v, q='sp', writes=OMIX.tiles)
            for key in ('e', 'sp', 'G', 'a', 'aTs'):
                t.pop(key, None)

        n = len(tiles)
        for s_ in range(n + 6):
            if 0 <= s_ - 4 < n:
                S4(tiles[s_ - 4])
            if 0 <= s_ - 2 < n:
                S3(tiles[s_ - 2])
            if 0 <= s_ - 1 < n:
                S2(tiles[s_ - 1])
            if s_ < n:
                S1(tiles[s_])
            if 0 <= s_ - 5 < n:
                S5(tiles[s_ - 5])
        k.phase_end()

    def phase_p1_ret(l, i, xsrc):
        k.phase_begin()
        pp = psum_pool(8)
        W = k.sb([128, 8, RET_IN], BF16, 'Wret')
        for kc in range(8):
            k.dma(W[:, kc, :], RWIb[i][kc * 128:(kc + 1) * 128, :])
        xp = sb_pool(1, [128, 8, 512], F32, 'xt')
        hp = sb_pool(1, [128, 8, 512], BF16, 'hT')
        csp = sb_pool(2, [128, 2, 512], F32, 'cs')
        tp = sb_pool(4, [128, 512], F32, 'rt')
        qst = sb_pool(2, [128, 8, 512], BF16, 'qst')
        kst = qst
        gst = sb_pool(1, [128, 8, 512], BF16, 'gst')
        vst = sb_pool(1, [128, 2, 2048], BF16, 'vst')
        norm_tile = make_norm(pp)
        for ti in range(NTT):
            xt = xp.next()
            k.dma(xt.v, fm_view(xsrc, ti, 0, 8))
            cs = csp.next()
            k.dma(cs.v, fm_view(ROPE, ti, 0, 2))
            hT = hp.next()
            norm_tile(xt, A_m[l], modc[l][:, 0:8], hT)
            cos = cs[:, 0, :]
            sin = cs[:, 1, :]
            for which in range(2):
                st = (qst if which == 0 else kst).next()
                for h in range(4):
                    c0 = which * 1024 + h * 256
                    p1 = pp.next()
                    p2 = pp.next()
                    for kc in range(8):
                        k.mm(p1.v, W[:, kc, c0:c0 + 128], hT[:, kc, :], start=(kc == 0), stop=(kc == 7))
                    for kc in range(8):
                        k.mm(p2.v, W[:, kc, c0 + 128:c0 + 256], hT[:, kc, :], start=(kc == 0), stop=(kc == 7))
                    sc = 1.0 if which == 0 else 1.0 / 16.0
                    t1 = tp.next()
                    t2 = tp.next()
                    t3 = tp.next()
                    t4 = tp.next()
                    k.stt(t1.v, p1.v, sc, cos, ALU.mult, ALU.mult)
                    k.stt(t2.v, p2.v, sc, sin, ALU.mult, ALU.mult)
                    k.stt(t3.v, p1.v, sc, sin, ALU.mult, ALU.mult)
                    k.stt(t4.v, p2.v, sc, cos, ALU.mult, ALU.mult)
                    k.tt(st[:, 2 * h, :], t1.v, t2.v, ALU.subtract, eng='pool')
                    k.tt(st[:, 2 * h + 1, :], t3.v, t4.v, ALU.add, eng='pool')
                k.dma(fm_view(RQ if which == 0 else RK, ti, 0, 8), st.v, q='pool')
            for half in range(2):
                g_ = gst.next()
                for cl in range(8):
                    c = half * 8 + cl
                    ps = pp.next()
                    c0 = 4096 + c * 128
                    for kc in range(8):
                        k.mm(ps.v, W[:, kc, c0:c0 + 128], hT[:, kc, :], start=(kc == 0), stop=(kc == 7))
                    k.act(g_[:, cl, :], ps.v, AF.Silu)
                k.dma(fm_view(RG, ti, half * 1024, 8), g_.v, q='pool')
            for half in range(2):
                v_ = vst.next()
                for tl in range(2):
                    tb = half * 2 + tl
                    for nb in range(4):
                        ps = pp.next()
                        c0 = 2048 + nb * 512
                        for kc in range(8):
                            k.mm(ps.v, hT[:, kc, tb * 128:(tb + 1) * 128], W[:, kc, c0:c0 + 512], start=(kc == 0), stop=(kc == 7))
                        k.copy(v_[:, tl, nb * 512:(nb + 1) * 512], ps.v, eng='dve' if nb % 2 else 'act')
                bt = RV.tile(ti)
                k.dma(View(bt, bt.ap[half * 256:(half + 1) * 256, :].rearrange("(t p) c -> p t c", p=128)), v_.v, q='pool')
        k.phase_end()

    def phase_ret(l, i):
        k.phase_begin()
        pt4p = psum_pool(1, (128, 512), F32, 'pt4')
        trpp = psum_pool(1, (128, 512), F32, 'trp')
        otp = psum_pool(4, (128, 512), F32, 'oTp')
        pp2 = psum_pool(1, (128, 1024), F32, 'dS')
        Sst = [k.sb([128, 1024], F32, f'S{h}') for h in range(4)]
        Sb = [k.sb([128, 2, 512], BF16, f'Sb{h}') for h in range(4)]
        for h in range(4):
            k.memset(Sst[h].v, 0.0)
            k.memset(Sb[h].v, 0.0, eng='dve')
        qp = sb_pool(2, [128, 8, 512], BF16, 'q8')
        kp = sb_pool(2, [128, 8, 512], BF16, 'k8')
        vp = sb_pool(2, [128, 4, 2048], BF16, 'v4')
        gp = sb_pool(1, [128, 16, 512], BF16, 'g16')
        osp = sb_pool(2, [128, 16, 512], BF16, 'o16')
        P4p = sb_pool(2, [128, 4, 128], BF16, 'P4')
        kd4p = sb_pool(2, [128, 4, 256], BF16, 'kd4')
        qd4p = sb_pool(2, [128, 8, 128], BF16, 'qd4')
        sqp = sb_pool(2, [128, 4, 512], BF16, 'rsq')
        rp = sb_pool(4, [128, 512], F32, 'rr')
        o2p = sb_pool(4, [128, 4, 128], F32, 'o2')
        kds = C('kds')
        for ti in range(NTT):
            q8 = qp.next()
            k.dma(q8.v, fm_view(RQ, ti, 0, 8))
            k8 = kp.next()
            k.dma(k8.v, fm_view(RK, ti, 0, 8))
            v4 = vp.next()
            k.dma(v4.v, tm_view(RV, ti, 0, 2048))
            g16 = gp.next()
            k.dma(g16.v, fm_view(RG, ti, 0, 16))
            ost = osp.next()
            for tb in range(4):
                blk = slice(tb * 128, (tb + 1) * 128)
                PT4 = pt4p.next()
                trp = trpp.next()
                tv = trp.v.bitcast(BF16)
                for h in range(4):
                    hs = slice(h * 128, (h + 1) * 128)
                    k.mm(PT4[:, hs], k8[:, 2 * h, blk], q8[:, 2 * h, blk], start=True, stop=False)
                    k.mm(PT4[:, hs], k8[:, 2 * h + 1, blk], q8[:, 2 * h + 1, blk], start=False, stop=True)
                for h in range(4):
                    for d in range(2):
                        k.transpose(tv[:, h * 256 + d * 128:h * 256 + (d + 1) * 128], k8[:, 2 * h + d, blk], ident_bf.v)
                P4 = P4p.next()
                kd4 = kd4p.next()
                qd4 = qd4p.next()
                for h in range(4):
                    k.tt(P4[:, h, :], PT4[:, h * 128:(h + 1) * 128], C(f'DT{h}'), ALU.mult)
                for h in range(4):
                    k.ts(kd4[:, h, :], tv[:, h * 256:(h + 1) * 256], kds[:, h:h + 1], ALU.mult)
                for h in range(4):
                    for d in range(2):
                        k.tt(qd4[:, 2 * h + d, :], q8[:, 2 * h + d, blk], C(f'QD{h}'), ALU.mult, eng='pool')
                oTps = []
                for h in range(4):
                    oTp = otp.next()
                    for dvc in range(4):
                        o_ = oTp[:, dvc * 128:(dvc + 1) * 128]
                        k.mm(o_, v4[:, tb, h * 512 + dvc * 128:h * 512 + (dvc + 1) * 128], P4[:, h, :], start=True, stop=False)
                        k.mm(o_, Sb[h][:, 0, dvc * 128:(dvc + 1) * 128], qd4[:, 2 * h, :], start=False, stop=False)
                        k.mm(o_, Sb[h][:, 1, dvc * 128:(dvc + 1) * 128], qd4[:, 2 * h + 1, :], start=False, stop=True)
                    oTps.append(oTp)
                for h in range(4):
                    dSp = pp2.next()
                    for d in range(2):
                        k.mm(dSp[:, d * 512:(d + 1) * 512], kd4[:, h, d * 128:(d + 1) * 128], v4[:, tb, h * 512:(h + 1) * 512])
                    k.stt(Sst[h].v, Sst[h].v, float(np.exp(RET_LG[h] * 128.0)), dSp.v, ALU.mult, ALU.add)
                    k.copy(Sb[h].v.re("p a b -> p (a b)"), Sst[h].v, eng='act')
                sq = sqp.next()
                for h in range(4):
                    k.act(sq[:, h, :], oTps[h].v, AF.Square)
                ssp = pt4p.next()
                for h in range(4):
                    for dvc in range(4):
                        k.mm(ssp[:, h * 128:(h + 1) * 128], ones_bf.v, sq[:, h, dvc * 128:(dvc + 1) * 128], start=(dvc == 0), stop=(dvc == 3))
                r1 = rp.next()
                k.act(r1.v, ssp.v, AF.Ln, scale=1.0 / 512, bias=EPS)
                r2 = rp.next()
                k.act(r2.v, r1.v, AF.Exp, scale=-0.5)
                for h in range(4):
                    o2 = o2p.next()
                    r2b = View(r2, r2.ap[:, h * 128:(h + 1) * 128].unsqueeze(1).broadcast_to([128, 4, 128]))
                    k.tt(o2.v, oTps[h].v.re("p (c t) -> p c t", c=4), r2b, ALU.mult)
                    k.tt(ost[:, 4 * h:4 * h + 4, blk], o2.v, g16[:, 4 * h:4 * h + 4, blk], ALU.mult, eng='pool')
            k.dma(fm_view(RO, ti, 0, 16), ost.v, q='pool')
        k.phase_end()

    def phase_out(l, i, hyb, xsrc):
        k.phase_begin()
        pp = psum_pool(4)
        if hyb:
            wA = k.sb([128, 8, D], BF16, 'wA')
            for c in range(0, 8, 4):
                k.dma(wA[:, c:c + 4, :], View(HWOb[i], HWOb[i].ap[c * 128:(c + 4) * 128, :].rearrange("(c p) n -> p c n", p=128)))
            oap = sb_pool(2, [128, 8, 512], BF16, 'oa')
        else:
            wR = k.sb([128, 16, D], BF16, 'wR')
            for c in range(0, 16, 4):
                k.dma(wR[:, c:c + 4, :], View(RWOb[i], RWOb[i].ap[c * 128:(c + 4) * 128, :].rearrange("(c p) n -> p c n", p=128)))
            oap = sb_pool(2, [128, 16, 512], BF16, 'oa')
        xp = sb_pool(2, [128, 8, 512], F32, 'xt')
        gt = modc[l][:, 16:24]
        for ti in range(NTT):
            xt = xp.next()
            k.dma(xt.v, fm_view(xsrc, ti, 0, 8))
            oa = oap.next()
            if hyb:
                k.dma(oa.v, fm_view(OMIX, ti, 0, 8))
            else:
                k.dma(oa.v, fm_view(RO, ti, 0, 16))
            for dc in range(8):
                ps = pp.next()
                ds_ = slice(dc * 128, (dc + 1) * 128)
                if hyb:
                    for c in range(8):
                        k.mm(ps.v, wA[:, c, ds_], oa[:, c, :], start=(c == 0), stop=(c == 7))
                else:
                    for c in range(16):
                        k.mm(ps.v, wR[:, c, ds_], oa[:, c, :], start=(c == 0), stop=(c == 15))
                k.stt(xt[:, dc, :], ps.v, gt[:, dc:dc + 1], xt[:, dc, :], ALU.mult, ALU.add)
            k.dma(fm_view(XT, ti, 0, 8), xt.v, q='pool')
        k.phase_end()

    def phase_ffn(l, xdst):
        k.phase_begin()
        pp = psum_pool(8)
        W1 = k.sb([128, 8, 2 * FFN_H], BF16, 'W1')
        for kc in range(8):
            k.dma(W1[:, kc, :], FWIb[l][kc * 128:(kc + 1) * 128, :])
        w2p = sb_pool(2, [128, 22, 128], BF16, 'w2')
        xp = sb_pool(2, [128, 8, 512], F32, 'xt')
        hp = sb_pool(2, [128, 8, 512], BF16, 'hT')
        ap_ = sb_pool(1, [128, 22, 512], BF16, 'actT')
        sp_ = sb_pool(3, [128, 512], F32, 'sl')
        norm_tile = make_norm(pp)
        gt = modc[l][:, 40:48]
        for ti in range(NTT):
            xt = xp.next()
            k.dma(xt.v, fm_view(XT, ti, 0, 8))
            hT = hp.next()
            norm_tile(xt, A_f[l], modc[l][:, 24:32], hT)
            aT = ap_.next()
            for j in range(22):
                pg = pp.next()
                pu = pp.next()
                for kc in range(8):
                    k.mm(pg.v, W1[:, kc, j * 128:(j + 1) * 128], hT[:, kc, :], start=(kc == 0), stop=(kc == 7))
                for kc in range(8):
                    k.mm(pu.v, W1[:, kc, FFN_H + j * 128:FFN_H + (j + 1) * 128], hT[:, kc, :], start=(kc == 0), stop=(kc == 7))
                s = sp_.next()
                k.act(s.v, pg.v, AF.Silu)
                k.tt(aT[:, j, :], s.v, pu.v, ALU.mult)
            for dc in range(8):
                w2 = w2p.next()
                k.dma(w2.v, View(FWOb[l], FWOb[l].ap[dc]))
                ps = pp.next()
                for j in range(22):
                    k.mm(ps.v, w2[:, j, :], aT[:, j, :], start=(j == 0), stop=(j == 21))
                k.stt(xt[:, dc, :], ps.v, gt[:, dc:dc + 1], xt[:, dc, :], ALU.mult, ALU.add)
            k.dma(fm_view(xdst, ti, 0, 8), xt.v, q='pool')
        k.phase_end()

    if want('cast'):
        phase_cast()
    if want('ada'):
        phase_ada()
    for l in range(n_layers):
        i = l // 2
        xsrc = XT_in if l == 0 else XT
        xdst = OUT if l == n_layers - 1 else XT
        if l % 2 == 0:
            if want(f'p1_{l}'):
                phase_p1_hyb(l, i, xsrc)
            if want(f'gdn_{l}'):
                phase_gdn(l, i)
            if want(f'sb_{l}'):
                phase_sb(l, i)
            if want(f'out_{l}'):
                phase_out(l, i, True, xsrc)
        else:
            if want(f'p1_{l}'):
                phase_p1_ret(l, i, xsrc)
            if want(f'ret_{l}'):
                phase_ret(l, i)
            if want(f'out_{l}'):
                phase_out(l, i, False, xsrc)
        if want(f'ffn_{l}'):
            phase_ffn(l, xdst)
    k.barrier()
    k.finalize()
    return nc, k


def host_inputs(b, T, x, c, ada_w, ada_b, norm_mix, norm_ffn, hyb_w_in, hyb_conv, gdn_a_log, gdn_dt_bias,
                gdn_norm, sb_q_norm, sb_k_norm, hyb_w_out, ret_w_in, ret_w_out, ffn_w_in, ffn_w_out, shared):
    f = np.float32
    m = dict(shared)
    m["xT"] = np.ascontiguousarray(x[b, :T].T)
    m["c_col"] = np.ascontiguousarray(c[b].reshape(8, 128).T)
    return m


def host_shared(T, ada_w, ada_b, norm_mix, norm_ffn, hyb_w_in, hyb_conv, gdn_a_log, gdn_dt_bias,
                gdn_norm, sb_q_norm, sb_k_norm, hyb_w_out, ret_w_in, ret_w_out, ffn_w_in, ffn_w_out):
    ca = np.ascontiguousarray
    m = {}
    m["ada_w"] = ca(ada_w)
    m["ada_b_col"] = ca(ada_b.reshape(4, 48, 128).transpose(0, 2, 1))
    m["norm_mix_col"] = ca(norm_mix.reshape(4, 8, 128).transpose(0, 2, 1))
    m["norm_ffn_col"] = ca(norm_ffn.reshape(4, 8, 128).transpose(0, 2, 1))
    m["hyb_w_in"] = ca(hyb_w_in)
    m["conv_col"] = ca(hyb_conv.reshape(2, 4, 12, 128).transpose(0, 3, 2, 1))
    m["a_log_b"] = ca(np.broadcast_to(gdn_a_log[:, None, :], (2, 128, 4)))
    m["dt_bias_b"] = ca(np.broadcast_to(gdn_dt_bias[:, None, :], (2, 128, 4)))
    m["gdn_norm_col"] = ca(gdn_norm.reshape(2, 128, 1))
    m["sb_q_norm_col"] = ca(np.concatenate([sb_q_norm, sb_q_norm], axis=1).reshape(2, 128, 1))
    m["sb_k_norm_col"] = ca(np.concatenate([sb_k_norm, sb_k_norm], axis=1).reshape(2, 128, 1))
    m["hyb_w_out"] = ca(hyb_w_out)
    m["ret_w_in"] = ca(ret_w_in)
    m["ret_w_out"] = ca(ret_w_out)
    m["ffn_w_in"] = ca(ffn_w_in)
    m["ffn_w_out"] = ca(ffn_w_out)
    m["cst"] = make_consts()
    m["rope"] = ca(make_rope(T).reshape(256, T))
    return m


_CACHE = {}


def kernel(x, c, ada_w, ada_b, norm_mix, norm_ffn, hyb_w_in, hyb_conv, gdn_a_log, gdn_dt_bias,
           gdn_norm, sb_q_norm, sb_k_norm, hyb_w_out, ret_w_in, ret_w_out, ffn_w_in, ffn_w_out):
    args = [np.asarray(a, dtype=np.float32) for a in
            (x, c, ada_w, ada_b, norm_mix, norm_ffn, hyb_w_in, hyb_conv, gdn_a_log, gdn_dt_bias,
             gdn_norm, sb_q_norm, sb_k_norm, hyb_w_out, ret_w_in, ret_w_out, ffn_w_in, ffn_w_out)]
    x, c = args[0], args[1]
    B, T, _ = x.shape
    shared = host_shared(T, *args[2:])
    in_maps = [host_inputs(b, T, *args, shared) for b in range(B)]
    if T not in _CACHE:
        _CACHE[T] = build(T)[0]
    nc = _CACHE[T]
    res = run_bass_kernel_spmd(nc, in_maps, core_ids=list(range(B)))
    out = np.stack([np.asarray(r["outT"]).T for r in res.results], axis=0)
    return np.ascontiguousarray(out.astype(np.float32))
```

```python
import numpy as np
import concourse.bass as bass
import concourse.mybir as mybir

F32 = mybir.dt.float32
BF16 = mybir.dt.bfloat16
AF = mybir.ActivationFunctionType
ALU = mybir.AluOpType
AX = mybir.AxisListType

EPOCH = 30000
COMPUTE = ('pe', 'act', 'dve', 'pool')
ALLQ = ('pe', 'act', 'dve', 'pool', 'sp')


class Buf:
    __slots__ = ('ap', 'name', 'last_w', 'readers', 'sem', 'cnt', 'space')

    def __init__(self, ap, name='', space='sb'):
        self.ap = ap
        self.name = name
        self.last_w = None
        self.readers = []
        self.sem = None
        self.cnt = 0
        self.space = space

    def __getitem__(self, idx):
        return View(self, self.ap[idx])

    @property
    def v(self):
        return View(self, self.ap)


class View:
    __slots__ = ('buf', 'ap')

    def __init__(self, buf, ap):
        self.buf = buf
        self.ap = ap

    def __getitem__(self, idx):
        return View(self.buf, self.ap[idx])

    def re(self, pat, **kw):
        return View(self.buf, self.ap.rearrange(pat, **kw))

    def bitcast(self, dt):
        return View(self.buf, self.ap.bitcast(dt))


class Op:
    __slots__ = ('eng', 'fn', 'deps', 'signal', 'n', 'is_dma', 'slot', 'val', 'ndma', 'tag', 'sem')

    def __init__(self, eng, fn, is_dma=False, slot=None, ndma=0, tag=''):
        self.eng = eng
        self.fn = fn
        self.deps = ()
        self.signal = False
        self.n = -1
        self.is_dma = is_dma
        self.slot = slot
        self.val = 0
        self.ndma = ndma
        self.tag = tag


def _bufs(xs):
    out = []
    for x in xs:
        if x is None:
            continue
        if isinstance(x, View):
            out.append(x.buf)
        elif isinstance(x, Buf):
            out.append(x)
        elif isinstance(x, (list, tuple)):
            out.extend(_bufs(x))
    return out


class Kern:
    def __init__(self, nc):
        self.nc = nc
        self.ops = {q: [] for q in ALLQ}
        self.all_ops = []
        self.sb_off = 0
        self.sb_names = 0
        self.last_dma_by_slot = {}
        import contextlib
        self.stacks = [contextlib.ExitStack()]
        self.free_sems = {}
        self.marks = []
        self.phase_slots = [[]]

    def sb(self, shape, dtype, name=None):
        self.sb_names += 1
        nm = f"sb{self.sb_names}_{name or ''}"
        cm = self.nc.sbuf_tensor(nm, list(shape), dtype)
        t = self.stacks[-1].enter_context(cm)
        return Buf(t.ap(), nm, 'sb')

    def phase_begin(self):
        import contextlib
        self.stacks.append(contextlib.ExitStack())
        self.phase_slots.append([])

    def phase_end(self):
        self.marks.append({q: sum(1 for o in self.ops[q] if o.fn is not None and not o.is_dma) for q in COMPUTE})
        self.barrier()
        self.stacks.pop().close()
        for slot, q in self.phase_slots.pop():
            self.free_sems.setdefault(q, []).append((slot.sem[q], slot.cnt[q]))

    def add(self, eng, fn, reads=(), writes=(), is_dma=False, slot=None, ndma=0, tag=''):
        op = Op(eng, fn, is_dma, slot, ndma, tag)
        R = _bufs(reads)
        W = _bufs(writes)
        deps = set()
        for b in R:
            if b.last_w is not None:
                deps.add(b.last_w)
            if b.space == 'ps':
                for r in b.readers:
                    if r.eng != eng:
                        deps.add(r)
        for b in W:
            if b.last_w is not None:
                deps.add(b.last_w)
            deps.update(b.readers)
        deps.discard(op)
        op.deps = tuple(deps)
        for b in W:
            b.last_w = op
            b.readers = []
        for b in R:
            if b in W:
                continue
            if not is_dma:
                b.readers = [r for r in b.readers if r.is_dma or r.eng != eng]
            b.readers.append(op)
        if is_dma:
            slot.cnt[eng] += ndma
            op.val = 16 * slot.cnt[eng]
            op.sem = slot.sem[eng]
            assert op.val < 60000, f"dma sem overflow on {slot.name}"
            self.last_dma_by_slot[(id(slot), eng)] = op
        self.ops[eng].append(op)
        self.all_ops.append(op)
        return op

    def barrier(self):
        lasts = []
        for q in COMPUTE:
            for o in reversed(self.ops[q]):
                if not o.is_dma and o.fn is not None:
                    lasts.append(o)
                    break
        lasts.extend(self.last_dma_by_slot.values())
        self.last_dma_by_slot = {}
        for q in ALLQ:
            op = Op(q, None)
            op.deps = tuple(lasts)
            self.ops[q].append(op)
            self.all_ops.append(op)

    def mm(self, out, lhsT, rhs, start=True, stop=True, **kw):
        o, l, r = out.ap, lhsT.ap, rhs.ap
        return self.add('pe', lambda e: e.matmul(o, l, r, start=start, stop=stop, **kw),
                        reads=[lhsT, rhs], writes=[out])

    def transpose(self, out, in_, ident):
        o, i, d = out.ap, in_.ap, ident.ap
        return self.add('pe', lambda e: e.transpose(o, i, d), reads=[in_, ident], writes=[out])

    def act(self, out, in_, func, bias=None, scale=None, accum_out=None, eng='act'):
        kw = {}
        reads = [in_]
        if bias is not None:
            if isinstance(bias, View):
                kw['bias'] = bias.ap
                reads.append(bias)
            else:
                kw['bias'] = bias
        if scale is not None:
            if isinstance(scale, View):
                kw['scale'] = scale.ap
                reads.append(scale)
            else:
                kw['scale'] = scale
        writes = [out]
        if accum_out is not None:
            kw['accum_out'] = accum_out.ap
            writes.append(accum_out)
        o, i = out.ap, in_.ap
        return self.add('act', lambda e: e.activation(o, i, func, **kw), reads=reads, writes=writes)

    def tt(self, out, in0, in1, op, eng='dve'):
        o, a, b = out.ap, in0.ap, in1.ap
        return self.add(eng, lambda e: e.tensor_tensor(o, a, b, op), reads=[in0, in1], writes=[out])

    def ts(self, out, in0, s1, op0, s2=None, op1=None, eng='dve', accum_out=None):
        reads = [in0]
        a1 = s1
        if isinstance(s1, View):
            a1 = s1.ap
            reads.append(s1)
        a2 = s2
        if isinstance(s2, View):
            a2 = s2.ap
            reads.append(s2)
        o, i = out.ap, in0.ap
        kw = {}
        writes = [out]
        if accum_out is not None:
            kw['accum_out'] = accum_out.ap
            writes.append(accum_out)
        if op1 is None:
            return self.add(eng, lambda e: e.tensor_scalar(o, i, a1, None, op0, **kw), reads=reads, writes=writes)
        return self.add(eng, lambda e: e.tensor_scalar(o, i, a1, a2, op0, op1, **kw), reads=reads, writes=writes)

    def stt(self, out, in0, scalar, in1, op0, op1, eng='dve'):
        reads = [in0, in1]
        sc = scalar
        if isinstance(scalar, View):
            sc = scalar.ap
            reads.append(scalar)
        o, a, b = out.ap, in0.ap, in1.ap
        return self.add(eng, lambda e: e.scalar_tensor_tensor(o, a, sc, b, op0, op1), reads=reads, writes=[out])

    def scan(self, out, d0, d1, initial, op0, op1):
        reads = [d0, d1]
        ini = initial
        if isinstance(initial, View):
            ini = initial.ap
            reads.append(initial)
        o, a, b = out.ap, d0.ap, d1.ap
        return self.add('dve', lambda e: e.tensor_tensor_scan(o, a, b, ini, op0, op1), reads=reads, writes=[out])

    def copy(self, out, in_, eng='dve'):
        o, i = out.ap, in_.ap
        if eng == 'act':
            return self.add('act', lambda e: e.copy(o, i), reads=[in_], writes=[out])
        return self.add(eng, lambda e: e.tensor_copy(o, i), reads=[in_], writes=[out])

    def memset(self, out, val, eng='pool'):
        o = out.ap
        return self.add(eng, lambda e: e.memset(o, val), reads=[], writes=[out])

    def dma(self, out, in_, q='sp', slot=None, reads=None, writes=None, **kw):
        outs = out if isinstance(out, (list, tuple)) else [out]
        ins = in_ if isinstance(in_, (list, tuple)) else [in_]
        pairs = [(o.ap, i.ap) for o, i in zip(outs, ins)]
        if slot is None:
            slot = outs[0].buf if outs[0].buf.space == 'sb' else ins[0].buf
        if slot.sem is None:
            slot.sem = {}
            slot.cnt = {}
        if q not in slot.sem:
            if self.free_sems.get(q):
                slot.sem[q], slot.cnt[q] = self.free_sems[q].pop()
            else:
                slot.sem[q] = self.new_sem()
                slot.cnt[q] = 0
            self.phase_slots[-1].append((slot, q))
        sem_ = slot.sem[q]

        def fn(e, pairs=pairs, sem=sem_, kw=kw):
            last = None
            for (o, i) in pairs:
                last = e.dma_start(out=o, in_=i, **kw).then_inc(sem, 16)
            return None
        op = self.add(q, fn, reads=ins if reads is None else reads,
                      writes=outs if writes is None else writes,
                      is_dma=True, slot=slot, ndma=len(pairs))
        return op

    def new_sem(self):
        self._nsem = getattr(self, '_nsem', 0) + 1
        cm = self.nc.semaphore(f"s{self._nsem}")
        sem = cm.__enter__()
        self._sem_cms = getattr(self, '_sem_cms', [])
        self._sem_cms.append(cm)
        return sem

    def finalize(self):
        nc = self.nc
        for op in self.all_ops:
            for d in op.deps:
                if d.is_dma:
                    continue
                if d.eng == 'pe' and op.eng == 'pe' and not op.is_dma:
                    continue
                d.signal = True
        eng_sems = {}
        for q in ALLQ:
            n = 0
            for op in self.ops[q]:
                if op.signal and not op.is_dma:
                    op.n = n
                    n += 1
            eng_sems[q] = [self.new_sem() for _ in range((n + EPOCH - 1) // EPOCH)]
        self.n_instr = {q: len(self.ops[q]) for q in ALLQ}

        def emit(q, e):
            waited_eng = {}
            waited_dma = {}
            for op in self.ops[q]:
                need_eng = {}
                need_dma = {}
                for d in op.deps:
                    if d.is_dma:
                        k = d.sem
                        if waited_dma.get(id(k), 0) < d.val:
                            if need_dma.get(id(k), (None, 0))[1] < d.val:
                                need_dma[id(k)] = (k, d.val)
                    else:
                        if d.eng == 'pe' and q == 'pe' and not op.is_dma:
                            continue
                        if waited_eng.get(d.eng, -1) < d.n:
                            if need_eng.get(d.eng, -1) < d.n:
                                need_eng[d.eng] = d.n
                for pe_, n in need_eng.items():
                    e.wait_ge(eng_sems[pe_][n // EPOCH], n % EPOCH + 1)
                    waited_eng[pe_] = n
                for _, (k, v) in need_dma.items():
                    e.wait_ge(k, v)
                    waited_dma[id(k)] = v
                if op.fn is None:
                    continue
                ins = op.fn(e)
                if op.signal and not op.is_dma:
                    ins.then_inc(eng_sems[q][op.n // EPOCH], 1)

        with nc.Block() as block:
            @block.tensor
            def _(e):
                emit('pe', e)

            @block.scalar
            def _(e):
                emit('act', e)

            @block.vector
            def _(e):
                emit('dve', e)

            @block.gpsimd
            def _(e):
                emit('pool', e)

            @block.sync
            def _(e):
                emit('sp', e)

from concourse.bass_utils import run_bass_kernel_spmd

D = 1024
EPS = 1e-6
FFN_H = 2816
HYB_IN = 3592
RET_IN = 6144
BIG = 30000.0
RET_LG = [float(np.log1p(-np.exp2(-5.0 - h))) for h in range(4)]

CST = {}
_off = 0
for _n, _w in [('ident', 128), ('tri2', 128), ('triu2', 128), ('cind0', 128), ('cind1', 128),
               ('mb1', 128), ('mb2', 128), ('mlow', 128), ('ones64', 128),
               ('DT0', 128), ('DT1', 128), ('DT2', 128), ('DT3', 128),
               ('QD0', 128), ('QD1', 128), ('QD2', 128), ('QD3', 128), ('kds', 4),
               ('identS', 128), ('mk1', 128), ('mk2', 128)]:
    CST[_n] = (_off, _w)
    _off += _w
NCST = _off


def make_consts():
    c = np.zeros((128, NCST), np.float64)
    i = np.arange(128)
    same = (i[:, None] // 64) == (i[None, :] // 64)

    def put(n, a):
        o, w = CST[n]
        c[:, o:o + w] = a
    put('ident', np.eye(128))
    put('tri2', (same & (i[:, None] <= i[None, :])) * 1.0)
    put('triu2', (same & (i[:, None] > i[None, :])) * 1.0)
    put('cind0', np.broadcast_to((i < 64)[:, None] * 1.0, (128, 128)))
    put('cind1', np.broadcast_to((i >= 64)[:, None] * 1.0, (128, 128)))
    put('mb1', np.where(same & (i[:, None] > i[None, :]), 0.0, BIG))
    put('mb2', np.where(same & (i[None, :] >= i[:, None]), 0.0, -BIG))
    put('mlow', (i[None, :] < i[:, None]) * 1.0)
    put('ones64', same * 1.0)
    put('identS', (i[None, :] == i[:, None] + 64) * 1.0)
    put('mk1', np.where(i[None, :] >= i[:, None], -BIG, 0.0))
    put('mk2', np.where(i[None, :] >= i[:, None] + 64, -BIG, 0.0))
    for h in range(4):
        lg = RET_LG[h]
        dif = i[None, :] - i[:, None]
        put(f'DT{h}', np.where(dif >= 0, np.exp(lg * np.maximum(dif, 0)), 0.0))
        put(f'QD{h}', np.broadcast_to(np.exp(lg * (i + 1.0))[None, :], (128, 128)))
        o, w = CST['kds']
        c[:, o + h] = np.exp(lg * (127.0 - i))
    return c.astype(np.float32)


def make_rope(T):
    inv = 1.0 / (10000.0 ** (np.arange(0, 256, 2, dtype=np.float64) / 256.0))
    ang = inv[:, None] * np.arange(T, dtype=np.float64)[None, :]
    return np.stack([np.cos(ang), np.sin(ang)]).astype(np.float32)


class Rot:
    def __init__(self, mk, n):
        self.bufs = [mk(i) for i in range(n)]
        self.i = 0

    def next(self):
        b = self.bufs[self.i % len(self.bufs)]
        self.i += 1
        return b


class DT_:
    def __init__(self, nc, name, rows, T, dt, fm=True, kind="Internal"):
        shape = [rows, T] if fm else [T, rows]
        self.full = nc.dram_tensor(name, shape, dt, kind=kind).ap()
        self.fm = fm
        self.tiles = []
        for i in range(T // 512):
            ap = self.full[:, i * 512:(i + 1) * 512] if fm else self.full[i * 512:(i + 1) * 512, :]
            self.tiles.append(Buf(ap, f"{name}_t{i}", 'dram'))

    def tile(self, i):
        return self.tiles[i]

    def whole(self, ap):
        return View(self.tiles[0], ap)


GDN_STAGE = 99
SB_W = 1024


def build(T, debug=False, phases=None, n_layers=4):
    nc = bass.Bass("TRN2", target_bir_lowering=False)
    k = Kern(nc)
    NT = T // 128
    NTT = T // 512
    skind = "ExternalOutput" if debug else "Internal"

    def want(p):
        return phases is None or p in phases

    def din(name, shape, dt=F32, used=True):
        return Buf(nc.dram_tensor(name, list(shape), dt, kind="ExternalInput" if used else "Internal").ap(), name, 'dram')

    XT_in = DT_(nc, "xT", D, T, F32, True, "ExternalInput")
    C_in = din("c_col", [128, 8])
    ADAW = din("ada_w", [4, D, 6 * D], used=want("ada"))
    ADAB = din("ada_b_col", [4, 128, 48])
    NMIX = din("norm_mix_col", [4, 128, 8])
    NFFN = din("norm_ffn_col", [4, 128, 8])
    HWI = din("hyb_w_in", [2, D, HYB_IN], used=want("cast"))
    CONV = din("conv_col", [2, 128, 12, 4])
    ALOG = din("a_log_b", [2, 128, 4])
    DTB = din("dt_bias_b", [2, 128, 4])
    GNORM = din("gdn_norm_col", [2, 128, 1])
    QN = din("sb_q_norm_col", [2, 128, 1])
    KN = din("sb_k_norm_col", [2, 128, 1])
    HWO = din("hyb_w_out", [2, D, D], used=want("cast"))
    RWI = din("ret_w_in", [2, D, RET_IN], used=want("cast"))
    RWO = din("ret_w_out", [2, 2048, D], used=want("cast"))
    FWI = din("ffn_w_in", [4, D, 2 * FFN_H], used=want("cast"))
    FWO = din("ffn_w_out", [4, FFN_H, D], used=want("cast"))
    CSTD = din("cst", [128, NCST])
    ROPE = DT_(nc, "rope", 256, T, F32, True, "ExternalInput")
    OUT = DT_(nc, "outT", D, T, F32, True, "ExternalOutput")

    def dscr(name, shape, dt):
        return Buf(nc.dram_tensor(name, list(shape), dt, kind=skind).ap(), name, 'dram')
    HWIb = [dscr(f"hwib{i}", [D, HYB_IN], BF16) for i in range(2)]
    HWOb = [dscr(f"hwob{i}", [D, D], BF16) for i in range(2)]
    RWIb = [dscr(f"rwib{i}", [D, RET_IN], BF16) for i in range(2)]
    RWOb = [dscr(f"rwob{i}", [2048, D], BF16) for i in range(2)]
    FWIb = [dscr(f"fwib{i}", [D, 2 * FFN_H], BF16) for i in range(4)]
    FWOb = [dscr(f"fwob{i}", [8, 128, 22, 128], BF16) for i in range(4)]
    XT = DT_(nc, "xres", D, T, F32, True, skind)
    GQ = DT_(nc, "gq", 512, T, BF16, True, skind)
    GK = DT_(nc, "gk", 512, T, BF16, True, skind)
    GV = DT_(nc, "gv", 512, T, BF16, True, skind)
    GS = DT_(nc, "gs", 512, T, BF16, True, skind)
    GB = DT_(nc, "gbeta", 8, T, F32, False, skind)
    SQ = DT_(nc, "sq", 512, T, BF16, True, skind)
    SK = DT_(nc, "sk", 512, T, BF16, True, skind)
    SV = DT_(nc, "sv", 512, T, BF16, False, skind)
    OMIX = DT_(nc, "omix", 1024, T, BF16, True, skind)
    RQ = DT_(nc, "rq", 1024, T, BF16, True, skind)
    RK = DT_(nc, "rk", 1024, T, BF16, True, skind)
    RV = DT_(nc, "rv", 2048, T, BF16, False, skind)
    RG = DT_(nc, "rg", 2048, T, BF16, True, skind)
    RO = DT_(nc, "ro", 2048, T, BF16, True, skind)

    cst = k.sb([128, NCST], F32, 'cst')
    k.dma(cst.v, CSTD.v)

    def C(n):
        o, w = CST[n]
        return cst[:, o:o + w]
    ident = C('ident')
    ones_f = k.sb([128, 1024], F32, 'ones_f')
    k.memset(ones_f.v, 1.0)
    ones_bf = k.sb([128, 128], BF16, 'ones_bf')
    k.memset(ones_bf.v, 1.0)
    ident_bf = k.sb([128, 128], BF16, 'ident_bf')
    k.copy(ident_bf.v, ident, eng='dve')
    ones64_bf = k.sb([128, 128], BF16, 'ones64_bf')
    k.copy(ones64_bf.v, C('ones64'), eng='dve')
    modc = [k.sb([128, 48], F32, f'modc{l}') for l in range(4)]
    A_m = [k.sb([128, 8], F32, f'Am{l}') for l in range(4)]
    A_f = [k.sb([128, 8], F32, f'Af{l}') for l in range(4)]

    def psum_pool(n, shape=(128, 512), dt=F32, name='ps'):
        def mk(i):
            cm = nc.psum_tensor(f"{name}{i}_{k.sb_names}", list(shape), dt)
            k.sb_names += 1
            t = k.stacks[-1].enter_context(cm)
            return Buf(t.ap(), f"{name}{i}", 'ps')
        return Rot(mk, n)

    def sb_pool(n, shape, dt, name):
        return Rot(lambda i: k.sb(shape, dt, f"{name}{i}"), n)

    act_rr = [0]

    def phase_cast():
        k.phase_begin()
        CW = 2048
        fp = sb_pool(3, [128, CW], F32, 'cf')
        bp = sb_pool(3, [128, CW], BF16, 'cb')
        engs = ['pool', 'act', 'dve']
        cnt = 0
        jobs = []
        for i in range(2):
            if n_layers > 2 * i:
                jobs += [(HWI[i], HWIb[i]), (HWO[i], HWOb[i])]
            if n_layers > 2 * i + 1:
                jobs += [(RWI[i], RWIb[i]), (RWO[i], RWOb[i])]
        for l in range(n_layers):
            jobs += [(FWI[l], FWIb[l]), (FWO[l], FWOb[l])]
        for src, dst in jobs:
            if len(dst.ap.shape) == 4:
                for r in range(22):
                    f = fp.next()
                    b = bp.next()
                    k.dma(f[:, 0:D], src[r * 128:(r + 1) * 128, :])
                    k.copy(b[:, 0:D], f[:, 0:D], eng=engs[cnt % 3])
                    cnt += 1
                    k.dma(View(dst, dst.ap[:, :, r, :].rearrange("c p n -> p c n")),
                          View(b, b.ap[:, 0:D].rearrange("p (c n) -> p c n", c=8)), q='pool')
                continue
            K_, N_ = dst.ap.shape
            for r in range(K_ // 128):
                for c0 in range(0, N_, CW):
                    w = min(CW, N_ - c0)
                    f = fp.next()
                    b = bp.next()
                    k.dma(f[:, 0:w], src[r * 128:(r + 1) * 128, c0:c0 + w])
                    k.copy(b[:, 0:w], f[:, 0:w], eng=engs[cnt % 3])
                    cnt += 1
                    k.dma(dst[r * 128:(r + 1) * 128, c0:c0 + w], b[:, 0:w], q='pool')
        k.phase_end()

    def phase_ada():
        k.phase_begin()
        pp = psum_pool(2)
        ccol = k.sb([128, 8], F32, 'ccol')
        k.dma(ccol.v, C_in.v)
        cact = k.sb([128, 8], F32, 'cact')
        k.act(cact.v, ccol.v, AF.Silu)
        wp = sb_pool(2, [128, 8, 512], F32, 'adaw')
        for l in range(n_layers):
            ps = pp.next()
            for nch in range(12):
                w = wp.next()
                k.dma([w[:, kc, :] for kc in range(8)],
                      [ADAW[l, kc * 128:(kc + 1) * 128, nch * 512:(nch + 1) * 512] for kc in range(8)])
                for jj in range(4):
                    j = nch * 4 + jj
                    for kc in range(8):
                        k.mm(ps[:, j:j + 1], w[:, kc, jj * 128:(jj + 1) * 128], cact[:, kc:kc + 1],
                             start=(kc == 0), stop=(kc == 7))
            bcol = k.sb([128, 48], F32, 'bcol')
            k.dma(bcol.v, ADAB[l])
            k.tt(modc[l].v, ps[:, 0:48], bcol.v, ALU.add)
            for (A, NG, c0) in ((A_m[l], NMIX, 8), (A_f[l], NFFN, 32)):
                g = k.sb([128, 8], F32, 'g')
                k.dma(g.v, NG[l])
                t = k.sb([128, 8], F32, 't')
                k.ts(t.v, modc[l][:, c0:c0 + 8], 1.0, ALU.add)
                k.tt(A.v, t.v, g.v, ALU.mult)
        k.phase_end()

    def make_norm(pp):
        sqp = sb_pool(2, [128, 512], BF16, 'nsq')
        rp = sb_pool(2, [128, 512], F32, 'nr')
        tp = sb_pool(2, [128, 512], F32, 'nt')

        def norm_tile(xt, A, sh, hT):
            ps = pp.next()
            for kc in range(8):
                sq = sqp.next()
                k.act(sq.v, xt[:, kc, :], AF.Square)
                k.mm(ps.v, ones_bf.v, sq.v, start=(kc == 0), stop=(kc == 7))
            r1 = rp.next()
            k.act(r1.v, ps.v, AF.Ln, scale=1.0 / D, bias=EPS)
            rstd = rp.next()
            k.act(rstd.v, r1.v, AF.Exp, scale=-0.5)
            for kc in range(8):
                t = tp.next()
                k.stt(t.v, xt[:, kc, :], A[:, kc:kc + 1], rstd.v, ALU.mult, ALU.mult)
                k.act(hT[:, kc, :], t.v, AF.Identity, bias=sh[:, kc:kc + 1])
        return norm_tile

    def fm_view(dt, ti, r0, nch, p=128):
        b = dt.tile(ti)
        return View(b, b.ap[r0:r0 + nch * p, :].rearrange("(c p) t -> p c t", p=p))

    def tm_view(dt, ti, c0, c1):
        b = dt.tile(ti)
        return View(b, b.ap[:, c0:c1].rearrange("(t p) c -> p t c", p=128))

    def phase_p1_hyb(l, i, xsrc):
        k.phase_begin()
        pp = psum_pool(8)
        W = k.sb([128, 8, HYB_IN], BF16, 'Whyb')
        for kc in range(8):
            k.dma(W[:, kc, :], HWIb[i][kc * 128:(kc + 1) * 128, :])
        conv = k.sb([128, 12, 4], F32, 'conv')
        k.dma(conv.v, CONV[i])
        dtb = k.sb([128, 4], F32, 'dtb')
        k.dma(dtb.v, DTB[i])
        alog = k.sb([128, 4], F32, 'alog')
        k.dma(alog.v, ALOG[i])
        negA = k.sb([128, 4], F32, 'negA')
        k.act(negA.v, alog.v, AF.Exp)
        k.ts(negA.v, negA.v, -1.0, ALU.mult, eng='pool')
        qg = k.sb([128, 1], F32, 'qg')
        k.dma(qg.v, QN[i])
        k.ts(qg.v, qg.v, 0.125, ALU.mult, eng='pool')
        kg = k.sb([128, 1], F32, 'kg')
        k.dma(kg.v, KN[i])
        halo = k.sb([128, 12, 3], F32, 'halo')
        k.memset(halo.v, 0.0)
        xp = sb_pool(2, [128, 8, 512], F32, 'xt')
        hp = sb_pool(2, [128, 8, 512], BF16, 'hT')
        rawp = sb_pool(5, [128, 515], F32, 'raw')
        yp = sb_pool(5, [128, 512], F32, 'y')
        sp_ = sb_pool(5, [128, 512], F32, 's')
        sqp = sb_pool(5, [128, 512], BF16, 'sq1')
        rp = sb_pool(9, [128, 512], F32, 'r')
        obp = sb_pool(8, [128, 512], BF16, 'ob')
        gbp = sb_pool(2, [128, 4, 8], F32, 'gbo')
        smp = sb_pool(6, [128, 4, 4], F32, 'sm')
        norm_tile = make_norm(pp)
        for ti in range(NTT):
            xt = xp.next()
            k.dma(xt.v, fm_view(xsrc, ti, 0, 8))
            hT = hp.next()
            norm_tile(xt, A_m[l], modc[l][:, 0:8], hT)
            for grp in range(3):
                ccs = list(range(4 * grp, 4 * grp + 4))
                raws, ys, ss, obs = {}, {}, {}, {}
                for cc in ccs:
                    ps = pp.next()
                    for kc in range(8):
                        k.mm(ps.v, W[:, kc, cc * 128:(cc + 1) * 128], hT[:, kc, :], start=(kc == 0), stop=(kc == 7))
                    raw = rawp.next()
                    k.copy(raw[:, 0:3], halo[:, cc, :], eng='pool')
                    k.copy(raw[:, 3:515], ps.v, eng='act')
                    k.copy(halo[:, cc, :], raw[:, 512:515], eng='pool')
                    raws[cc] = raw
                for cc in ccs:
                    raw = raws[cc]
                    y = yp.next()
                    k.ts(y.v, raw[:, 0:512], conv[:, cc, 0:1], ALU.mult)
                    for j in range(1, 4):
                        k.stt(y.v, raw[:, j:j + 512], conv[:, cc, j:j + 1], y.v, ALU.mult, ALU.add)
                    ys[cc] = y
                for cc in ccs:
                    s_ = sp_.next()
                    k.act(s_.v, ys[cc].v, AF.Silu)
                    ss[cc] = s_
                if grp < 2:
                    sqs, ps2s, r2s = {}, {}, {}
                    for cc in ccs:
                        sq = sqp.next()
                        k.act(sq.v, ss[cc].v, AF.Square)
                        sqs[cc] = sq
                    for cc in ccs:
                        ps2 = pp.next()
                        k.mm(ps2.v, ones_bf.v, sqs[cc].v)
                        ps2s[cc] = ps2
                    for cc in ccs:
                        r1 = rp.next()
                        k.act(r1.v, ps2s[cc].v, AF.Ln, bias=EPS)
                        r2 = rp.next()
                        k.act(r2.v, r1.v, AF.Exp, scale=-0.5)
                        r2s[cc] = r2
                for cc in ccs:
                    ob = obp.next()
                    hh = cc % 4
                    if grp == 0:
                        k.stt(ob.v, ss[cc].v, 128.0 ** -0.5, r2s[cc].v, ALU.mult, ALU.mult)
                        dst = GQ
                    elif grp == 1:
                        k.tt(ob.v, ss[cc].v, r2s[cc].v, ALU.mult, eng='pool')
                        dst = GK
                    else:
                        k.copy(ob.v, ss[cc].v, eng='pool')
                        dst = GV
                    k.dma(dst.tile(ti)[hh * 128:(hh + 1) * 128, :], ob.v, q='pool')
            pss = []
            for hh in range(4):
                ps = pp.next()
                c0 = 1536 + hh * 128
                for kc in range(8):
                    k.mm(ps.v, W[:, kc, c0:c0 + 128], hT[:, kc, :], start=(kc == 0), stop=(kc == 7))
                pss.append(ps)
            for hh in range(4):
                ob = obp.next()
                k.act(ob.v, pss[hh].v, AF.Silu)
                k.dma(GS.tile(ti)[hh * 128:(hh + 1) * 128, :], ob.v, q='pool')
            for grp in range(2):
                cs4 = list(range(4 * grp, 4 * grp + 4))
                pss, sqs, ps2s, r2s = {}, {}, {}, {}
                for c in cs4:
                    ps = pp.next()
                    c0 = 2056 + c * 128
                    for kc in range(8):
                        k.mm(ps.v, W[:, kc, c0:c0 + 128], hT[:, kc, :], start=(kc == 0), stop=(kc == 7))
                    pss[c] = ps
                for c in cs4:
                    sq = sqp.next()
                    k.act(sq.v, pss[c].v, AF.Square)
                    sqs[c] = sq
                for c in cs4:
                    ps2 = pp.next()
                    k.mm(ps2.v, ones64_bf.v, sqs[c].v)
                    ps2s[c] = ps2
                for c in cs4:
                    r1 = rp.next()
                    k.act(r1.v, ps2s[c].v, AF.Ln, scale=1.0 / 64, bias=EPS)
                    r2 = rp.next()
                    k.act(r2.v, r1.v, AF.Exp, scale=-0.5)
                    r2s[c] = r2
                for c in cs4:
                    ob = obp.next()
                    k.stt(ob.v, pss[c].v, (qg if c < 4 else kg)[:, 0:1], r2s[c].v, ALU.mult, ALU.mult)
                    dst = SQ if c < 4 else SK
                    cc = c % 4
                    k.dma(dst.tile(ti)[cc * 128:(cc + 1) * 128, :], ob.v, q='pool')
            for tb in range(4):
                ps = pp.next()
                for kc in range(8):
                    k.mm(ps.v, hT[:, kc, tb * 128:(tb + 1) * 128], W[:, kc, 3080:3592], start=(kc == 0), stop=(kc == 7))
                ob = obp.next()
                k.copy(ob.v, ps.v, eng='act' if tb % 2 else 'dve')
                k.dma(SV.tile(ti)[tb * 128:(tb + 1) * 128, :], ob.v, q='pool')
            ps = pp.next()
            for tb in range(4):
                for kc in range(8):
                    k.mm(ps[:, tb * 8:(tb + 1) * 8], hT[:, kc, tb * 128:(tb + 1) * 128], W[:, kc, 2048:2056],
                         start=(kc == 0), stop=(kc == 7))
            pv = ps[:, 0:32].re("p (t e) -> p t e", e=8)
            dtb_b = View(dtb, dtb.ap.unsqueeze(1).broadcast_to([128, 4, 4]))
            negA_b = View(negA, negA.ap.unsqueeze(1).broadcast_to([128, 4, 4]))
            z = smp.next()
            k.tt(z.v, pv[:, :, 0:4], dtb_b, ALU.add)
            e1 = smp.next()
            k.act(e1.v, z.v, AF.Exp)
            s1 = smp.next()
            k.act(s1.v, e1.v, AF.Ln, bias=1.0)
            gbo = gbp.next()
            k.tt(gbo[:, :, 0:4], s1.v, negA_b, ALU.mult)
            e2 = smp.next()
            k.act(e2.v, pv[:, :, 4:8], AF.Exp, scale=-1.0)
            d2 = smp.next()
            k.ts(d2.v, e2.v, 1.0, ALU.add)
            k.add('dve', lambda e, o=gbo.ap[:, :, 4:8], i_=d2.ap: e.reciprocal(o, i_), reads=[d2], writes=[gbo])
            k.dma(tm_view(GB, ti, 0, 8), gbo.v, q='pool')
        k.phase_end()

    def phase_gdn(l, i):
        k.phase_begin()
        pp = psum_pool(4)
        pc = psum_pool(2, name='psc')
        ppo = psum_pool(2, name='pso')
        gn = k.sb([128, 1], F32, 'gn')
        k.dma(gn.v, GNORM[i])
        ident4 = k.sb([128, 4, 128], F32, 'ident4')
        for h in range(4):
            k.copy(ident4[:, h, :], ident, eng='pool')
        S = k.sb([128, 4, 128], F32, 'S')
        k.memset(S.v, 0.0)
        mb1x4 = k.sb([128, 4, 128], F32, 'mb1x4')
        mb2x4 = k.sb([128, 4, 128], F32, 'mb2x4')
        for h in range(4):
            k.copy(mb1x4[:, h, :], C('mb1'), eng='pool')
            k.copy(mb2x4[:, h, :], C('mb2'), eng='pool')
        ldp = {n: sb_pool(2, [128, 4, 512], BF16, n) for n in ('kT', 'qT', 'vT', 'gsT')}
        gbp = sb_pool(2, [128, 4, 8], F32, 'gbt')
        osp = sb_pool(2, [128, 4, 512], BF16, 'ost')
        f4 = {n: sb_pool(2, [128, 4, 128], F32, n) for n in
              ('kdec', 'kw', 'vb', 'gbc', 'RAs', 'RBs', 'E', 'ET', 'EQ', 't1', 'A', 'Aqk', 'u', 'wT', 'qd', 'osb', 'o2', 'r1', 'r2')}
        f4r = {n: sb_pool(3, [128, 4, 128], F32, n) for n in ('Sx', 'STx', 'PT')}
        nbf = {n: sb_pool(3, [128, 4, 128], BF16, n + 'b') for n in ('Sx', 'STx', 'PTb')}
        sqp = sb_pool(2, [128, 4, 128], BF16, 'gsq')
        smp = {n: sb_pool(2, [128, w], F32, n) for n, w in (('cs', 16), ('ex', 16), ('ngc', 4), ('kws', 4), ('nb4', 4))}
        vnzp = [sb_pool(2, [128, 4, 128], F32, f'vnz{c}') for c in range(2)]
        for c in range(2):
            for b in vnzp[c].bufs:
                k.memset(b.v, 0.0)

        def f2(b):
            return b.v.re("p h d -> p (h d)")
        tctx = {}

        def tile_ctx(ti):
            if ti not in tctx:
                kT4 = ldp['kT'].next()
                k.dma(kT4.v, fm_view(GK, ti, 0, 4))
                qT4 = ldp['qT'].next()
                k.dma(qT4.v, fm_view(GQ, ti, 0, 4))
                vT4 = ldp['vT'].next()
                k.dma(vT4.v, fm_view(GV, ti, 0, 4))
                gs4 = ldp['gsT'].next()
                k.dma(gs4.v, fm_view(GS, ti, 0, 4))
                gbt = gbp.next()
                k.dma(gbt.v, tm_view(GB, ti, 0, 8))
                tctx[ti] = dict(kT4=kT4, qT4=qT4, vT4=vT4, gs4=gs4, gbt=gbt, ost=osp.next())
            return tctx[ti]

        def prep(ti, tb, out):
            c_ = tile_ctx(ti)
            kT4, qT4, vT4, gs4, gbt, ost = c_['kT4'], c_['qT4'], c_['vT4'], c_['gs4'], c_['gbt'], c_['ost']
            blk = slice(tb * 128, (tb + 1) * 128)
            g4 = gbt[:, tb, 0:4]
            b4 = gbt[:, tb, 4:8]
            cps = pp.next()
            k.mm(cps[:, 0:4], C('tri2'), g4)
            k.mm(cps[:, 4:8], C('triu2'), g4)
            k.mm(cps[:, 8:12], C('cind0'), g4)
            k.mm(cps[:, 12:16], C('cind1'), g4)
            cs = smp['cs'].next()
            k.copy(cs.v, cps[:, 0:16], eng='dve')
            ex = smp['ex'].next()
            k.act(ex.v, cps[:, 0:16], AF.Exp)
            ngc = smp['ngc'].next()
            k.ts(ngc.v, cs[:, 0:4], -1.0, ALU.mult, eng='pool')
            kws = smp['kws'].next()
            k.tt(kws.v, b4, ex[:, 0:4], ALU.mult, eng='pool')
            nb4 = smp['nb4'].next()
            k.ts(nb4.v, b4, -1.0, ALU.mult, eng='pool')
            yield
            trp = pp.next()
            tv = trp.v.bitcast(BF16)
            for h in range(4):
                k.transpose(tv[:, h * 128:(h + 1) * 128], kT4[:, h, blk], ident_bf.v)
            for h in range(4):
                k.transpose(tv[:, 512 + h * 128:512 + (h + 1) * 128], vT4[:, h, blk], ident_bf.v)
            kdec = f4['kdec'].next()
            kw = f4['kw'].next()
            vb = f4['vb'].next()
            for h in range(4):
                k.ts(kdec[:, h, :], tv[:, h * 128:(h + 1) * 128], ex[:, 4 + h:5 + h], ALU.mult)
                k.act(kw[:, h, :], tv[:, h * 128:(h + 1) * 128], AF.Identity, scale=kws[:, h:h + 1])
                k.act(vb[:, h, :], tv[:, 512 + h * 128:512 + (h + 1) * 128], AF.Identity, scale=b4[:, h:h + 1])
            yield
            gbc = f4['gbc'].next()
            for h in range(4):
                k.ts(gbc[:, h, :], ones_f[:, 0:128], g4[:, h:h + 1], ALU.mult, eng='pool')
            RC = pp.next()
            for h in range(4):
                hs = slice(h * 128, (h + 1) * 128)
                k.mm(RC[:, hs], gbc[:, h, :], C('tri2'))
            RAs = f4['RAs'].next()
            k.tt(f2(RAs), RC.v, f2(mb1x4), ALU.add)
            RBs = f4['RBs'].next()
            k.tt(f2(RBs), RC.v, f2(mb2x4), ALU.add)
            E = f4['E'].next()
            ET = f4['ET'].next()
            EQ = f4['EQ'].next()
            for h in range(4):
                k.act(E[:, h, :], RAs[:, h, :], AF.Exp, scale=-1.0, bias=cs[:, h:h + 1])
                k.act(ET[:, h, :], RBs[:, h, :], AF.Exp, bias=ngc[:, h:h + 1])
            k.act(f2(EQ), RC.v, AF.Exp)
            yield
            KK = pp.next()
            KQ = pp.next()
            for h in range(4):
                hs = slice(h * 128, (h + 1) * 128)
                k.mm(KK[:, hs], kT4[:, h, blk], kT4[:, h, blk])
                k.mm(KQ[:, hs], kT4[:, h, blk], qT4[:, h, blk])
            t1 = f4['t1'].next()
            k.tt(f2(t1), KK.v, f2(E), ALU.mult)
            A = f4['A'].next()
            for h in range(4):
                k.ts(A[:, h, :], t1[:, h, :], nb4[:, h:h + 1], ALU.mult, eng='pool')
            Aqk = f4['Aqk'].next()
            k.tt(f2(Aqk), KQ.v, f2(ET), ALU.mult)
            yield
            ATp = pp.next()
            for h in range(4):
                k.transpose(ATp[:, h * 128:(h + 1) * 128], A[:, h, :], ident)
            ST_ = f4r['STx'].next()
            k.copy(f2(ST_), ATp.v, eng='act')
            PT = f4r['PT'].next()
            k.tt(f2(PT), ATp.v, f2(ident4), ALU.add)
            S_ = A
            PTb = None
            for lev in range(1, 6):
                lo = lev >= 2
                Sp = pp.next()
                for h in range(4):
                    k.mm(Sp[:, h * 128:(h + 1) * 128], ST_[:, h, :], S_[:, h, :])
                Sn = nbf['Sx'].next()
                k.copy(f2(Sn), Sp.v, eng='act')
                if lev < 5:
                    STp = pp.next()
                    for h in range(4):
                        k.mm(STp[:, h * 128:(h + 1) * 128], S_[:, h, :], ST_[:, h, :])
                    STn = nbf['STx'].next()
                    k.copy(f2(STn), STp.v, eng='dve')
                Pp = pp.next()
                if lev == 1:
                    Sn32 = f4r['Sx'].next()
                    k.copy(f2(Sn32), Sp.v, eng='act')
                    for h in range(4):
                        k.mm(Pp[:, h * 128:(h + 1) * 128], Sn32[:, h, :], PT[:, h, :])
                else:
                    for h in range(4):
                        k.mm(Pp[:, h * 128:(h + 1) * 128], Sn[:, h, :], PTb[:, h, :])
                PTn = f4r['PT'].next()
                k.tt(f2(PTn), Pp.v, f2(PT), ALU.add)
                if lev < 5:
                    PTb = nbf['PTb'].next()
                    k.copy(f2(PTb), f2(PTn), eng='pool')
                PT = PTn
                S_ = Sn
                if lev < 5:
                    ST_ = STn
                yield
            yield
            up = pp.next()
            wp_ = pp.next()
            for h in range(4):
                hs = slice(h * 128, (h + 1) * 128)
                k.mm(up[:, hs], PT[:, h, :], vb[:, h, :])
                k.mm(wp_[:, hs], kw[:, h, :], PT[:, h, :])
            u = f4['u'].next()
            k.copy(f2(u), up.v, eng='act')
            wT = f4['wT'].next()
            k.copy(f2(wT), wp_.v, eng='dve')
            qd = f4['qd'].next()
            k.tt(qd.v, qT4[:, :, blk], EQ.v, ALU.mult, eng='pool')
            out.update(dict(kdec=kdec, wT=wT, u=u, qd=qd, Aqk=Aqk, ex=ex, blk=blk, gs4=gs4, ost=ost))
            yield

        def chain(ti, tb, o, pump):
            kdec, wT, u, qd, Aqk, ex, blk, gs4, ost = (o[x] for x in ('kdec', 'wT', 'u', 'qd', 'Aqk', 'ex', 'blk', 'gs4', 'ost'))
            oTp = ppo.next()
            for c in range(2):
                cs_ = slice(c * 64, (c + 1) * 64)
                vnz = vnzp[c].next()
                vnp = pc.next()
                for h in range(4):
                    k.mm(vnp[:, h * 128:(h + 1) * 128], wT[:, h, :], S[:, h, :])
                k.tt(vnz[cs_, :, :].re("p h d -> p (h d)"), u[cs_, :, :].re("p h d -> p (h d)"), vnp[cs_, :], ALU.subtract)
                pump(2)
                for h in range(4):
                    o_ = oTp[:, h * 128 + c * 64:h * 128 + (c + 1) * 64]
                    k.mm(o_, S[:, h, :], qd[:, h, cs_], start=True, stop=False)
                    k.mm(o_, vnz[:, h, :], Aqk[:, h, cs_], start=False, stop=True)
                dSp = pc.next()
                for h in range(4):
                    k.mm(dSp[:, h * 128:(h + 1) * 128], kdec[:, h, :], vnz[:, h, :])
                for h in range(4):
                    k.stt(S[:, h, :], S[:, h, :], ex[:, 8 + 4 * c + h:9 + 4 * c + h], dSp[:, h * 128:(h + 1) * 128],
                          ALU.mult, ALU.add)
                pump(3)
            osb = f4['osb'].next()
            k.copy(f2(osb), oTp.v, eng='dve')
            sq = sqp.next()
            k.act(f2(sq), oTp.v, AF.Square)
            ssp = pc.next()
            for h in range(4):
                k.mm(ssp[:, h * 128:(h + 1) * 128], ones_bf.v, sq[:, h, :])
            r1 = f4['r1'].next()
            k.act(f2(r1), ssp.v, AF.Ln, scale=1.0 / 128, bias=EPS)
            r2 = f4['r2'].next()
            k.act(f2(r2), f2(r1), AF.Exp, scale=-0.5)
            o2 = f4['o2'].next()
            k.tt(f2(o2), f2(osb), f2(r2), ALU.mult)
            k.stt(ost[:, :, blk], o2.v, gn[:, 0:1], gs4[:, :, blk], ALU.mult, ALU.mult)
            if tb == 3:
                k.dma(fm_view(OMIX, ti, 0, 4), ost.v, q='pool')

        blocks = [(ti, tb) for ti in range(NTT) for tb in range(4)]
        outs = [dict() for _ in blocks]
        g0 = prep(blocks[0][0], blocks[0][1], outs[0])
        for _ in g0:
            pass
        for bi, (ti, tb) in enumerate(blocks):
            nxt = prep(blocks[bi + 1][0], blocks[bi + 1][1], outs[bi + 1]) if bi + 1 < len(blocks) else None

            def pump(n, nxt=nxt):
                if nxt is None:
                    return
                for _ in range(n):
                    try:
                        next(nxt)
                    except StopIteration:
                        return
            chain(ti, tb, outs[bi], pump)
            if nxt is not None:
                for _ in nxt:
                    pass
            outs[bi].clear()
        k.phase_end()

    def phase_sb(l, i):
        k.phase_begin()
        Wk = SB_W
        zp = psum_pool(2, (128, Wk), F32, 'z')
        atp = psum_pool(2, (128, Wk), BF16, 'aT')
        op_ = psum_pool(2, (128, 512), F32, 'oT')
        qp = sb_pool(2, [64, T], BF16, 'qh')
        kp = sb_pool(2, [64, T], BF16, 'kh')
        vp = sb_pool(2, [128, NT, 64], BF16, 'vh')
        ohp = sb_pool(2, [64, T], BF16, 'oh')
        ep = sb_pool(4, [128, Wk], F32, 'e')
        spp = sb_pool(3, [128, Wk], F32, 'sp')
        gp = sb_pool(3, [128, Wk + 1], F32, 'G')
        for b_ in gp.bufs:
            k.memset(b_[:, 0:1], 0.0)
        pp_ = sb_pool(2, [128, Wk], BF16, 'p')
        ap_ = sb_pool(4, [128, Wk], BF16, 'a')
        atsp = sb_pool(3, [128, Wk], BF16, 'aTs')
        bp = sb_pool(10, [128, 1], F32, 'bias')
        mkc = k.sb([64, 3, 128], BF16, 'mkc')
        k.copy(mkc[:, 0, :], C('identS')[0:64, :], eng='dve')
        k.copy(mkc[:, 1, :], C('mk1')[0:64, :], eng='dve')
        k.copy(mkc[:, 2, :], C('mk2')[0:64, :], eng='dve')
        tiles = []
        for h in range(8):
            for qb in range(NT):
                t0 = qb * 128
                nkt = (t0 + 128 + Wk - 1) // Wk
                for kt in reversed(range(nkt)):
                    tiles.append(dict(h=h, qb=qb, t0=t0, k0=kt * Wk, w=min(Wk, t0 + 128 - kt * Wk),
                                      diag=(kt == nkt - 1), lastq=(kt == 0), idx=len(tiles)))
        heads = {}

        def load_head(h):
            if h in heads or h >= 8:
                return
            qh = qp.next()
            k.dma(qh.v, SQ.whole(SQ.full[h * 64:(h + 1) * 64, :]), reads=SQ.tiles)
            kh = kp.next()
            k.dma(kh.v, SK.whole(SK.full[h * 64:(h + 1) * 64, :]), reads=SK.tiles)
            vh = vp.next()
            vstep = min(8, NT)
            for n0 in range(0, NT, vstep):
                k.dma(vh[:, n0:n0 + vstep, :],
                      SV.whole(SV.full[n0 * 128:(n0 + vstep) * 128, h * 64:(h + 1) * 64].rearrange("(n p) d -> p n d", p=128)),
                      reads=SV.tiles)
            heads[h] = dict(qh=qh, kh=kh, vh=vh, oh=ohp.next())

        def S1(t):
            h = t['h']
            if h not in heads:
                load_head(h)
            hd = heads[h]
            w, k0, t0 = t['w'], t['k0'], t['t0']
            z = zp.next()
            wm = w - 128 if t['diag'] else w
            for c0 in range(0, wm, 512):
                cw = min(512, wm - c0)
                k.mm(z[:, c0:c0 + cw], hd['qh'][:, t0:t0 + 128], hd['kh'][:, k0 + c0:k0 + c0 + cw])
            if t['diag']:
                k.mm(z[:, wm:w], hd['qh'][:, t0:t0 + 128], hd['kh'][:, k0 + wm:k0 + w], start=True, stop=False)
                k.mm(z[:, wm:w], ident_bf[0:64, :], mkc[:, 1, :], start=False, stop=False)
                k.mm(z[:, wm:w], mkc[:, 0, :], mkc[:, 2, :], start=False, stop=True)
            e = ep.next()
            k.act(e[:, 0:w], z[:, 0:w], AF.Exp)
            sp = spp.next()
            k.act(sp[:, 0:w], e[:, 0:w], AF.Ln, bias=1.0)
            t['e'] = e
            t['sp'] = sp

        def S2(t):
            w = t['w']
            G = gp.next()
            k.scan(G[:, 1:w + 1], ones_f[:, 0:w], t['sp'][:, 0:w], 0.0, ALU.mult, ALU.add)
            t['G'] = G

        def S3(t):
            w = t['w']
            G = t['G']
            bias = bp.next()
            if t['diag']:
                k.act(bias.v, G[:, w:w + 1], AF.Identity, scale=-1.0)
            else:
                k.act(bias.v, G[:, w:w + 1], AF.Identity, scale=-1.0, bias=tiles[t['idx'] - 1]['bias'][:, 0:1])
            t['bias'] = bias
            p = pp_.next()
            k.act(p[:, 0:w], G[:, 0:w], AF.Exp, bias=bias[:, 0:1])
            a = ap_.next()
            k.tt(a[:, 0:w], t['e'][:, 0:w], p[:, 0:w], ALU.mult, eng='pool')
            t['a'] = a

        def S4(t):
            w = t['w']
            aT = atp.next()
            for sb in range(w // 128):
                k.transpose(aT[:, sb * 128:(sb + 1) * 128], t['a'][:, sb * 128:(sb + 1) * 128], ident_bf.v)
            aTs = atsp.next()
            k.copy(aTs[:, 0:w], aT[:, 0:w], eng='dve')
            t['aTs'] = aTs

        def S5(t):
            w, k0, t0 = t['w'], t['k0'], t['t0']
            hd = heads[t['h']]
            if t['diag'] and t['qb'] == 0:
                load_head(t['h'] + 1)
            if t['diag']:
                t['oT'] = op_.next()
            else:
                t['oT'] = tiles[t['idx'] - 1]['oT']
            oT = t['oT']
            nsb = w // 128
            for sb in range(nsb):
                k.mm(oT[0:64, 0:128], hd['vh'][:, k0 // 128 + sb, :], t['aTs'][:, sb * 128:(sb + 1) * 128],
                     start=(t['diag'] and sb == 0), stop=(t['lastq'] and sb == nsb - 1))
            if t['lastq']:
                k.copy(hd['oh'][:, t0:t0 + 128], oT[0:64, 0:128], eng='act')
                if t['qb'] == NT - 1:
                    h = t['h']
                    k.dma(OMIX.whole(OMIX.full[512 + h * 64:512 + (h + 1) * 64, :]), hd['oh'].v, q='sp', writes=OMIX.tiles)
            for key in ('e', 'sp', 'G', 'a', 'aTs'):
                t.pop(key, None)

        n = len(tiles)
        for s_ in range(n + 6):
            if 0 <= s_ - 4 < n:
                S4(tiles[s_ - 4])
            if 0 <= s_ - 2 < n:
                S3(tiles[s_ - 2])
            if 0 <= s_ - 1 < n:
                S2(tiles[s_ - 1])
            if s_ < n:
                S1(tiles[s_])
            if 0 <= s_ - 5 < n:
                S5(tiles[s_ - 5])
        k.phase_end()

    def phase_p1_ret(l, i, xsrc):
        k.phase_begin()
        pp = psum_pool(8)
        W = k.sb([128, 8, RET_IN], BF16, 'Wret')
        for kc in range(8):
            k.dma(W[:, kc, :], RWIb[i][kc * 128:(kc + 1) * 128, :])
        xp = sb_pool(1, [128, 8, 512], F32, 'xt')
        hp = sb_pool(1, [128, 8, 512], BF16, 'hT')
        csp = sb_pool(2, [128, 2, 512], F32, 'cs')
        tp = sb_pool(4, [128, 512], F32, 'rt')
        qst = sb_pool(2, [128, 8, 512], BF16, 'qst')
        kst = qst
        gst = sb_pool(1, [128, 8, 512], BF16, 'gst')
        vst = sb_pool(1, [128, 2, 2048], BF16, 'vst')
        norm_tile = make_norm(pp)
        for ti in range(NTT):
            xt = xp.next()
            k.dma(xt.v, fm_view(xsrc, ti, 0, 8))
            cs = csp.next()
            k.dma(cs.v, fm_view(ROPE, ti, 0, 2))
            hT = hp.next()
            norm_tile(xt, A_m[l], modc[l][:, 0:8], hT)
            cos = cs[:, 0, :]
            sin = cs[:, 1, :]
            for which in range(2):
                st = (qst if which == 0 else kst).next()
                for h in range(4):
                    c0 = which * 1024 + h * 256
                    p1 = pp.next()
                    p2 = pp.next()
                    for kc in range(8):
                        k.mm(p1.v, W[:, kc, c0:c0 + 128], hT[:, kc, :], start=(kc == 0), stop=(kc == 7))
                    for kc in range(8):
                        k.mm(p2.v, W[:, kc, c0 + 128:c0 + 256], hT[:, kc, :], start=(kc == 0), stop=(kc == 7))
                    sc = 1.0 if which == 0 else 1.0 / 16.0
                    t1 = tp.next()
                    t2 = tp.next()
                    t3 = tp.next()
                    t4 = tp.next()
                    k.stt(t1.v, p1.v, sc, cos, ALU.mult, ALU.mult)
                    k.stt(t2.v, p2.v, sc, sin, ALU.mult, ALU.mult)
                    k.stt(t3.v, p1.v, sc, sin, ALU.mult, ALU.mult)
                    k.stt(t4.v, p2.v, sc, cos, ALU.mult, ALU.mult)
                    k.tt(st[:, 2 * h, :], t1.v, t2.v, ALU.subtract, eng='pool')
                    k.tt(st[:, 2 * h + 1, :], t3.v, t4.v, ALU.add, eng='pool')
                k.dma(fm_view(RQ if which == 0 else RK, ti, 0, 8), st.v, q='pool')
            for half in range(2):
                g_ = gst.next()
                for cl in range(8):
                    c = half * 8 + cl
                    ps = pp.next()
                    c0 = 4096 + c * 128
                    for kc in range(8):
                        k.mm(ps.v, W[:, kc, c0:c0 + 128], hT[:, kc, :], start=(kc == 0), stop=(kc == 7))
                    k.act(g_[:, cl, :], ps.v, AF.Silu)
                k.dma(fm_view(RG, ti, half * 1024, 8), g_.v, q='pool')
            for half in range(2):
                v_ = vst.next()
                for tl in range(2):
                    tb = half * 2 + tl
                    for nb in range(4):
                        ps = pp.next()
                        c0 = 2048 + nb * 512
                        for kc in range(8):
                            k.mm(ps.v, hT[:, kc, tb * 128:(tb + 1) * 128], W[:, kc, c0:c0 + 512], start=(kc == 0), stop=(kc == 7))
                        k.copy(v_[:, tl, nb * 512:(nb + 1) * 512], ps.v, eng='dve' if nb % 2 else 'act')
                bt = RV.tile(ti)
                k.dma(View(bt, bt.ap[half * 256:(half + 1) * 256, :].rearrange("(t p) c -> p t c", p=128)), v_.v, q='pool')
        k.phase_end()

    def phase_ret(l, i):
        k.phase_begin()
        pt4p = psum_pool(1, (128, 512), F32, 'pt4')
        trpp = psum_pool(1, (128, 512), F32, 'trp')
        otp = psum_pool(4, (128, 512), F32, 'oTp')
        pp2 = psum_pool(1, (128, 1024), F32, 'dS')
        Sst = [k.sb([128, 1024], F32, f'S{h}') for h in range(4)]
        Sb = [k.sb([128, 2, 512], BF16, f'Sb{h}') for h in range(4)]
        for h in range(4):
            k.memset(Sst[h].v, 0.0)
            k.memset(Sb[h].v, 0.0, eng='dve')
        qp = sb_pool(2, [128, 8, 512], BF16, 'q8')
        kp = sb_pool(2, [128, 8, 512], BF16, 'k8')
        vp = sb_pool(2, [128, 4, 2048], BF16, 'v4')
        gp = sb_pool(1, [128, 16, 512], BF16, 'g16')
        osp = sb_pool(2, [128, 16, 512], BF16, 'o16')
        P4p = sb_pool(2, [128, 4, 128], BF16, 'P4')
        kd4p = sb_pool(2, [128, 4, 256], BF16, 'kd4')
        qd4p = sb_pool(2, [128, 8, 128], BF16, 'qd4')
        sqp = sb_pool(2, [128, 4, 512], BF16, 'rsq')
        rp = sb_pool(4, [128, 512], F32, 'rr')
        o2p = sb_pool(4, [128, 4, 128], F32, 'o2')
        kds = C('kds')
        for ti in range(NTT):
            q8 = qp.next()
            k.dma(q8.v, fm_view(RQ, ti, 0, 8))
            k8 = kp.next()
            k.dma(k8.v, fm_view(RK, ti, 0, 8))
            v4 = vp.next()
            k.dma(v4.v, tm_view(RV, ti, 0, 2048))
            g16 = gp.next()
            k.dma(g16.v, fm_view(RG, ti, 0, 16))
            ost = osp.next()
            for tb in range(4):
                blk = slice(tb * 128, (tb + 1) * 128)
                PT4 = pt4p.next()
                trp = trpp.next()
                tv = trp.v.bitcast(BF16)
                for h in range(4):
                    hs = slice(h * 128, (h + 1) * 128)
                    k.mm(PT4[:, hs], k8[:, 2 * h, blk], q8[:, 2 * h, blk], start=True, stop=False)
                    k.mm(PT4[:, hs], k8[:, 2 * h + 1, blk], q8[:, 2 * h + 1, blk], start=False, stop=True)
                for h in range(4):
                    for d in range(2):
                        k.transpose(tv[:, h * 256 + d * 128:h * 256 + (d + 1) * 128], k8[:, 2 * h + d, blk], ident_bf.v)
                P4 = P4p.next()
                kd4 = kd4p.next()
                qd4 = qd4p.next()
                for h in range(4):
                    k.tt(P4[:, h, :], PT4[:, h * 128:(h + 1) * 128], C(f'DT{h}'), ALU.mult)
                for h in range(4):
                    k.ts(kd4[:, h, :], tv[:, h * 256:(h + 1) * 256], kds[:, h:h + 1], ALU.mult)
                for h in range(4):
                    for d in range(2):
                        k.tt(qd4[:, 2 * h + d, :], q8[:, 2 * h + d, blk], C(f'QD{h}'), ALU.mult, eng='pool')
                oTps = []
                for h in range(4):
                    oTp = otp.next()
                    for dvc in range(4):
                        o_ = oTp[:, dvc * 128:(dvc + 1) * 128]
                        k.mm(o_, v4[:, tb, h * 512 + dvc * 128:h * 512 + (dvc + 1) * 128], P4[:, h, :], start=True, stop=False)
                        k.mm(o_, Sb[h][:, 0, dvc * 128:(dvc + 1) * 128], qd4[:, 2 * h, :], start=False, stop=False)
                        k.mm(o_, Sb[h][:, 1, dvc * 128:(dvc + 1) * 128], qd4[:, 2 * h + 1, :], start=False, stop=True)
                    oTps.append(oTp)
                for h in range(4):
                    dSp = pp2.next()
                    for d in range(2):
                        k.mm(dSp[:, d * 512:(d + 1) * 512], kd4[:, h, d * 128:(d + 1) * 128], v4[:, tb, h * 512:(h + 1) * 512])
                    k.stt(Sst[h].v, Sst[h].v, float(np.exp(RET_LG[h] * 128.0)), dSp.v, ALU.mult, ALU.add)
                    k.copy(Sb[h].v.re("p a b -> p (a b)"), Sst[h].v, eng='act')
                sq = sqp.next()
                for h in range(4):
                    k.act(sq[:, h, :], oTps[h].v, AF.Square)
                ssp = pt4p.next()
                for h in range(4):
                    for dvc in range(4):
                        k.mm(ssp[:, h * 128:(h + 1) * 128], ones_bf.v, sq[:, h, dvc * 128:(dvc + 1) * 128], start=(dvc == 0), stop=(dvc == 3))
                r1 = rp.next()
                k.act(r1.v, ssp.v, AF.Ln, scale=1.0 / 512, bias=EPS)
                r2 = rp.next()
                k.act(r2.v, r1.v, AF.Exp, scale=-0.5)
                for h in range(4):
                    o2 = o2p.next()
                    r2b = View(r2, r2.ap[:, h * 128:(h + 1) * 128].unsqueeze(1).broadcast_to([128, 4, 128]))
                    k.tt(o2.v, oTps[h].v.re("p (c t) -> p c t", c=4), r2b, ALU.mult)
                    k.tt(ost[:, 4 * h:4 * h + 4, blk], o2.v, g16[:, 4 * h:4 * h + 4, blk], ALU.mult, eng='pool')
            k.dma(fm_view(RO, ti, 0, 16), ost.v, q='pool')
        k.phase_end()

    def phase_out(l, i, hyb, xsrc):
        k.phase_begin()
        pp = psum_pool(4)
        if hyb:
            wA = k.sb([128, 8, D], BF16, 'wA')
            for c in range(0, 8, 4):
                k.dma(wA[:, c:c + 4, :], View(HWOb[i], HWOb[i].ap[c * 128:(c + 4) * 128, :].rearrange("(c p) n -> p c n", p=128)))
            oap = sb_pool(2, [128, 8, 512], BF16, 'oa')
        else:
            wR = k.sb([128, 16, D], BF16, 'wR')
            for c in range(0, 16, 4):
                k.dma(wR[:, c:c + 4, :], View(RWOb[i], RWOb[i].ap[c * 128:(c + 4) * 128, :].rearrange("(c p) n -> p c n", p=128)))
            oap = sb_pool(2, [128, 16, 512], BF16, 'oa')
        xp = sb_pool(2, [128, 8, 512], F32, 'xt')
        gt = modc[l][:, 16:24]
        for ti in range(NTT):
            xt = xp.next()
            k.dma(xt.v, fm_view(xsrc, ti, 0, 8))
            oa = oap.next()
            if hyb:
                k.dma(oa.v, fm_view(OMIX, ti, 0, 8))
            else:
                k.dma(oa.v, fm_view(RO, ti, 0, 16))
            for dc in range(8):
                ps = pp.next()
                ds_ = slice(dc * 128, (dc + 1) * 128)
                if hyb:
                    for c in range(8):
                        k.mm(ps.v, wA[:, c, ds_], oa[:, c, :], start=(c == 0), stop=(c == 7))
                else:
                    for c in range(16):
                        k.mm(ps.v, wR[:, c, ds_], oa[:, c, :], start=(c == 0), stop=(c == 15))
                k.stt(xt[:, dc, :], ps.v, gt[:, dc:dc + 1], xt[:, dc, :], ALU.mult, ALU.add)
            k.dma(fm_view(XT, ti, 0, 8), xt.v, q='pool')
        k.phase_end()

    def phase_ffn(l, xdst):
        k.phase_begin()
        pp = psum_pool(8)
        W1 = k.sb([128, 8, 2 * FFN_H], BF16, 'W1')
        for kc in range(8):
            k.dma(W1[:, kc, :], FWIb[l][kc * 128:(kc + 1) * 128, :])
        w2p = sb_pool(2, [128, 22, 128], BF16, 'w2')
        xp = sb_pool(2, [128, 8, 512], F32, 'xt')
        hp = sb_pool(2, [128, 8, 512], BF16, 'hT')
        ap_ = sb_pool(1, [128, 22, 512], BF16, 'actT')
        sp_ = sb_pool(3, [128, 512], F32, 'sl')
        norm_tile = make_norm(pp)
        gt = modc[l][:, 40:48]
        for ti in range(NTT):
            xt = xp.next()
            k.dma(xt.v, fm_view(XT, ti, 0, 8))
            hT = hp.next()
            norm_tile(xt, A_f[l], modc[l][:, 24:32], hT)
            aT = ap_.next()
            for j in range(22):
                pg = pp.next()
                pu = pp.next()
                for kc in range(8):
                    k.mm(pg.v, W1[:, kc, j * 128:(j + 1) * 128], hT[:, kc, :], start=(kc == 0), stop=(kc == 7))
                for kc in range(8):
                    k.mm(pu.v, W1[:, kc, FFN_H + j * 128:FFN_H + (j + 1) * 128], hT[:, kc, :], start=(kc == 0), stop=(kc == 7))
                s = sp_.next()
                k.act(s.v, pg.v, AF.Silu)
                k.tt(aT[:, j, :], s.v, pu.v, ALU.mult)
            for dc in range(8):
                w2 = w2p.next()
                k.dma(w2.v, View(FWOb[l], FWOb[l].ap[dc]))
                ps = pp.next()
                for j in range(22):
                    k.mm(ps.v, w2[:, j, :], aT[:, j, :], start=(j == 0), stop=(j == 21))
                k.stt(xt[:, dc, :], ps.v, gt[:, dc:dc + 1], xt[:, dc, :], ALU.mult, ALU.add)
            k.dma(fm_view(xdst, ti, 0, 8), xt.v, q='pool')
        k.phase_end()

    if want('cast'):
        phase_cast()
    if want('ada'):
        phase_ada()
    for l in range(n_layers):
        i = l // 2
        xsrc = XT_in if l == 0 else XT
        xdst = OUT if l == n_layers - 1 else XT
        if l % 2 == 0:
            if want(f'p1_{l}'):
                phase_p1_hyb(l, i, xsrc)
            if want(f'gdn_{l}'):
                phase_gdn(l, i)
            if want(f'sb_{l}'):
                phase_sb(l, i)
            if want(f'out_{l}'):
                phase_out(l, i, True, xsrc)
        else:
            if want(f'p1_{l}'):
                phase_p1_ret(l, i, xsrc)
            if want(f'ret_{l}'):
                phase_ret(l, i)
            if want(f'out_{l}'):
                phase_out(l, i, False, xsrc)
        if want(f'ffn_{l}'):
            phase_ffn(l, xdst)
    k.barrier()
    k.finalize()
    return nc, k


def host_inputs(b, T, x, c, ada_w, ada_b, norm_mix, norm_ffn, hyb_w_in, hyb_conv, gdn_a_log, gdn_dt_bias,
                gdn_norm, sb_q_norm, sb_k_norm, hyb_w_out, ret_w_in, ret_w_out, ffn_w_in, ffn_w_out, shared):
    f = np.float32
    m = dict(shared)
    m["xT"] = np.ascontiguousarray(x[b, :T].T)
    m["c_col"] = np.ascontiguousarray(c[b].reshape(8, 128).T)
    return m


def host_shared(T, ada_w, ada_b, norm_mix, norm_ffn, hyb_w_in, hyb_conv, gdn_a_log, gdn_dt_bias,
                gdn_norm, sb_q_norm, sb_k_norm, hyb_w_out, ret_w_in, ret_w_out, ffn_w_in, ffn_w_out):
    ca = np.ascontiguousarray
    m = {}
    m["ada_w"] = ca(ada_w)
    m["ada_b_col"] = ca(ada_b.reshape(4, 48, 128).transpose(0, 2, 1))
    m["norm_mix_col"] = ca(norm_mix.reshape(4, 8, 128).transpose(0, 2, 1))
    m["norm_ffn_col"] = ca(norm_ffn.reshape(4, 8, 128).transpose(0, 2, 1))
    m["hyb_w_in"] = ca(hyb_w_in)
    m["conv_col"] = ca(hyb_conv.reshape(2, 4, 12, 128).transpose(0, 3, 2, 1))
    m["a_log_b"] = ca(np.broadcast_to(gdn_a_log[:, None, :], (2, 128, 4)))
    m["dt_bias_b"] = ca(np.broadcast_to(gdn_dt_bias[:, None, :], (2, 128, 4)))
    m["gdn_norm_col"] = ca(gdn_norm.reshape(2, 128, 1))
    m["sb_q_norm_col"] = ca(np.concatenate([sb_q_norm, sb_q_norm], axis=1).reshape(2, 128, 1))
    m["sb_k_norm_col"] = ca(np.concatenate([sb_k_norm, sb_k_norm], axis=1).reshape(2, 128, 1))
    m["hyb_w_out"] = ca(hyb_w_out)
    m["ret_w_in"] = ca(ret_w_in)
    m["ret_w_out"] = ca(ret_w_out)
    m["ffn_w_in"] = ca(ffn_w_in)
    m["ffn_w_out"] = ca(ffn_w_out)
    m["cst"] = make_consts()
    m["rope"] = ca(make_rope(T).reshape(256, T))
    return m


_CACHE = {}


def kernel(x, c, ada_w, ada_b, norm_mix, norm_ffn, hyb_w_in, hyb_conv, gdn_a_log, gdn_dt_bias,
           gdn_norm, sb_q_norm, sb_k_norm, hyb_w_out, ret_w_in, ret_w_out, ffn_w_in, ffn_w_out):
    args = [np.asarray(a, dtype=np.float32) for a in
            (x, c, ada_w, ada_b, norm_mix, norm_ffn, hyb_w_in, hyb_conv, gdn_a_log, gdn_dt_bias,
             gdn_norm, sb_q_norm, sb_k_norm, hyb_w_out, ret_w_in, ret_w_out, ffn_w_in, ffn_w_out)]
    x, c = args[0], args[1]
    B, T, _ = x.shape
    shared = host_shared(T, *args[2:])
    in_maps = [host_inputs(b, T, *args, shared) for b in range(B)]
    if T not in _CACHE:
        _CACHE[T] = build(T)[0]
    nc = _CACHE[T]
    res = run_bass_kernel_spmd(nc, in_maps, core_ids=list(range(B)))
    out = np.stack([np.asarray(r["outT"]).T for r in res.results], axis=0)
    return np.ascontiguousarray(out.astype(np.float32))
```

```python
import numpy as np
import concourse.bass as bass
import concourse.mybir as mybir

F32 = mybir.dt.float32
BF16 = mybir.dt.bfloat16
AF = mybir.ActivationFunctionType
ALU = mybir.AluOpType
AX = mybir.AxisListType

EPOCH = 30000
COMPUTE = ('pe', 'act', 'dve', 'pool')
ALLQ = ('pe', 'act', 'dve', 'pool', 'sp')


class Buf:
    __slots__ = ('ap', 'name', 'last_w', 'readers', 'sem', 'cnt', 'space')

    def __init__(self, ap, name='', space='sb'):
        self.ap = ap
        self.name = name
        self.last_w = None
        self.readers = []
        self.sem = None
        self.cnt = 0
        self.space = space

    def __getitem__(self, idx):
        return View(self, self.ap[idx])

    @property
    def v(self):
        return View(self, self.ap)


class View:
    __slots__ = ('buf', 'ap')

    def __init__(self, buf, ap):
        self.buf = buf
        self.ap = ap

    def __getitem__(self, idx):
        return View(self.buf, self.ap[idx])

    def re(self, pat, **kw):
        return View(self.buf, self.ap.rearrange(pat, **kw))

    def bitcast(self, dt):
        return View(self.buf, self.ap.bitcast(dt))


class Op:
    __slots__ = ('eng', 'fn', 'deps', 'signal', 'n', 'is_dma', 'slot', 'val', 'ndma', 'tag', 'sem')

    def __init__(self, eng, fn, is_dma=False, slot=None, ndma=0, tag=''):
        self.eng = eng
        self.fn = fn
        self.deps = ()
        self.signal = False
        self.n = -1
        self.is_dma = is_dma
        self.slot = slot
        self.val = 0
        self.ndma = ndma
        self.tag = tag


def _bufs(xs):
    out = []
    for x in xs:
        if x is None:
            continue
        if isinstance(x, View):
            out.append(x.buf)
        elif isinstance(x, Buf):
            out.append(x)
        elif isinstance(x, (list, tuple)):
            out.extend(_bufs(x))
    return out


class Kern:
    def __init__(self, nc):
        self.nc = nc
        self.ops = {q: [] for q in ALLQ}
        self.all_ops = []
        self.sb_off = 0
        self.sb_names = 0
        self.last_dma_by_slot = {}
        import contextlib
        self.stacks = [contextlib.ExitStack()]
        self.free_sems = {}
        self.marks = []
        self.phase_slots = [[]]

    def sb(self, shape, dtype, name=None):
        self.sb_names += 1
        nm = f"sb{self.sb_names}_{name or ''}"
        cm = self.nc.sbuf_tensor(nm, list(shape), dtype)
        t = self.stacks[-1].enter_context(cm)
        return Buf(t.ap(), nm, 'sb')

    def phase_begin(self):
        import contextlib
        self.stacks.append(contextlib.ExitStack())
        self.phase_slots.append([])

    def phase_end(self):
        self.marks.append({q: sum(1 for o in self.ops[q] if o.fn is not None and not o.is_dma) for q in COMPUTE})
        self.barrier()
        self.stacks.pop().close()
        for slot, q in self.phase_slots.pop():
            self.free_sems.setdefault(q, []).append((slot.sem[q], slot.cnt[q]))

    def add(self, eng, fn, reads=(), writes=(), is_dma=False, slot=None, ndma=0, tag=''):
        op = Op(eng, fn, is_dma, slot, ndma, tag)
        R = _bufs(reads)
        W = _bufs(writes)
        deps = set()
        for b in R:
            if b.last_w is not None:
                deps.add(b.last_w)
            if b.space == 'ps':
                for r in b.readers:
                    if r.eng != eng:
                        deps.add(r)
        for b in W:
            if b.last_w is not None:
                deps.add(b.last_w)
            deps.update(b.readers)
        deps.discard(op)
        op.deps = tuple(deps)
        for b in W:
            b.last_w = op
            b.readers = []
        for b in R:
            if b in W:
                continue
            if not is_dma:
                b.readers = [r for r in b.readers if r.is_dma or r.eng != eng]
            b.readers.append(op)
        if is_dma:
            slot.cnt[eng] += ndma
            op.val = 16 * slot.cnt[eng]
            op.sem = slot.sem[eng]
            assert op.val < 60000, f"dma sem overflow on {slot.name}"
            self.last_dma_by_slot[(id(slot), eng)] = op
        self.ops[eng].append(op)
        self.all_ops.append(op)
        return op

    def barrier(self):
        lasts = []
        for q in COMPUTE:
            for o in reversed(self.ops[q]):
                if not o.is_dma and o.fn is not None:
                    lasts.append(o)
                    break
        lasts.extend(self.last_dma_by_slot.values())
        self.last_dma_by_slot = {}
        for q in ALLQ:
            op = Op(q, None)
            op.deps = tuple(lasts)
            self.ops[q].append(op)
            self.all_ops.append(op)

    def mm(self, out, lhsT, rhs, start=True, stop=True, **kw):
        o, l, r = out.ap, lhsT.ap, rhs.ap
        return self.add('pe', lambda e: e.matmul(o, l, r, start=start, stop=stop, **kw),
                        reads=[lhsT, rhs], writes=[out])

    def transpose(self, out, in_, ident):
        o, i, d = out.ap, in_.ap, ident.ap
        return self.add('pe', lambda e: e.transpose(o, i, d), reads=[in_, ident], writes=[out])

    def act(self, out, in_, func, bias=None, scale=None, accum_out=None, eng='act'):
        kw = {}
        reads = [in_]
        if bias is not None:
            if isinstance(bias, View):
                kw['bias'] = bias.ap
                reads.append(bias)
            else:
                kw['bias'] = bias
        if scale is not None:
            if isinstance(scale, View):
                kw['scale'] = scale.ap
                reads.append(scale)
            else:
                kw['scale'] = scale
        writes = [out]
        if accum_out is not None:
            kw['accum_out'] = accum_out.ap
            writes.append(accum_out)
        o, i = out.ap, in_.ap
        return self.add('act', lambda e: e.activation(o, i, func, **kw), reads=reads, writes=writes)

    def tt(self, out, in0, in1, op, eng='dve'):
        o, a, b = out.ap, in0.ap, in1.ap
        return self.add(eng, lambda e: e.tensor_tensor(o, a, b, op), reads=[in0, in1], writes=[out])

    def ts(self, out, in0, s1, op0, s2=None, op1=None, eng='dve', accum_out=None):
        reads = [in0]
        a1 = s1
        if isinstance(s1, View):
            a1 = s1.ap
            reads.append(s1)
        a2 = s2
        if isinstance(s2, View):
            a2 = s2.ap
            reads.append(s2)
        o, i = out.ap, in0.ap
        kw = {}
        writes = [out]
        if accum_out is not None:
            kw['accum_out'] = accum_out.ap
            writes.append(accum_out)
        if op1 is None:
            return self.add(eng, lambda e: e.tensor_scalar(o, i, a1, None, op0, **kw), reads=reads, writes=writes)
        return self.add(eng, lambda e: e.tensor_scalar(o, i, a1, a2, op0, op1, **kw), reads=reads, writes=writes)

    def stt(self, out, in0, scalar, in1, op0, op1, eng='dve'):
        reads = [in0, in1]
        sc = scalar
        if isinstance(scalar, View):
            sc = scalar.ap
            reads.append(scalar)
        o, a, b = out.ap, in0.ap, in1.ap
        return self.add(eng, lambda e: e.scalar_tensor_tensor(o, a, sc, b, op0, op1), reads=reads, writes=[out])

    def scan(self, out, d0, d1, initial, op0, op1):
        reads = [d0, d1]
        ini = initial
        if isinstance(initial, View):
            ini = initial.ap
            reads.append(initial)
        o, a, b = out.ap, d0.ap, d1.ap
        return self.add('dve', lambda e: e.tensor_tensor_scan(o, a, b, ini, op0, op1), reads=reads, writes=[out])

    def copy(self, out, in_, eng='dve'):
        o, i = out.ap, in_.ap
        if eng == 'act':
            return self.add('act', lambda e: e.copy(o, i), reads=[in_], writes=[out])
        return self.add(eng, lambda e: e.tensor_copy(o, i), reads=[in_], writes=[out])

    def memset(self, out, val, eng='pool'):
        o = out.ap
        return self.add(eng, lambda e: e.memset(o, val), reads=[], writes=[out])

    def dma(self, out, in_, q='sp', slot=None, reads=None, writes=None, **kw):
        outs = out if isinstance(out, (list, tuple)) else [out]
        ins = in_ if isinstance(in_, (list, tuple)) else [in_]
        pairs = [(o.ap, i.ap) for o, i in zip(outs, ins)]
        if slot is None:
            slot = outs[0].buf if outs[0].buf.space == 'sb' else ins[0].buf
        if slot.sem is None:
            slot.sem = {}
            slot.cnt = {}
        if q not in slot.sem:
            if self.free_sems.get(q):
                slot.sem[q], slot.cnt[q] = self.free_sems[q].pop()
            else:
                slot.sem[q] = self.new_sem()
                slot.cnt[q] = 0
            self.phase_slots[-1].append((slot, q))
        sem_ = slot.sem[q]

        def fn(e, pairs=pairs, sem=sem_, kw=kw):
            last = None
            for (o, i) in pairs:
                last = e.dma_start(out=o, in_=i, **kw).then_inc(sem, 16)
            return None
        op = self.add(q, fn, reads=ins if reads is None else reads,
                      writes=outs if writes is None else writes,
                      is_dma=True, slot=slot, ndma=len(pairs))
        return op

    def new_sem(self):
        self._nsem = getattr(self, '_nsem', 0) + 1
        cm = self.nc.semaphore(f"s{self._nsem}")
        sem = cm.__enter__()
        self._sem_cms = getattr(self, '_sem_cms', [])
        self._sem_cms.append(cm)
        return sem

    def finalize(self):
        nc = self.nc
        for op in self.all_ops:
            for d in op.deps:
                if d.is_dma:
                    continue
                if d.eng == 'pe' and op.eng == 'pe' and not op.is_dma:
                    continue
                d.signal = True
        eng_sems = {}
        for q in ALLQ:
            n = 0
            for op in self.ops[q]:
                if op.signal and not op.is_dma:
                    op.n = n
                    n += 1
            eng_sems[q] = [self.new_sem() for _ in range((n + EPOCH - 1) // EPOCH)]
        self.n_instr = {q: len(self.ops[q]) for q in ALLQ}

        def emit(q, e):
            waited_eng = {}
            waited_dma = {}
            for op in self.ops[q]:
                need_eng = {}
                need_dma = {}
                for d in op.deps:
                    if d.is_dma:
                        k = d.sem
                        if waited_dma.get(id(k), 0) < d.val:
                            if need_dma.get(id(k), (None, 0))[1] < d.val:
                                need_dma[id(k)] = (k, d.val)
                    else:
                        if d.eng == 'pe' and q == 'pe' and not op.is_dma:
                            continue
                        if waited_eng.get(d.eng, -1) < d.n:
                            if need_eng.get(d.eng, -1) < d.n:
                                need_eng[d.eng] = d.n
                for pe_, n in need_eng.items():
                    e.wait_ge(eng_sems[pe_][n // EPOCH], n % EPOCH + 1)
                    waited_eng[pe_] = n
                for _, (k, v) in need_dma.items():
                    e.wait_ge(k, v)
                    waited_dma[id(k)] = v
                if op.fn is None:
                    continue
                ins = op.fn(e)
                if op.signal and not op.is_dma:
                    ins.then_inc(eng_sems[q][op.n // EPOCH], 1)

        with nc.Block() as block:
            @block.tensor
            def _(e):
                emit('pe', e)

            @block.scalar
            def _(e):
                emit('act', e)

            @block.vector
            def _(e):
                emit('dve', e)

            @block.gpsimd
            def _(e):
                emit('pool', e)

            @block.sync
            def _(e):
                emit('sp', e)

from concourse.bass_utils import run_bass_kernel_spmd

D = 1024
EPS = 1e-6
FFN_H = 2816
HYB_IN = 3592
RET_IN = 6144
BIG = 30000.0
RET_LG = [float(np.log1p(-np.exp2(-5.0 - h))) for h in range(4)]

CST = {}
_off = 0
for _n, _w in [('ident', 128), ('tri2', 128), ('triu2', 128), ('cind0', 128), ('cind1', 128),
               ('mb1', 128), ('mb2', 128), ('mlow', 128), ('ones64', 128),
               ('DT0', 128), ('DT1', 128), ('DT2', 128), ('DT3', 128),
               ('QD0', 128), ('QD1', 128), ('QD2', 128), ('QD3', 128), ('kds', 4),
               ('identS', 128), ('mk1', 128), ('mk2', 128)]:
    CST[_n] = (_off, _w)
    _off += _w
NCST = _off


def make_consts():
    c = np.zeros((128, NCST), np.float64)
    i = np.arange(128)
    same = (i[:, None] // 64) == (i[None, :] // 64)

    def put(n, a):
        o, w = CST[n]
        c[:, o:o + w] = a
    put('ident', np.eye(128))
    put('tri2', (same & (i[:, None] <= i[None, :])) * 1.0)
    put('triu2', (same & (i[:, None] > i[None, :])) * 1.0)
    put('cind0', np.broadcast_to((i < 64)[:, None] * 1.0, (128, 128)))
    put('cind1', np.broadcast_to((i >= 64)[:, None] * 1.0, (128, 128)))
    put('mb1', np.where(same & (i[:, None] > i[None, :]), 0.0, BIG))
    put('mb2', np.where(same & (i[None, :] >= i[:, None]), 0.0, -BIG))
    put('mlow', (i[None, :] < i[:, None]) * 1.0)
    put('ones64', same * 1.0)
    put('identS', (i[None, :] == i[:, None] + 64) * 1.0)
    put('mk1', np.where(i[None, :] >= i[:, None], -BIG, 0.0))
    put('mk2', np.where(i[None, :] >= i[:, None] + 64, -BIG, 0.0))
    for h in range(4):
        lg = RET_LG[h]
        dif = i[None, :] - i[:, None]
        put(f'DT{h}', np.where(dif >= 0, np.exp(lg * np.maximum(dif, 0)), 0.0))
        put(f'QD{h}', np.broadcast_to(np.exp(lg * (i + 1.0))[None, :], (128, 128)))
        o, w = CST['kds']
        c[:, o + h] = np.exp(lg * (127.0 - i))
    return c.astype(np.float32)


def make_rope(T):
    inv = 1.0 / (10000.0 ** (np.arange(0, 256, 2, dtype=np.float64) / 256.0))
    ang = inv[:, None] * np.arange(T, dtype=np.float64)[None, :]
    return np.stack([np.cos(ang), np.sin(ang)]).astype(np.float32)


class Rot:
    def __init__(self, mk, n):
        self.bufs = [mk(i) for i in range(n)]
        self.i = 0

    def next(self):
        b = self.bufs[self.i % len(self.bufs)]
        self.i += 1
        return b


class DT_:
    def __init__(self, nc, name, rows, T, dt, fm=True, kind="Internal"):
        shape = [rows, T] if fm else [T, rows]
        self.full = nc.dram_tensor(name, shape, dt, kind=kind).ap()
        self.fm = fm
        self.tiles = []
        for i in range(T // 512):
            ap = self.full[:, i * 512:(i + 1) * 512] if fm else self.full[i * 512:(i + 1) * 512, :]
            self.tiles.append(Buf(ap, f"{name}_t{i}", 'dram'))

    def tile(self, i):
        return self.tiles[i]

    def whole(self, ap):
        return View(self.tiles[0], ap)


GDN_STAGE = 99
SB_W = 1024


def build(T, debug=False, phases=None, n_layers=4):
    nc = bass.Bass("TRN2", target_bir_lowering=False)
    k = Kern(nc)
    NT = T // 128
    NTT = T // 512
    skind = "ExternalOutput" if debug else "Internal"

    def want(p):
        return phases is None or p in phases

    def din(name, shape, dt=F32, used=True):
        return Buf(nc.dram_tensor(name, list(shape), dt, kind="ExternalInput" if used else "Internal").ap(), name, 'dram')

    XT_in = DT_(nc, "xT", D, T, F32, True, "ExternalInput")
    C_in = din("c_col", [128, 8])
    ADAW = din("ada_w", [4, D, 6 * D], used=want("ada"))
    ADAB = din("ada_b_col", [4, 128, 48])
    NMIX = din("norm_mix_col", [4, 128, 8])
    NFFN = din("norm_ffn_col", [4, 128, 8])
    HWI = din("hyb_w_in", [2, D, HYB_IN], used=want("cast"))
    CONV = din("conv_col", [2, 128, 12, 4])
    ALOG = din("a_log_b", [2, 128, 4])
    DTB = din("dt_bias_b", [2, 128, 4])
    GNORM = din("gdn_norm_col", [2, 128, 1])
    QN = din("sb_q_norm_col", [2, 128, 1])
    KN = din("sb_k_norm_col", [2, 128, 1])
    HWO = din("hyb_w_out", [2, D, D], used=want("cast"))
    RWI = din("ret_w_in", [2, D, RET_IN], used=want("cast"))
    RWO = din("ret_w_out", [2, 2048, D], used=want("cast"))
    FWI = din("ffn_w_in", [4, D, 2 * FFN_H], used=want("cast"))
    FWO = din("ffn_w_out", [4, FFN_H, D], used=want("cast"))
    CSTD = din("cst", [128, NCST])
    ROPE = DT_(nc, "rope", 256, T, F32, True, "ExternalInput")
    OUT = DT_(nc, "outT", D, T, F32, True, "ExternalOutput")

    def dscr(name, shape, dt):
        return Buf(nc.dram_tensor(name, list(shape), dt, kind=skind).ap(), name, 'dram')
    HWIb = [dscr(f"hwib{i}", [D, HYB_IN], BF16) for i in range(2)]
    HWOb = [dscr(f"hwob{i}", [D, D], BF16) for i in range(2)]
    RWIb = [dscr(f"rwib{i}", [D, RET_IN], BF16) for i in range(2)]
    RWOb = [dscr(f"rwob{i}", [2048, D], BF16) for i in range(2)]
    FWIb = [dscr(f"fwib{i}", [D, 2 * FFN_H], BF16) for i in range(4)]
    FWOb = [dscr(f"fwob{i}", [8, 128, 22, 128], BF16) for i in range(4)]
    XT = DT_(nc, "xres", D, T, F32, True, skind)
    GQ = DT_(nc, "gq", 512, T, BF16, True, skind)
    GK = DT_(nc, "gk", 512, T, BF16, True, skind)
    GV = DT_(nc, "gv", 512, T, BF16, True, skind)
    GS = DT_(nc, "gs", 512, T, BF16, True, skind)
    GB = DT_(nc, "gbeta", 8, T, F32, False, skind)
    SQ = DT_(nc, "sq", 512, T, BF16, True, skind)
    SK = DT_(nc, "sk", 512, T, BF16, True, skind)
    SV = DT_(nc, "sv", 512, T, BF16, False, skind)
    OMIX = DT_(nc, "omix", 1024, T, BF16, True, skind)
    RQ = DT_(nc, "rq", 1024, T, BF16, True, skind)
    RK = DT_(nc, "rk", 1024, T, BF16, True, skind)
    RV = DT_(nc, "rv", 2048, T, BF16, False, skind)
    RG = DT_(nc, "rg", 2048, T, BF16, True, skind)
    RO = DT_(nc, "ro", 2048, T, BF16, True, skind)

    cst = k.sb([128, NCST], F32, 'cst')
    k.dma(cst.v, CSTD.v)

    def C(n):
        o, w = CST[n]
        return cst[:, o:o + w]
    ident = C('ident')
    ones_f = k.sb([128, 1024], F32, 'ones_f')
    k.memset(ones_f.v, 1.0)
    ones_bf = k.sb([128, 128], BF16, 'ones_bf')
    k.memset(ones_bf.v, 1.0)
    ident_bf = k.sb([128, 128], BF16, 'ident_bf')
    k.copy(ident_bf.v, ident, eng='dve')
    ones64_bf = k.sb([128, 128], BF16, 'ones64_bf')
    k.copy(ones64_bf.v, C('ones64'), eng='dve')
    modc = [k.sb([128, 48], F32, f'modc{l}') for l in range(4)]
    A_m = [k.sb([128, 8], F32, f'Am{l}') for l in range(4)]
    A_f = [k.sb([128, 8], F32, f'Af{l}') for l in range(4)]

    def psum_pool(n, shape=(128, 512), dt=F32, name='ps'):
        def mk(i):
            cm = nc.psum_tensor(f"{name}{i}_{k.sb_names}", list(shape), dt)
            k.sb_names += 1
            t = k.stacks[-1].enter_context(cm)
            return Buf(t.ap(), f"{name}{i}", 'ps')
        return Rot(mk, n)

    def sb_pool(n, shape, dt, name):
        return Rot(lambda i: k.sb(shape, dt, f"{name}{i}"), n)

    act_rr = [0]

    def phase_cast():
        k.phase_begin()
        CW = 2048
        fp = sb_pool(3, [128, CW], F32, 'cf')
        bp = sb_pool(3, [128, CW], BF16, 'cb')
        engs = ['pool', 'act', 'dve']
        cnt = 0
        jobs = []
        for i in range(2):
            if n_layers > 2 * i:
                jobs += [(HWI[i], HWIb[i]), (HWO[i], HWOb[i])]
            if n_layers > 2 * i + 1:
                jobs += [(RWI[i], RWIb[i]), (RWO[i], RWOb[i])]
        for l in range(n_layers):
            jobs += [(FWI[l], FWIb[l]), (FWO[l], FWOb[l])]
        for src, dst in jobs:
            if len(dst.ap.shape) == 4:
                for r in range(22):
                    f = fp.next()
                    b = bp.next()
                    k.dma(f[:, 0:D], src[r * 128:(r + 1) * 128, :])
                    k.copy(b[:, 0:D], f[:, 0:D], eng=engs[cnt % 3])
                    cnt += 1
                    k.dma(View(dst, dst.ap[:, :, r, :].rearrange("c p n -> p c n")),
                          View(b, b.ap[:, 0:D].rearrange("p (c n) -> p c n", c=8)), q='pool')
                continue
            K_, N_ = dst.ap.shape
            for r in range(K_ // 128):
                for c0 in range(0, N_, CW):
                    w = min(CW, N_ - c0)
                    f = fp.next()
                    b = bp.next()
                    k.dma(f[:, 0:w], src[r * 128:(r + 1) * 128, c0:c0 + w])
                    k.copy(b[:, 0:w], f[:, 0:w], eng=engs[cnt % 3])
                    cnt += 1
                    k.dma(dst[r * 128:(r + 1) * 128, c0:c0 + w], b[:, 0:w], q='pool')
        k.phase_end()

    def phase_ada():
        k.phase_begin()
        pp = psum_pool(2)
        ccol = k.sb([128, 8], F32, 'ccol')
        k.dma(ccol.v, C_in.v)
        cact = k.sb([128, 8], F32, 'cact')
        k.act(cact.v, ccol.v, AF.Silu)
        wp = sb_pool(2, [128, 8, 512], F32, 'adaw')
        for l in range(n_layers):
            ps = pp.next()
            for nch in range(12):
                w = wp.next()
                k.dma([w[:, kc, :] for kc in range(8)],
                      [ADAW[l, kc * 128:(kc + 1) * 128, nch * 512:(nch + 1) * 512] for kc in range(8)])
                for jj in range(4):
                    j = nch * 4 + jj
                    for kc in range(8):
                        k.mm(ps[:, j:j + 1], w[:, kc, jj * 128:(jj + 1) * 128], cact[:, kc:kc + 1],
                             start=(kc == 0), stop=(kc == 7))
            bcol = k.sb([128, 48], F32, 'bcol')
            k.dma(bcol.v, ADAB[l])
            k.tt(modc[l].v, ps[:, 0:48], bcol.v, ALU.add)
            for (A, NG, c0) in ((A_m[l], NMIX, 8), (A_f[l], NFFN, 32)):
                g = k.sb([128, 8], F32, 'g')
                k.dma(g.v, NG[l])
                t = k.sb([128, 8], F32, 't')
                k.ts(t.v, modc[l][:, c0:c0 + 8], 1.0, ALU.add)
                k.tt(A.v, t.v, g.v, ALU.mult)
        k.phase_end()

    def make_norm(pp):
        sqp = sb_pool(2, [128, 512], BF16, 'nsq')
        rp = sb_pool(2, [128, 512], F32, 'nr')
        tp = sb_pool(2, [128, 512], F32, 'nt')

        def norm_tile(xt, A, sh, hT):
            ps = pp.next()
            for kc in range(8):
                sq = sqp.next()
                k.act(sq.v, xt[:, kc, :], AF.Square)
                k.mm(ps.v, ones_bf.v, sq.v, start=(kc == 0), stop=(kc == 7))
            r1 = rp.next()
            k.act(r1.v, ps.v, AF.Ln, scale=1.0 / D, bias=EPS)
            rstd = rp.next()
            k.act(rstd.v, r1.v, AF.Exp, scale=-0.5)
            for kc in range(8):
                t = tp.next()
                k.stt(t.v, xt[:, kc, :], A[:, kc:kc + 1], rstd.v, ALU.mult, ALU.mult)
                k.act(hT[:, kc, :], t.v, AF.Identity, bias=sh[:, kc:kc + 1])
        return norm_tile

    def fm_view(dt, ti, r0, nch, p=128):
        b = dt.tile(ti)
        return View(b, b.ap[r0:r0 + nch * p, :].rearrange("(c p) t -> p c t", p=p))

    def tm_view(dt, ti, c0, c1):
        b = dt.tile(ti)
        return View(b, b.ap[:, c0:c1].rearrange("(t p) c -> p t c", p=128))

    def phase_p1_hyb(l, i, xsrc):
        k.phase_begin()
        pp = psum_pool(8)
        W = k.sb([128, 8, HYB_IN], BF16, 'Whyb')
        for kc in range(8):
            k.dma(W[:, kc, :], HWIb[i][kc * 128:(kc + 1) * 128, :])
        conv = k.sb([128, 12, 4], F32, 'conv')
        k.dma(conv.v, CONV[i])
        dtb = k.sb([128, 4], F32, 'dtb')
        k.dma(dtb.v, DTB[i])
        alog = k.sb([128, 4], F32, 'alog')
        k.dma(alog.v, ALOG[i])
        negA = k.sb([128, 4], F32, 'negA')
        k.act(negA.v, alog.v, AF.Exp)
        k.ts(negA.v, negA.v, -1.0, ALU.mult, eng='pool')
        qg = k.sb([128, 1], F32, 'qg')
        k.dma(qg.v, QN[i])
        k.ts(qg.v, qg.v, 0.125, ALU.mult, eng='pool')
        kg = k.sb([128, 1], F32, 'kg')
        k.dma(kg.v, KN[i])
        halo = k.sb([128, 12, 3], F32, 'halo')
        k.memset(halo.v, 0.0)
        xp = sb_pool(2, [128, 8, 512], F32, 'xt')
        hp = sb_pool(2, [128, 8, 512], BF16, 'hT')
        rawp = sb_pool(5, [128, 515], F32, 'raw')
        yp = sb_pool(5, [128, 512], F32, 'y')
        sp_ = sb_pool(5, [128, 512], F32, 's')
        sqp = sb_pool(5, [128, 512], BF16, 'sq1')
        rp = sb_pool(9, [128, 512], F32, 'r')
        obp = sb_pool(8, [128, 512], BF16, 'ob')
        gbp = sb_pool(2, [128, 4, 8], F32, 'gbo')
        smp = sb_pool(6, [128, 4, 4], F32, 'sm')
        norm_tile = make_norm(pp)
        for ti in range(NTT):
            xt = xp.next()
            k.dma(xt.v, fm_view(xsrc, ti, 0, 8))
            hT = hp.next()
            norm_tile(xt, A_m[l], modc[l][:, 0:8], hT)
            for grp in range(3):
                ccs = list(range(4 * grp, 4 * grp + 4))
                raws, ys, ss, obs = {}, {}, {}, {}
                for cc in ccs:
                    ps = pp.next()
                    for kc in range(8):
                        k.mm(ps.v, W[:, kc, cc * 128:(cc + 1) * 128], hT[:, kc, :], start=(kc == 0), stop=(kc == 7))
                    raw = rawp.next()
                    k.copy(raw[:, 0:3], halo[:, cc, :], eng='pool')
                    k.copy(raw[:, 3:515], ps.v, eng='act')
                    k.copy(halo[:, cc, :], raw[:, 512:515], eng='pool')
                    raws[cc] = raw
                for cc in ccs:
                    raw = raws[cc]
                    y = yp.next()
                    k.ts(y.v, raw[:, 0:512], conv[:, cc, 0:1], ALU.mult)
                    for j in range(1, 4):
                        k.stt(y.v, raw[:, j:j + 512], conv[:, cc, j:j + 1], y.v, ALU.mult, ALU.add)
                    ys[cc] = y
                for cc in ccs:
                    s_ = sp_.next()
                    k.act(s_.v, ys[cc].v, AF.Silu)
                    ss[cc] = s_
                if grp < 2:
                    sqs, ps2s, r2s = {}, {}, {}
                    for cc in ccs:
                        sq = sqp.next()
                        k.act(sq.v, ss[cc].v, AF.Square)
                        sqs[cc] = sq
                    for cc in ccs:
                        ps2 = pp.next()
                        k.mm(ps2.v, ones_bf.v, sqs[cc].v)
                        ps2s[cc] = ps2
                    for cc in ccs:
                        r1 = rp.next()
                        k.act(r1.v, ps2s[cc].v, AF.Ln, bias=EPS)
                        r2 = rp.next()
                        k.act(r2.v, r1.v, AF.Exp, scale=-0.5)
                        r2s[cc] = r2
                for cc in ccs:
                    ob = obp.next()
                    hh = cc % 4
                    if grp == 0:
                        k.stt(ob.v, ss[cc].v, 128.0 ** -0.5, r2s[cc].v, ALU.mult, ALU.mult)
                        dst = GQ
                    elif grp == 1:
                        k.tt(ob.v, ss[cc].v, r2s[cc].v, ALU.mult, eng='pool')
                        dst = GK
                    else:
                        k.copy(ob.v, ss[cc].v, eng='pool')
                        dst = GV
                    k.dma(dst.tile(ti)[hh * 128:(hh + 1) * 128, :], ob.v, q='pool')
            pss = []
            for hh in range(4):
                ps = pp.next()
                c0 = 1536 + hh * 128
                for kc in range(8):
                    k.mm(ps.v, W[:, kc, c0:c0 + 128], hT[:, kc, :], start=(kc == 0), stop=(kc == 7))
                pss.append(ps)
            for hh in range(4):
                ob = obp.next()
                k.act(ob.v, pss[hh].v, AF.Silu)
                k.dma(GS.tile(ti)[hh * 128:(hh + 1) * 128, :], ob.v, q='pool')
            for grp in range(2):
                cs4 = list(range(4 * grp, 4 * grp + 4))
                pss, sqs, ps2s, r2s = {}, {}, {}, {}
                for c in cs4:
                    ps = pp.next()
                    c0 = 2056 + c * 128
                    for kc in range(8):
                        k.mm(ps.v, W[:, kc, c0:c0 + 128], hT[:, kc, :], start=(kc == 0), stop=(kc == 7))
                    pss[c] = ps
                for c in cs4:
                    sq = sqp.next()
                    k.act(sq.v, pss[c].v, AF.Square)
                    sqs[c] = sq
                for c in cs4:
                    ps2 = pp.next()
                    k.mm(ps2.v, ones64_bf.v, sqs[c].v)
                    ps2s[c] = ps2
                for c in cs4:
                    r1 = rp.next()
                    k.act(r1.v, ps2s[c].v, AF.Ln, scale=1.0 / 64, bias=EPS)
                    r2 = rp.next()
                    k.act(r2.v, r1.v, AF.Exp, scale=-0.5)
                    r2s[c] = r2
                for c in cs4:
                    ob = obp.next()
                    k.stt(ob.v, pss[c].v, (qg if c < 4 else kg)[:, 0:1], r2s[c].v, ALU.mult, ALU.mult)
                    dst = SQ if c < 4 else SK
                    cc = c % 4
                    k.dma(dst.tile(ti)[cc * 128:(cc + 1) * 128, :], ob.v, q='pool')
            for tb in range(4):
                ps = pp.next()
                for kc in range(8):
                    k.mm(ps.v, hT[:, kc, tb * 128:(tb + 1) * 128], W[:, kc, 3080:3592], start=(kc == 0), stop=(kc == 7))
                ob = obp.next()
                k.copy(ob.v, ps.v, eng='act' if tb % 2 else 'dve')
                k.dma(SV.tile(ti)[tb * 128:(tb + 1) * 128, :], ob.v, q='pool')
            ps = pp.next()
            for tb in range(4):
                for kc in range(8):
                    k.mm(ps[:, tb * 8:(tb + 1) * 8], hT[:, kc, tb * 128:(tb + 1) * 128], W[:, kc, 2048:2056],
                         start=(kc == 0), stop=(kc == 7))
            pv = ps[:, 0:32].re("p (t e) -> p t e", e=8)
            dtb_b = View(dtb, dtb.ap.unsqueeze(1).broadcast_to([128, 4, 4]))
            negA_b = View(negA, negA.ap.unsqueeze(1).broadcast_to([128, 4, 4]))
            z = smp.next()
            k.tt(z.v, pv[:, :, 0:4], dtb_b, ALU.add)
            e1 = smp.next()
            k.act(e1.v, z.v, AF.Exp)
            s1 = smp.next()
            k.act(s1.v, e1.v, AF.Ln, bias=1.0)
            gbo = gbp.next()
            k.tt(gbo[:, :, 0:4], s1.v, negA_b, ALU.mult)
            e2 = smp.next()
            k.act(e2.v, pv[:, :, 4:8], AF.Exp, scale=-1.0)
            d2 = smp.next()
            k.ts(d2.v, e2.v, 1.0, ALU.add)
            k.add('dve', lambda e, o=gbo.ap[:, :, 4:8], i_=d2.ap: e.reciprocal(o, i_), reads=[d2], writes=[gbo])
            k.dma(tm_view(GB, ti, 0, 8), gbo.v, q='pool')
        k.phase_end()

    def phase_gdn(l, i):
        k.phase_begin()
        pp = psum_pool(4)
        pc = psum_pool(2, name='psc')
        ppo = psum_pool(2, name='pso')
        gn = k.sb([128, 1], F32, 'gn')
        k.dma(gn.v, GNORM[i])
        ident4 = k.sb([128, 4, 128], F32, 'ident4')
        for h in range(4):
            k.copy(ident4[:, h, :], ident, eng='pool')
        S = k.sb([128, 4, 128], F32, 'S')
        k.memset(S.v, 0.0)
        mb1x4 = k.sb([128, 4, 128], F32, 'mb1x4')
        mb2x4 = k.sb([128, 4, 128], F32, 'mb2x4')
        for h in range(4):
            k.copy(mb1x4[:, h, :], C('mb1'), eng='pool')
            k.copy(mb2x4[:, h, :], C('mb2'), eng='pool')
        ldp = {n: sb_pool(2, [128, 4, 512], BF16, n) for n in ('kT', 'qT', 'vT', 'gsT')}
        gbp = sb_pool(2, [128, 4, 8], F32, 'gbt')
        osp = sb_pool(2, [128, 4, 512], BF16, 'ost')
        f4 = {n: sb_pool(2, [128, 4, 128], F32, n) for n in
              ('kdec', 'kw', 'vb', 'gbc', 'RAs', 'RBs', 'E', 'ET', 'EQ', 't1', 'A', 'Aqk', 'u', 'wT', 'qd', 'osb', 'o2', 'r1', 'r2')}
        f4r = {n: sb_pool(3, [128, 4, 128], F32, n) for n in ('Sx', 'STx', 'PT')}
        nbf = {n: sb_pool(3, [128, 4, 128], BF16, n + 'b') for n in ('Sx', 'STx', 'PTb')}
        sqp = sb_pool(2, [128, 4, 128], BF16, 'gsq')
        smp = {n: sb_pool(2, [128, w], F32, n) for n, w in (('cs', 16), ('ex', 16), ('ngc', 4), ('kws', 4), ('nb4', 4))}
        vnzp = [sb_pool(2, [128, 4, 128], F32, f'vnz{c}') for c in range(2)]
        for c in range(2):
            for b in vnzp[c].bufs:
                k.memset(b.v, 0.0)

        def f2(b):
            return b.v.re("p h d -> p (h d)")
        tctx = {}

        def tile_ctx(ti):
            if ti not in tctx:
                kT4 = ldp['kT'].next()
                k.dma(kT4.v, fm_view(GK, ti, 0, 4))
                qT4 = ldp['qT'].next()
                k.dma(qT4.v, fm_view(GQ, ti, 0, 4))
                vT4 = ldp['vT'].next()
                k.dma(vT4.v, fm_view(GV, ti, 0, 4))
                gs4 = ldp['gsT'].next()
                k.dma(gs4.v, fm_view(GS, ti, 0, 4))
                gbt = gbp.next()
                k.dma(gbt.v, tm_view(GB, ti, 0, 8))
                tctx[ti] = dict(kT4=kT4, qT4=qT4, vT4=vT4, gs4=gs4, gbt=gbt, ost=osp.next())
            return tctx[ti]

        def prep(ti, tb, out):
            c_ = tile_ctx(ti)
            kT4, qT4, vT4, gs4, gbt, ost = c_['kT4'], c_['qT4'], c_['vT4'], c_['gs4'], c_['gbt'], c_['ost']
            blk = slice(tb * 128, (tb + 1) * 128)
            g4 = gbt[:, tb, 0:4]
            b4 = gbt[:, tb, 4:8]
            cps = pp.next()
            k.mm(cps[:, 0:4], C('tri2'), g4)
            k.mm(cps[:, 4:8], C('triu2'), g4)
            k.mm(cps[:, 8:12], C('cind0'), g4)
            k.mm(cps[:, 12:16], C('cind1'), g4)
            cs = smp['cs'].next()
            k.copy(cs.v, cps[:, 0:16], eng='dve')
            ex = smp['ex'].next()
            k.act(ex.v, cps[:, 0:16], AF.Exp)
            ngc = smp['ngc'].next()
            k.ts(ngc.v, cs[:, 0:4], -1.0, ALU.mult, eng='pool')
            kws = smp['kws'].next()
            k.tt(kws.v, b4, ex[:, 0:4], ALU.mult, eng='pool')
            nb4 = smp['nb4'].next()
            k.ts(nb4.v, b4, -1.0, ALU.mult, eng='pool')
            yield
            trp = pp.next()
            tv = trp.v.bitcast(BF16)
            for h in range(4):
                k.transpose(tv[:, h * 128:(h + 1) * 128], kT4[:, h, blk], ident_bf.v)
            for h in range(4):
                k.transpose(tv[:, 512 + h * 128:512 + (h + 1) * 128], vT4[:, h, blk], ident_bf.v)
            kdec = f4['kdec'].next()
            kw = f4['kw'].next()
            vb = f4['vb'].next()
            for h in range(4):
                k.ts(kdec[:, h, :], tv[:, h * 128:(h + 1) * 128], ex[:, 4 + h:5 + h], ALU.mult)
                k.act(kw[:, h, :], tv[:, h * 128:(h + 1) * 128], AF.Identity, scale=kws[:, h:h + 1])
                k.act(vb[:, h, :], tv[:, 512 + h * 128:512 + (h + 1) * 128], AF.Identity, scale=b4[:, h:h + 1])
            yield
            gbc = f4['gbc'].next()
            for h in range(4):
                k.act(gbc[:, h, :], ones_f[:, 0:128], AF.Identity, scale=g4[:, h:h + 1])
            RC = pp.next()
            for h in range(4):
                hs = slice(h * 128, (h + 1) * 128)
                k.mm(RC[:, hs], gbc[:, h, :], C('tri2'))
            RAs = f4['RAs'].next()
            k.tt(f2(RAs), RC.v, f2(mb1x4), ALU.add)
            RBs = f4['RBs'].next()
            k.tt(f2(RBs), RC.v, f2(mb2x4), ALU.add)
            E = f4['E'].next()
            ET = f4['ET'].next()
            EQ = f4['EQ'].next()
            for h in range(4):
                k.act(E[:, h, :], RAs[:, h, :], AF.Exp, scale=-1.0, bias=cs[:, h:h + 1])
                k.act(ET[:, h, :], RBs[:, h, :], AF.Exp, bias=ngc[:, h:h + 1])
            k.act(f2(EQ), RC.v, AF.Exp)
            yield
            KK = pp.next()
            KQ = pp.next()
            for h in range(4):
                hs = slice(h * 128, (h + 1) * 128)
                k.mm(KK[:, hs], kT4[:, h, blk], kT4[:, h, blk])
                k.mm(KQ[:, hs], kT4[:, h, blk], qT4[:, h, blk])
            t1 = f4['t1'].next()
            k.tt(f2(t1), KK.v, f2(E), ALU.mult)
            A = f4['A'].next()
            for h in range(4):
                k.ts(A[:, h, :], t1[:, h, :], nb4[:, h:h + 1], ALU.mult, eng='dve')
            Aqk = f4['Aqk'].next()
            k.tt(f2(Aqk), KQ.v, f2(ET), ALU.mult)
            yield
            ATp = pp.next()
            for h in range(4):
                k.transpose(ATp[:, h * 128:(h + 1) * 128], A[:, h, :], ident)
            ST_ = f4r['STx'].next()
            k.copy(f2(ST_), ATp.v, eng='act')
            PT = f4r['PT'].next()
            k.tt(f2(PT), ATp.v, f2(ident4), ALU.add)
            S_ = A
            PTb = None
            for lev in range(1, 6):
                lo = lev >= 2
                Sp = pp.next()
                for h in range(4):
                    k.mm(Sp[:, h * 128:(h + 1) * 128], ST_[:, h, :], S_[:, h, :])
                Sn = nbf['Sx'].next()
                k.copy(f2(Sn), Sp.v, eng='act')
                if lev < 5:
                    STp = pp.next()
                    for h in range(4):
                        k.mm(STp[:, h * 128:(h + 1) * 128], S_[:, h, :], ST_[:, h, :])
                    STn = nbf['STx'].next()
                    k.copy(f2(STn), STp.v, eng='dve')
                Pp = pp.next()
                if lev == 1:
                    Sn32 = f4r['Sx'].next()
                    k.copy(f2(Sn32), Sp.v, eng='act')
                    for h in range(4):
                        k.mm(Pp[:, h * 128:(h + 1) * 128], Sn32[:, h, :], PT[:, h, :])
                else:
                    for h in range(4):
                        k.mm(Pp[:, h * 128:(h + 1) * 128], Sn[:, h, :], PTb[:, h, :])
                PTn = f4r['PT'].next()
                k.tt(f2(PTn), Pp.v, f2(PT), ALU.add)
                if lev < 5:
                    PTb = nbf['PTb'].next()
                    k.copy(f2(PTb), f2(PTn), eng='act' if lev % 2 else 'dve')
                PT = PTn
                S_ = Sn
                if lev < 5:
                    ST_ = STn
                yield
            yield
            up = pp.next()
            wp_ = pp.next()
            for h in range(4):
                hs = slice(h * 128, (h + 1) * 128)
                k.mm(up[:, hs], PT[:, h, :], vb[:, h, :])
                k.mm(wp_[:, hs], kw[:, h, :], PT[:, h, :])
            u = f4['u'].next()
            k.copy(f2(u), up.v, eng='act')
            wT = f4['wT'].next()
            k.copy(f2(wT), wp_.v, eng='dve')
            qd = f4['qd'].next()
            k.tt(qd.v, qT4[:, :, blk], EQ.v, ALU.mult, eng='pool')
            out.update(dict(kdec=kdec, wT=wT, u=u, qd=qd, Aqk=Aqk, ex=ex, blk=blk, gs4=gs4, ost=ost))
            yield

        def chain(ti, tb, o, pump):
            kdec, wT, u, qd, Aqk, ex, blk, gs4, ost = (o[x] for x in ('kdec', 'wT', 'u', 'qd', 'Aqk', 'ex', 'blk', 'gs4', 'ost'))
            oTp = ppo.next()
            for c in range(2):
                cs_ = slice(c * 64, (c + 1) * 64)
                vnz = vnzp[c].next()
                vnp = pc.next()
                for h in range(4):
                    k.mm(vnp[:, h * 128:(h + 1) * 128], wT[:, h, :], S[:, h, :])
                k.tt(vnz[cs_, :, :].re("p h d -> p (h d)"), u[cs_, :, :].re("p h d -> p (h d)"), vnp[cs_, :], ALU.subtract)
                pump(2)
                for h in range(4):
                    o_ = oTp[:, h * 128 + c * 64:h * 128 + (c + 1) * 64]
                    k.mm(o_, S[:, h, :], qd[:, h, cs_], start=True, stop=False)
                    k.mm(o_, vnz[:, h, :], Aqk[:, h, cs_], start=False, stop=True)
                dSp = pc.next()
                for h in range(4):
                    k.mm(dSp[:, h * 128:(h + 1) * 128], kdec[:, h, :], vnz[:, h, :])
                for h in range(4):
                    k.stt(S[:, h, :], S[:, h, :], ex[:, 8 + 4 * c + h:9 + 4 * c + h], dSp[:, h * 128:(h + 1) * 128],
                          ALU.mult, ALU.add)
                pump(3)
            osb = f4['osb'].next()
            k.copy(f2(osb), oTp.v, eng='dve')
            sq = sqp.next()
            k.act(f2(sq), oTp.v, AF.Square)
            ssp = pc.next()
            for h in range(4):
                k.mm(ssp[:, h * 128:(h + 1) * 128], ones_bf.v, sq[:, h, :])
            r1 = f4['r1'].next()
            k.act(f2(r1), ssp.v, AF.Ln, scale=1.0 / 128, bias=EPS)
            r2 = f4['r2'].next()
            k.act(f2(r2), f2(r1), AF.Exp, scale=-0.5)
            o2 = f4['o2'].next()
            k.tt(f2(o2), f2(osb), f2(r2), ALU.mult)
            k.stt(ost[:, :, blk], o2.v, gn[:, 0:1], gs4[:, :, blk], ALU.mult, ALU.mult)
            if tb == 3:
                k.dma(fm_view(OMIX, ti, 0, 4), ost.v, q='pool')

        blocks = [(ti, tb) for ti in range(NTT) for tb in range(4)]
        outs = [dict() for _ in blocks]
        g0 = prep(blocks[0][0], blocks[0][1], outs[0])
        for _ in g0:
            pass
        for bi, (ti, tb) in enumerate(blocks):
            nxt = prep(blocks[bi + 1][0], blocks[bi + 1][1], outs[bi + 1]) if bi + 1 < len(blocks) else None

            def pump(n, nxt=nxt):
                if nxt is None:
                    return
                for _ in range(n):
                    try:
                        next(nxt)
                    except StopIteration:
                        return
            chain(ti, tb, outs[bi], pump)
            if nxt is not None:
                for _ in nxt:
                    pass
            outs[bi].clear()
        k.phase_end()

    def phase_sb(l, i):
        k.phase_begin()
        Wk = SB_W
        zp = psum_pool(2, (128, Wk), F32, 'z')
        atp = psum_pool(2, (128, Wk), BF16, 'aT')
        op_ = psum_pool(2, (128, 512), F32, 'oT')
        qp = sb_pool(2, [64, T], BF16, 'qh')
        kp = sb_pool(2, [64, T], BF16, 'kh')
        vp = sb_pool(2, [128, NT, 64], BF16, 'vh')
        ohp = sb_pool(2, [64, T], BF16, 'oh')
        ep = sb_pool(4, [128, Wk], F32, 'e')
        spp = sb_pool(3, [128, Wk], F32, 'sp')
        gp = sb_pool(3, [128, Wk + 1], F32, 'G')
        for b_ in gp.bufs:
            k.memset(b_[:, 0:1], 0.0)
        pp_ = sb_pool(2, [128, Wk], BF16, 'p')
        ap_ = sb_pool(4, [128, Wk], BF16, 'a')
        atsp = sb_pool(3, [128, Wk], BF16, 'aTs')
        bp = sb_pool(10, [128, 1], F32, 'bias')
        mkc = k.sb([64, 3, 128], BF16, 'mkc')
        k.copy(mkc[:, 0, :], C('identS')[0:64, :], eng='dve')
        k.copy(mkc[:, 1, :], C('mk1')[0:64, :], eng='dve')
        k.copy(mkc[:, 2, :], C('mk2')[0:64, :], eng='dve')
        tiles = []
        for h in range(8):
            for qb in range(NT):
                t0 = qb * 128
                nkt = (t0 + 128 + Wk - 1) // Wk
                for kt in reversed(range(nkt)):
                    tiles.append(dict(h=h, qb=qb, t0=t0, k0=kt * Wk, w=min(Wk, t0 + 128 - kt * Wk),
                                      diag=(kt == nkt - 1), lastq=(kt == 0), idx=len(tiles)))
        heads = {}

        def load_head(h):
            if h in heads or h >= 8:
                return
            qh = qp.next()
            k.dma(qh.v, SQ.whole(SQ.full[h * 64:(h + 1) * 64, :]), reads=SQ.tiles)
            kh = kp.next()
            k.dma(kh.v, SK.whole(SK.full[h * 64:(h + 1) * 64, :]), reads=SK.tiles)
            vh = vp.next()
            vstep = min(8, NT)
            for n0 in range(0, NT, vstep):
                k.dma(vh[:, n0:n0 + vstep, :],
                      SV.whole(SV.full[n0 * 128:(n0 + vstep) * 128, h * 64:(h + 1) * 64].rearrange("(n p) d -> p n d", p=128)),
                      reads=SV.tiles)
            heads[h] = dict(qh=qh, kh=kh, vh=vh, oh=ohp.next())

        def S1(t):
            h = t['h']
            if h not in heads:
                load_head(h)
            hd = heads[h]
            w, k0, t0 = t['w'], t['k0'], t['t0']
            z = zp.next()
            wm = w - 128 if t['diag'] else w
            for c0 in range(0, wm, 512):
                cw = min(512, wm - c0)
                k.mm(z[:, c0:c0 + cw], hd['qh'][:, t0:t0 + 128], hd['kh'][:, k0 + c0:k0 + c0 + cw])
            if t['diag']:
                k.mm(z[:, wm:w], hd['qh'][:, t0:t0 + 128], hd['kh'][:, k0 + wm:k0 + w], start=True, stop=False)
                k.mm(z[:, wm:w], ident_bf[0:64, :], mkc[:, 1, :], start=False, stop=False)
                k.mm(z[:, wm:w], mkc[:, 0, :], mkc[:, 2, :], start=False, stop=True)
            e = ep.next()
            k.act(e[:, 0:w], z[:, 0:w], AF.Exp)
            sp = spp.next()
            k.act(sp[:, 0:w], e[:, 0:w], AF.Ln, bias=1.0)
            t['e'] = e
            t['sp'] = sp

        def S2(t):
            w = t['w']
            G = gp.next()
            k.scan(G[:, 1:w + 1], ones_f[:, 0:w], t['sp'][:, 0:w], 0.0, ALU.mult, ALU.add)
            t['G'] = G

        def S3(t):
            w = t['w']
            G = t['G']
            bias = bp.next()
            if t['diag']:
                k.act(bias.v, G[:, w:w + 1], AF.Identity, scale=-1.0)
            else:
                k.act(bias.v, G[:, w:w + 1], AF.Identity, scale=-1.0, bias=tiles[t['idx'] - 1]['bias'][:, 0:1])
            t['bias'] = bias
            p = pp_.next()
            k.act(p[:, 0:w], G[:, 0:w], AF.Exp, bias=bias[:, 0:1])
            a = ap_.next()
            k.tt(a[:, 0:w], t['e'][:, 0:w], p[:, 0:w], ALU.mult, eng='pool')
            t['a'] = a

        def S4(t):
            w = t['w']
            aT = atp.next()
            for sb in range(w // 128):
                k.transpose(aT[:, sb * 128:(sb + 1) * 128], t['a'][:, sb * 128:(sb + 1) * 128], ident_bf.v)
            aTs = atsp.next()
            k.copy(aTs[:, 0:w], aT[:, 0:w], eng='dve')
            t['aTs'] = aTs

        def S5(t):
            w, k0, t0 = t['w'], t['k0'], t['t0']
            hd = heads[t['h']]
            if t['diag'] and t['qb'] == 0:
                load_head(t['h'] + 1)
            if t['diag']:
                t['oT'] = op_.next()
            else:
                t['oT'] = tiles[t['idx'] - 1]['oT']
            oT = t['oT']
            nsb = w // 128
            for sb in range(nsb):
                k.mm(oT[0:64, 0:128], hd['vh'][:, k0 // 128 + sb, :], t['aTs'][:, sb * 128:(sb + 1) * 128],
                     start=(t['diag'] and sb == 0), stop=(t['lastq'] and sb == nsb - 1))
            if t['lastq']:
                k.copy(hd['oh'][:, t0:t0 + 128], oT[0:64, 0:128], eng='act')
                if t['qb'] == NT - 1:
                    h = t['h']
                    k.dma(OMIX.whole(OMIX.full[512 + h * 64:512 + (h + 1) * 64, :]), hd['oh'].v, q='sp', writes=OMIX.tiles)
            for key in ('e', 'sp', 'G', 'a', 'aTs'):
                t.pop(key, None)

        n = len(tiles)
        for s_ in range(n + 6):
            if 0 <= s_ - 4 < n:
                S4(tiles[s_ - 4])
            if 0 <= s_ - 2 < n:
                S3(tiles[s_ - 2])
            if 0 <= s_ - 1 < n:
                S2(tiles[s_ - 1])
            if s_ < n:
                S1(tiles[s_])
            if 0 <= s_ - 5 < n:
                S5(tiles[s_ - 5])
        k.phase_end()

    def phase_p1_ret(l, i, xsrc):
        k.phase_begin()
        pp = psum_pool(8)
        W = k.sb([128, 8, RET_IN], BF16, 'Wret')
        for kc in range(8):
            k.dma(W[:, kc, :], RWIb[i][kc * 128:(kc + 1) * 128, :])
        xp = sb_pool(1, [128, 8, 512], F32, 'xt')
        hp = sb_pool(1, [128, 8, 512], BF16, 'hT')
        csp = sb_pool(2, [128, 2, 512], F32, 'cs')
        tp = sb_pool(4, [128, 512], F32, 'rt')
        qst = sb_pool(2, [128, 8, 512], BF16, 'qst')
        kst = qst
        gst = sb_pool(1, [128, 8, 512], BF16, 'gst')
        vst = sb_pool(1, [128, 2, 2048], BF16, 'vst')
        norm_tile = make_norm(pp)
        for ti in range(NTT):
            xt = xp.next()
            k.dma(xt.v, fm_view(xsrc, ti, 0, 8))
            cs = csp.next()
            k.dma(cs.v, fm_view(ROPE, ti, 0, 2))
            hT = hp.next()
            norm_tile(xt, A_m[l], modc[l][:, 0:8], hT)
            cos = cs[:, 0, :]
            sin = cs[:, 1, :]
            for which in range(2):
                st = (qst if which == 0 else kst).next()
                for h in range(4):
                    c0 = which * 1024 + h * 256
                    p1 = pp.next()
                    p2 = pp.next()
                    for kc in range(8):
                        k.mm(p1.v, W[:, kc, c0:c0 + 128], hT[:, kc, :], start=(kc == 0), stop=(kc == 7))
                    for kc in range(8):
                        k.mm(p2.v, W[:, kc, c0 + 128:c0 + 256], hT[:, kc, :], start=(kc == 0), stop=(kc == 7))
                    sc = 1.0 if which == 0 else 1.0 / 16.0
                    t1 = tp.next()
                    t2 = tp.next()
                    t3 = tp.next()
                    t4 = tp.next()
                    k.stt(t1.v, p1.v, sc, cos, ALU.mult, ALU.mult)
                    k.stt(t2.v, p2.v, sc, sin, ALU.mult, ALU.mult)
                    k.stt(t3.v, p1.v, sc, sin, ALU.mult, ALU.mult)
                    k.stt(t4.v, p2.v, sc, cos, ALU.mult, ALU.mult)
                    k.tt(st[:, 2 * h, :], t1.v, t2.v, ALU.subtract, eng='pool')
                    k.tt(st[:, 2 * h + 1, :], t3.v, t4.v, ALU.add, eng='pool')
                k.dma(fm_view(RQ if which == 0 else RK, ti, 0, 8), st.v, q='pool')
            for half in range(2):
                g_ = gst.next()
                for cl in range(8):
                    c = half * 8 + cl
                    ps = pp.next()
                    c0 = 4096 + c * 128
                    for kc in range(8):
                        k.mm(ps.v, W[:, kc, c0:c0 + 128], hT[:, kc, :], start=(kc == 0), stop=(kc == 7))
                    k.act(g_[:, cl, :], ps.v, AF.Silu)
                k.dma(fm_view(RG, ti, half * 1024, 8), g_.v, q='pool')
            for half in range(2):
                v_ = vst.next()
                for tl in range(2):
                    tb = half * 2 + tl
                    for nb in range(4):
                        ps = pp.next()
                        c0 = 2048 + nb * 512
                        for kc in range(8):
                            k.mm(ps.v, hT[:, kc, tb * 128:(tb + 1) * 128], W[:, kc, c0:c0 + 512], start=(kc == 0), stop=(kc == 7))
                        k.copy(v_[:, tl, nb * 512:(nb + 1) * 512], ps.v, eng='dve' if nb % 2 else 'act')
                bt = RV.tile(ti)
                k.dma(View(bt, bt.ap[half * 256:(half + 1) * 256, :].rearrange("(t p) c -> p t c", p=128)), v_.v, q='pool')
        k.phase_end()

    def phase_ret(l, i):
        k.phase_begin()
        pt4p = psum_pool(1, (128, 512), F32, 'pt4')
        trpp = psum_pool(1, (128, 512), F32, 'trp')
        otp = psum_pool(4, (128, 512), F32, 'oTp')
        pp2 = psum_pool(1, (128, 1024), F32, 'dS')
        Sst = [k.sb([128, 1024], F32, f'S{h}') for h in range(4)]
        Sb = [k.sb([128, 2, 512], BF16, f'Sb{h}') for h in range(4)]
        for h in range(4):
            k.memset(Sst[h].v, 0.0)
            k.memset(Sb[h].v, 0.0, eng='dve')
        qp = sb_pool(2, [128, 8, 512], BF16, 'q8')
        kp = sb_pool(2, [128, 8, 512], BF16, 'k8')
        vp = sb_pool(2, [128, 4, 2048], BF16, 'v4')
        gp = sb_pool(1, [128, 16, 512], BF16, 'g16')
        osp = sb_pool(2, [128, 16, 512], BF16, 'o16')
        P4p = sb_pool(2, [128, 4, 128], BF16, 'P4')
        kd4p = sb_pool(2, [128, 4, 256], BF16, 'kd4')
        qd4p = sb_pool(2, [128, 8, 128], BF16, 'qd4')
        sqp = sb_pool(2, [128, 4, 512], BF16, 'rsq')
        rp = sb_pool(4, [128, 512], F32, 'rr')
        o2p = sb_pool(4, [128, 4, 128], F32, 'o2')
        kds = C('kds')
        for ti in range(NTT):
            q8 = qp.next()
            k.dma(q8.v, fm_view(RQ, ti, 0, 8))
            k8 = kp.next()
            k.dma(k8.v, fm_view(RK, ti, 0, 8))
            v4 = vp.next()
            k.dma(v4.v, tm_view(RV, ti, 0, 2048))
            g16 = gp.next()
            k.dma(g16.v, fm_view(RG, ti, 0, 16))
            ost = osp.next()
            for tb in range(4):
                blk = slice(tb * 128, (tb + 1) * 128)
                PT4 = pt4p.next()
                trp = trpp.next()
                tv = trp.v.bitcast(BF16)
                for h in range(4):
                    hs = slice(h * 128, (h + 1) * 128)
                    k.mm(PT4[:, hs], k8[:, 2 * h, blk], q8[:, 2 * h, blk], start=True, stop=False)
                    k.mm(PT4[:, hs], k8[:, 2 * h + 1, blk], q8[:, 2 * h + 1, blk], start=False, stop=True)
                for h in range(4):
                    for d in range(2):
                        k.transpose(tv[:, h * 256 + d * 128:h * 256 + (d + 1) * 128], k8[:, 2 * h + d, blk], ident_bf.v)
                P4 = P4p.next()
                kd4 = kd4p.next()
                qd4 = qd4p.next()
                for h in range(4):
                    k.tt(P4[:, h, :], PT4[:, h * 128:(h + 1) * 128], C(f'DT{h}'), ALU.mult)
                for h in range(4):
                    k.ts(kd4[:, h, :], tv[:, h * 256:(h + 1) * 256], kds[:, h:h + 1], ALU.mult)
                for h in range(4):
                    for d in range(2):
                        k.tt(qd4[:, 2 * h + d, :], q8[:, 2 * h + d, blk], C(f'QD{h}'), ALU.mult, eng='pool')
                oTps = []
                for h in range(4):
                    oTp = otp.next()
                    for dvc in range(4):
                        o_ = oTp[:, dvc * 128:(dvc + 1) * 128]
                        k.mm(o_, v4[:, tb, h * 512 + dvc * 128:h * 512 + (dvc + 1) * 128], P4[:, h, :], start=True, stop=False)
                        k.mm(o_, Sb[h][:, 0, dvc * 128:(dvc + 1) * 128], qd4[:, 2 * h, :], start=False, stop=False)
                        k.mm(o_, Sb[h][:, 1, dvc * 128:(dvc + 1) * 128], qd4[:, 2 * h + 1, :], start=False, stop=True)
                    oTps.append(oTp)
                for h in range(4):
                    dSp = pp2.next()
                    for d in range(2):
                        k.mm(dSp[:, d * 512:(d + 1) * 512], kd4[:, h, d * 128:(d + 1) * 128], v4[:, tb, h * 512:(h + 1) * 512])
                    k.stt(Sst[h].v, Sst[h].v, float(np.exp(RET_LG[h] * 128.0)), dSp.v, ALU.mult, ALU.add)
                    k.copy(Sb[h].v.re("p a b -> p (a b)"), Sst[h].v, eng='act')
                sq = sqp.next()
                for h in range(4):
                    k.act(sq[:, h, :], oTps[h].v, AF.Square)
                ssp = pt4p.next()
                for h in range(4):
                    for dvc in range(4):
                        k.mm(ssp[:, h * 128:(h + 1) * 128], ones_bf.v, sq[:, h, dvc * 128:(dvc + 1) * 128], start=(dvc == 0), stop=(dvc == 3))
                r1 = rp.next()
                k.act(r1.v, ssp.v, AF.Ln, scale=1.0 / 512, bias=EPS)
                r2 = rp.next()
                k.act(r2.v, r1.v, AF.Exp, scale=-0.5)
                for h in range(4):
                    o2 = o2p.next()
                    r2b = View(r2, r2.ap[:, h * 128:(h + 1) * 128].unsqueeze(1).broadcast_to([128, 4, 128]))
                    k.tt(o2.v, oTps[h].v.re("p (c t) -> p c t", c=4), r2b, ALU.mult)
                    k.tt(ost[:, 4 * h:4 * h + 4, blk], o2.v, g16[:, 4 * h:4 * h + 4, blk], ALU.mult, eng='pool')
            k.dma(fm_view(RO, ti, 0, 16), ost.v, q='pool')
        k.phase_end()

    def phase_out(l, i, hyb, xsrc):
        k.phase_begin()
        pp = psum_pool(4)
        if hyb:
            wA = k.sb([128, 8, D], BF16, 'wA')
            for c in range(0, 8, 4):
                k.dma(wA[:, c:c + 4, :], View(HWOb[i], HWOb[i].ap[c * 128:(c + 4) * 128, :].rearrange("(c p) n -> p c n", p=128)))
            oap = sb_pool(2, [128, 8, 512], BF16, 'oa')
        else:
            wR = k.sb([128, 16, D], BF16, 'wR')
            for c in range(0, 16, 4):
                k.dma(wR[:, c:c + 4, :], View(RWOb[i], RWOb[i].ap[c * 128:(c + 4) * 128, :].rearrange("(c p) n -> p c n", p=128)))
            oap = sb_pool(2, [128, 16, 512], BF16, 'oa')
        xp = sb_pool(2, [128, 8, 512], F32, 'xt')
        gt = modc[l][:, 16:24]
        for ti in range(NTT):
            xt = xp.next()
            k.dma(xt.v, fm_view(xsrc, ti, 0, 8))
            oa = oap.next()
            if hyb:
                k.dma(oa.v, fm_view(OMIX, ti, 0, 8))
            else:
                k.dma(oa.v, fm_view(RO, ti, 0, 16))
            for dc in range(8):
                ps = pp.next()
                ds_ = slice(dc * 128, (dc + 1) * 128)
                if hyb:
                    for c in range(8):
                        k.mm(ps.v, wA[:, c, ds_], oa[:, c, :], start=(c == 0), stop=(c == 7))
                else:
                    for c in range(16):
                        k.mm(ps.v, wR[:, c, ds_], oa[:, c, :], start=(c == 0), stop=(c == 15))
                k.stt(xt[:, dc, :], ps.v, gt[:, dc:dc + 1], xt[:, dc, :], ALU.mult, ALU.add)
            k.dma(fm_view(XT, ti, 0, 8), xt.v, q='pool')
        k.phase_end()

    def phase_ffn(l, xdst):
        k.phase_begin()
        pp = psum_pool(8)
        W1 = k.sb([128, 8, 2 * FFN_H], BF16, 'W1')
        for kc in range(8):
            k.dma(W1[:, kc, :], FWIb[l][kc * 128:(kc + 1) * 128, :])
        w2p = sb_pool(2, [128, 22, 128], BF16, 'w2')
        xp = sb_pool(2, [128, 8, 512], F32, 'xt')
        hp = sb_pool(2, [128, 8, 512], BF16, 'hT')
        ap_ = sb_pool(1, [128, 22, 512], BF16, 'actT')
        sp_ = sb_pool(3, [128, 512], F32, 'sl')
        norm_tile = make_norm(pp)
        gt = modc[l][:, 40:48]
        for ti in range(NTT):
            xt = xp.next()
            k.dma(xt.v, fm_view(XT, ti, 0, 8))
            hT = hp.next()
            norm_tile(xt, A_f[l], modc[l][:, 24:32], hT)
            aT = ap_.next()
            for j in range(22):
                pg = pp.next()
                pu = pp.next()
                for kc in range(8):
                    k.mm(pg.v, W1[:, kc, j * 128:(j + 1) * 128], hT[:, kc, :], start=(kc == 0), stop=(kc == 7))
                for kc in range(8):
                    k.mm(pu.v, W1[:, kc, FFN_H + j * 128:FFN_H + (j + 1) * 128], hT[:, kc, :], start=(kc == 0), stop=(kc == 7))
                s = sp_.next()
                k.act(s.v, pg.v, AF.Silu)
                k.tt(aT[:, j, :], s.v, pu.v, ALU.mult)
            for dc in range(8):
                w2 = w2p.next()
                k.dma(w2.v, View(FWOb[l], FWOb[l].ap[dc]))
                ps = pp.next()
                for j in range(22):
                    k.mm(ps.v, w2[:, j, :], aT[:, j, :], start=(j == 0), stop=(j == 21))
                k.stt(xt[:, dc, :], ps.v, gt[:, dc:dc + 1], xt[:, dc, :], ALU.mult, ALU.add)
            k.dma(fm_view(xdst, ti, 0, 8), xt.v, q='pool')
        k.phase_end()

    if want('cast'):
        phase_cast()
    if want('ada'):
        phase_ada()
    for l in range(n_layers):
        i = l // 2
        xsrc = XT_in if l == 0 else XT
        xdst = OUT if l == n_layers - 1 else XT
        if l % 2 == 0:
            if want(f'p1_{l}'):
                phase_p1_hyb(l, i, xsrc)
            if want(f'gdn_{l}'):
                phase_gdn(l, i)
            if want(f'sb_{l}'):
                phase_sb(l, i)
            if want(f'out_{l}'):
                phase_out(l, i, True, xsrc)
        else:
            if want(f'p1_{l}'):
                phase_p1_ret(l, i, xsrc)
            if want(f'ret_{l}'):
                phase_ret(l, i)
            if want(f'out_{l}'):
                phase_out(l, i, False, xsrc)
        if want(f'ffn_{l}'):
            phase_ffn(l, xdst)
    k.barrier()
    k.finalize()
    return nc, k


def host_inputs(b, T, x, c, ada_w, ada_b, norm_mix, norm_ffn, hyb_w_in, hyb_conv, gdn_a_log, gdn_dt_bias,
                gdn_norm, sb_q_norm, sb_k_norm, hyb_w_out, ret_w_in, ret_w_out, ffn_w_in, ffn_w_out, shared):
    f = np.float32
    m = dict(shared)
    m["xT"] = np.ascontiguousarray(x[b, :T].T)
    m["c_col"] = np.ascontiguousarray(c[b].reshape(8, 128).T)
    return m


def host_shared(T, ada_w, ada_b, norm_mix, norm_ffn, hyb_w_in, hyb_conv, gdn_a_log, gdn_dt_bias,
                gdn_norm, sb_q_norm, sb_k_norm, hyb_w_out, ret_w_in, ret_w_out, ffn_w_in, ffn_w_out):
    ca = np.ascontiguousarray
    m = {}
    m["ada_w"] = ca(ada_w)
    m["ada_b_col"] = ca(ada_b.reshape(4, 48, 128).transpose(0, 2, 1))
    m["norm_mix_col"] = ca(norm_mix.reshape(4, 8, 128).transpose(0, 2, 1))
    m["norm_ffn_col"] = ca(norm_ffn.reshape(4, 8, 128).transpose(0, 2, 1))
    m["hyb_w_in"] = ca(hyb_w_in)
    m["conv_col"] = ca(hyb_conv.reshape(2, 4, 12, 128).transpose(0, 3, 2, 1))
    m["a_log_b"] = ca(np.broadcast_to(gdn_a_log[:, None, :], (2, 128, 4)))
    m["dt_bias_b"] = ca(np.broadcast_to(gdn_dt_bias[:, None, :], (2, 128, 4)))
    m["gdn_norm_col"] = ca(gdn_norm.reshape(2, 128, 1))
    m["sb_q_norm_col"] = ca(np.concatenate([sb_q_norm, sb_q_norm], axis=1).reshape(2, 128, 1))
    m["sb_k_norm_col"] = ca(np.concatenate([sb_k_norm, sb_k_norm], axis=1).reshape(2, 128, 1))
    m["hyb_w_out"] = ca(hyb_w_out)
    m["ret_w_in"] = ca(ret_w_in)
    m["ret_w_out"] = ca(ret_w_out)
    m["ffn_w_in"] = ca(ffn_w_in)
    m["ffn_w_out"] = ca(ffn_w_out)
    m["cst"] = make_consts()
    m["rope"] = ca(make_rope(T).reshape(256, T))
    return m


_CACHE = {}


def kernel(x, c, ada_w, ada_b, norm_mix, norm_ffn, hyb_w_in, hyb_conv, gdn_a_log, gdn_dt_bias,
           gdn_norm, sb_q_norm, sb_k_norm, hyb_w_out, ret_w_in, ret_w_out, ffn_w_in, ffn_w_out):
    args = [np.asarray(a, dtype=np.float32) for a in
            (x, c, ada_w, ada_b, norm_mix, norm_ffn, hyb_w_in, hyb_conv, gdn_a_log, gdn_dt_bias,
             gdn_norm, sb_q_norm, sb_k_norm, hyb_w_out, ret_w_in, ret_w_out, ffn_w_in, ffn_w_out)]
    x, c = args[0], args[1]
    B, T, _ = x.shape
    shared = host_shared(T, *args[2:])
    in_maps = [host_inputs(b, T, *args, shared) for b in range(B)]
    if T not in _CACHE:
        _CACHE[T] = build(T)[0]
    nc = _CACHE[T]
    res = run_bass_kernel_spmd(nc, in_maps, core_ids=list(range(B)))
    out = np.stack([np.asarray(r["outT"]).T for r in res.results], axis=0)
    return np.ascontiguousarray(out.astype(np.float32))
```

```python
import numpy as np
import concourse.bass as bass
import concourse.mybir as mybir

F32 = mybir.dt.float32
BF16 = mybir.dt.bfloat16
AF = mybir.ActivationFunctionType
ALU = mybir.AluOpType
AX = mybir.AxisListType

EPOCH = 30000
COMPUTE = ('pe', 'act', 'dve', 'pool')
ALLQ = ('pe', 'act', 'dve', 'pool', 'sp')


class Buf:
    __slots__ = ('ap', 'name', 'last_w', 'readers', 'sem', 'cnt', 'space')

    def __init__(self, ap, name='', space='sb'):
        self.ap = ap
        self.name = name
        self.last_w = None
        self.readers = []
        self.sem = None
        self.cnt = 0
        self.space = space

    def __getitem__(self, idx):
        return View(self, self.ap[idx])

    @property
    def v(self):
        return View(self, self.ap)


class View:
    __slots__ = ('buf', 'ap')

    def __init__(self, buf, ap):
        self.buf = buf
        self.ap = ap

    def __getitem__(self, idx):
        return View(self.buf, self.ap[idx])

    def re(self, pat, **kw):
        return View(self.buf, self.ap.rearrange(pat, **kw))

    def bitcast(self, dt):
        return View(self.buf, self.ap.bitcast(dt))


class Op:
    __slots__ = ('eng', 'fn', 'deps', 'signal', 'n', 'is_dma', 'slot', 'val', 'ndma', 'tag', 'sem')

    def __init__(self, eng, fn, is_dma=False, slot=None, ndma=0, tag=''):
        self.eng = eng
        self.fn = fn
        self.deps = ()
        self.signal = False
        self.n = -1
        self.is_dma = is_dma
        self.slot = slot
        self.val = 0
        self.ndma = ndma
        self.tag = tag


def _bufs(xs):
    out = []
    for x in xs:
        if x is None:
            continue
        if isinstance(x, View):
            out.append(x.buf)
        elif isinstance(x, Buf):
            out.append(x)
        elif isinstance(x, (list, tuple)):
            out.extend(_bufs(x))
    return out


class Kern:
    def __init__(self, nc):
        self.nc = nc
        self.ops = {q: [] for q in ALLQ}
        self.all_ops = []
        self.sb_off = 0
        self.sb_names = 0
        self.last_dma_by_slot = {}
        import contextlib
        self.stacks = [contextlib.ExitStack()]
        self.free_sems = {}
        self.marks = []
        self.phase_slots = [[]]

    def sb(self, shape, dtype, name=None):
        self.sb_names += 1
        nm = f"sb{self.sb_names}_{name or ''}"
        cm = self.nc.sbuf_tensor(nm, list(shape), dtype)
        t = self.stacks[-1].enter_context(cm)
        return Buf(t.ap(), nm, 'sb')

    def phase_begin(self):
        import contextlib
        self.stacks.append(contextlib.ExitStack())
        self.phase_slots.append([])

    def phase_end(self):
        self.marks.append({q: sum(1 for o in self.ops[q] if o.fn is not None and not o.is_dma) for q in COMPUTE})
        self.barrier()
        self.stacks.pop().close()
        for slot, q in self.phase_slots.pop():
            self.free_sems.setdefault(q, []).append((slot.sem[q], slot.cnt[q]))

    def add(self, eng, fn, reads=(), writes=(), is_dma=False, slot=None, ndma=0, tag=''):
        op = Op(eng, fn, is_dma, slot, ndma, tag)
        R = _bufs(reads)
        W = _bufs(writes)
        deps = set()
        for b in R:
            if b.last_w is not None:
                deps.add(b.last_w)
            if b.space == 'ps':
                for r in b.readers:
                    if r.eng != eng:
                        deps.add(r)
        for b in W:
            if b.last_w is not None:
                deps.add(b.last_w)
            deps.update(b.readers)
        deps.discard(op)
        op.deps = tuple(deps)
        for b in W:
            b.last_w = op
            b.readers = []
        for b in R:
            if b in W:
                continue
            if not is_dma:
                b.readers = [r for r in b.readers if r.is_dma or r.eng != eng]
            b.readers.append(op)
        if is_dma:
            slot.cnt[eng] += ndma
            op.val = 16 * slot.cnt[eng]
            op.sem = slot.sem[eng]
            assert op.val < 60000, f"dma sem overflow on {slot.name}"
            self.last_dma_by_slot[(id(slot), eng)] = op
        self.ops[eng].append(op)
        self.all_ops.append(op)
        return op

    def barrier(self):
        lasts = []
        for q in COMPUTE:
            for o in reversed(self.ops[q]):
                if not o.is_dma and o.fn is not None:
                    lasts.append(o)
                    break
        lasts.extend(self.last_dma_by_slot.values())
        self.last_dma_by_slot = {}
        for q in ALLQ:
            op = Op(q, None)
            op.deps = tuple(lasts)
            self.ops[q].append(op)
            self.all_ops.append(op)

    def mm(self, out, lhsT, rhs, start=True, stop=True, **kw):
        o, l, r = out.ap, lhsT.ap, rhs.ap
        return self.add('pe', lambda e: e.matmul(o, l, r, start=start, stop=stop, **kw),
                        reads=[lhsT, rhs], writes=[out])

    def transpose(self, out, in_, ident):
        o, i, d = out.ap, in_.ap, ident.ap
        return self.add('pe', lambda e: e.transpose(o, i, d), reads=[in_, ident], writes=[out])

    def act(self, out, in_, func, bias=None, scale=None, accum_out=None, eng='act'):
        kw = {}
        reads = [in_]
        if bias is not None:
            if isinstance(bias, View):
                kw['bias'] = bias.ap
                reads.append(bias)
            else:
                kw['bias'] = bias
        if scale is not None:
            if isinstance(scale, View):
                kw['scale'] = scale.ap
                reads.append(scale)
            else:
                kw['scale'] = scale
        writes = [out]
        if accum_out is not None:
            kw['accum_out'] = accum_out.ap
            writes.append(accum_out)
        o, i = out.ap, in_.ap
        return self.add('act', lambda e: e.activation(o, i, func, **kw), reads=reads, writes=writes)

    def tt(self, out, in0, in1, op, eng='dve'):
        o, a, b = out.ap, in0.ap, in1.ap
        return self.add(eng, lambda e: e.tensor_tensor(o, a, b, op), reads=[in0, in1], writes=[out])

    def ts(self, out, in0, s1, op0, s2=None, op1=None, eng='dve', accum_out=None):
        reads = [in0]
        a1 = s1
        if isinstance(s1, View):
            a1 = s1.ap
            reads.append(s1)
        a2 = s2
        if isinstance(s2, View):
            a2 = s2.ap
            reads.append(s2)
        o, i = out.ap, in0.ap
        kw = {}
        writes = [out]
        if accum_out is not None:
            kw['accum_out'] = accum_out.ap
            writes.append(accum_out)
        if op1 is None:
            return self.add(eng, lambda e: e.tensor_scalar(o, i, a1, None, op0, **kw), reads=reads, writes=writes)
        return self.add(eng, lambda e: e.tensor_scalar(o, i, a1, a2, op0, op1, **kw), reads=reads, writes=writes)

    def stt(self, out, in0, scalar, in1, op0, op1, eng='dve'):
        reads = [in0, in1]
        sc = scalar
        if isinstance(scalar, View):
            sc = scalar.ap
            reads.append(scalar)
        o, a, b = out.ap, in0.ap, in1.ap
        return self.add(eng, lambda e: e.scalar_tensor_tensor(o, a, sc, b, op0, op1), reads=reads, writes=[out])

    def scan(self, out, d0, d1, initial, op0, op1):
        reads = [d0, d1]
        ini = initial
        if isinstance(initial, View):
            ini = initial.ap
            reads.append(initial)
        o, a, b = out.ap, d0.ap, d1.ap
        return self.add('dve', lambda e: e.tensor_tensor_scan(o, a, b, ini, op0, op1), reads=reads, writes=[out])

    def copy(self, out, in_, eng='dve'):
        o, i = out.ap, in_.ap
        if eng == 'act':
            return self.add('act', lambda e: e.copy(o, i), reads=[in_], writes=[out])
        return self.add(eng, lambda e: e.tensor_copy(o, i), reads=[in_], writes=[out])

    def memset(self, out, val, eng='pool'):
        o = out.ap
        return self.add(eng, lambda e: e.memset(o, val), reads=[], writes=[out])

    def dma(self, out, in_, q='sp', slot=None, reads=None, writes=None, **kw):
        outs = out if isinstance(out, (list, tuple)) else [out]
        ins = in_ if isinstance(in_, (list, tuple)) else [in_]
        pairs = [(o.ap, i.ap) for o, i in zip(outs, ins)]
        if slot is None:
            slot = outs[0].buf if outs[0].buf.space == 'sb' else ins[0].buf
        if slot.sem is None:
            slot.sem = {}
            slot.cnt = {}
        if q not in slot.sem:
            if self.free_sems.get(q):
                slot.sem[q], slot.cnt[q] = self.free_sems[q].pop()
            else:
                slot.sem[q] = self.new_sem()
                slot.cnt[q] = 0
            self.phase_slots[-1].append((slot, q))
        sem_ = slot.sem[q]

        def fn(e, pairs=pairs, sem=sem_, kw=kw):
            last = None
            for (o, i) in pairs:
                last = e.dma_start(out=o, in_=i, **kw).then_inc(sem, 16)
            return None
        op = self.add(q, fn, reads=ins if reads is None else reads,
                      writes=outs if writes is None else writes,
                      is_dma=True, slot=slot, ndma=len(pairs))
        return op

    def new_sem(self):
        self._nsem = getattr(self, '_nsem', 0) + 1
        cm = self.nc.semaphore(f"s{self._nsem}")
        sem = cm.__enter__()
        self._sem_cms = getattr(self, '_sem_cms', [])
        self._sem_cms.append(cm)
        return sem

    def finalize(self):
        nc = self.nc
        for op in self.all_ops:
            for d in op.deps:
                if d.is_dma:
                    continue
                if d.eng == 'pe' and op.eng == 'pe' and not op.is_dma:
                    continue
                d.signal = True
        eng_sems = {}
        for q in ALLQ:
            n = 0
            for op in self.ops[q]:
                if op.signal and not op.is_dma:
                    op.n = n
                    n += 1
            eng_sems[q] = [self.new_sem() for _ in range((n + EPOCH - 1) // EPOCH)]
        self.n_instr = {q: len(self.ops[q]) for q in ALLQ}

        def emit(q, e):
            waited_eng = {}
            waited_dma = {}
            for op in self.ops[q]:
                need_eng = {}
                need_dma = {}
                for d in op.deps:
                    if d.is_dma:
                        k = d.sem
                        if waited_dma.get(id(k), 0) < d.val:
                            if need_dma.get(id(k), (None, 0))[1] < d.val:
                                need_dma[id(k)] = (k, d.val)
                    else:
                        if d.eng == 'pe' and q == 'pe' and not op.is_dma:
                            continue
                        if waited_eng.get(d.eng, -1) < d.n:
                            if need_eng.get(d.eng, -1) < d.n:
                                need_eng[d.eng] = d.n
                for pe_, n in need_eng.items():
                    e.wait_ge(eng_sems[pe_][n // EPOCH], n % EPOCH + 1)
                    waited_eng[pe_] = n
                for _, (k, v) in need_dma.items():
                    e.wait_ge(k, v)
                    waited_dma[id(k)] = v
                if op.fn is None:
                    continue
                ins = op.fn(e)
                if op.signal and not op.is_dma:
                    ins.then_inc(eng_sems[q][op.n // EPOCH], 1)

        with nc.Block() as block:
            @block.tensor
            def _(e):
                emit('pe', e)

            @block.scalar
            def _(e):
                emit('act', e)

            @block.vector
            def _(e):
                emit('dve', e)

            @block.gpsimd
            def _(e):
                emit('pool', e)

            @block.sync
            def _(e):
                emit('sp', e)

from concourse.bass_utils import run_bass_kernel_spmd

D = 1024
EPS = 1e-6
FFN_H = 2816
HYB_IN = 3592
RET_IN = 6144
BIG = 30000.0
RET_LG = [float(np.log1p(-np.exp2(-5.0 - h))) for h in range(4)]

CST = {}
_off = 0
for _n, _w in [('ident', 128), ('tri2', 128), ('triu2', 128), ('cind0', 128), ('cind1', 128),
               ('mb1', 128), ('mb2', 128), ('mlow', 128), ('ones64', 128),
               ('DT0', 128), ('DT1', 128), ('DT2', 128), ('DT3', 128),
               ('QD0', 128), ('QD1', 128), ('QD2', 128), ('QD3', 128), ('kds', 4),
               ('identS', 128), ('mk1', 128), ('mk2', 128)]:
    CST[_n] = (_off, _w)
    _off += _w
NCST = _off


def make_consts():
    c = np.zeros((128, NCST), np.float64)
    i = np.arange(128)
    same = (i[:, None] // 64) == (i[None, :] // 64)

    def put(n, a):
        o, w = CST[n]
        c[:, o:o + w] = a
    put('ident', np.eye(128))
    put('tri2', (same & (i[:, None] <= i[None, :])) * 1.0)
    put('triu2', (same & (i[:, None] > i[None, :])) * 1.0)
    put('cind0', np.broadcast_to((i < 64)[:, None] * 1.0, (128, 128)))
    put('cind1', np.broadcast_to((i >= 64)[:, None] * 1.0, (128, 128)))
    put('mb1', np.where(same & (i[:, None] > i[None, :]), 0.0, BIG))
    put('mb2', np.where(same & (i[None, :] >= i[:, None]), 0.0, -BIG))
    put('mlow', (i[None, :] < i[:, None]) * 1.0)
    put('ones64', same * 1.0)
    put('identS', (i[None, :] == i[:, None] + 64) * 1.0)
    put('mk1', np.where(i[None, :] >= i[:, None], -BIG, 0.0))
    put('mk2', np.where(i[None, :] >= i[:, None] + 64, -BIG, 0.0))
    for h in range(4):
        lg = RET_LG[h]
        dif = i[None, :] - i[:, None]
        put(f'DT{h}', np.where(dif >= 0, np.exp(lg * np.maximum(dif, 0)), 0.0))
        put(f'QD{h}', np.broadcast_to(np.exp(lg * (i + 1.0))[None, :], (128, 128)))
        o, w = CST['kds']
        c[:, o + h] = np.exp(lg * (127.0 - i))
    return c.astype(np.float32)


def make_rope(T):
    inv = 1.0 / (10000.0 ** (np.arange(0, 256, 2, dtype=np.float64) / 256.0))
    ang = inv[:, None] * np.arange(T, dtype=np.float64)[None, :]
    return np.stack([np.cos(ang), np.sin(ang)]).astype(np.float32)


class Rot:
    def __init__(self, mk, n):
        self.bufs = [mk(i) for i in range(n)]
        self.i = 0

    def next(self):
        b = self.bufs[self.i % len(self.bufs)]
        self.i += 1
        return b


class DT_:
    def __init__(self, nc, name, rows, T, dt, fm=True, kind="Internal"):
        shape = [rows, T] if fm else [T, rows]
        self.full = nc.dram_tensor(name, shape, dt, kind=kind).ap()
        self.fm = fm
        self.tiles = []
        for i in range(T // 512):
            ap = self.full[:, i * 512:(i + 1) * 512] if fm else self.full[i * 512:(i + 1) * 512, :]
            self.tiles.append(Buf(ap, f"{name}_t{i}", 'dram'))

    def tile(self, i):
        return self.tiles[i]

    def whole(self, ap):
        return View(self.tiles[0], ap)


GDN_STAGE = 99
SB_W = 1024


def build(T, debug=False, phases=None, n_layers=4):
    nc = bass.Bass("TRN2", target_bir_lowering=False)
    k = Kern(nc)
    NT = T // 128
    NTT = T // 512
    skind = "ExternalOutput" if debug else "Internal"

    def want(p):
        return phases is None or p in phases

    def din(name, shape, dt=F32, used=True):
        return Buf(nc.dram_tensor(name, list(shape), dt, kind="ExternalInput" if used else "Internal").ap(), name, 'dram')

    XT_in = DT_(nc, "xT", D, T, F32, True, "ExternalInput")
    C_in = din("c_col", [128, 8])
    ADAW = din("ada_w", [4, D, 6 * D], used=want("ada"))
    ADAB = din("ada_b_col", [4, 128, 48])
    NMIX = din("norm_mix_col", [4, 128, 8])
    NFFN = din("norm_ffn_col", [4, 128, 8])
    HWI = din("hyb_w_in", [2, D, HYB_IN], used=want("cast"))
    CONV = din("conv_col", [2, 128, 12, 4])
    ALOG = din("a_log_b", [2, 128, 4])
    DTB = din("dt_bias_b", [2, 128, 4])
    GNORM = din("gdn_norm_col", [2, 128, 1])
    QN = din("sb_q_norm_col", [2, 128, 1])
    KN = din("sb_k_norm_col", [2, 128, 1])
    HWO = din("hyb_w_out", [2, D, D], used=want("cast"))
    RWI = din("ret_w_in", [2, D, RET_IN], used=want("cast"))
    RWO = din("ret_w_out", [2, 2048, D], used=want("cast"))
    FWI = din("ffn_w_in", [4, D, 2 * FFN_H], used=want("cast"))
    FWO = din("ffn_w_out", [4, FFN_H, D], used=want("cast"))
    CSTD = din("cst", [128, NCST])
    ROPE = DT_(nc, "rope", 256, T, F32, True, "ExternalInput")
    OUT = DT_(nc, "outT", D, T, F32, True, "ExternalOutput")

    def dscr(name, shape, dt):
        return Buf(nc.dram_tensor(name, list(shape), dt, kind=skind).ap(), name, 'dram')
    HWIb = [dscr(f"hwib{i}", [D, HYB_IN], BF16) for i in range(2)]
    HWOb = [dscr(f"hwob{i}", [D, D], BF16) for i in range(2)]
    RWIb = [dscr(f"rwib{i}", [D, RET_IN], BF16) for i in range(2)]
    RWOb = [dscr(f"rwob{i}", [2048, D], BF16) for i in range(2)]
    FWIb = [dscr(f"fwib{i}", [D, 2 * FFN_H], BF16) for i in range(4)]
    FWOb = [dscr(f"fwob{i}", [8, 128, 22, 128], BF16) for i in range(4)]
    XT = DT_(nc, "xres", D, T, F32, True, skind)
    GQ = DT_(nc, "gq", 512, T, BF16, True, skind)
    GK = DT_(nc, "gk", 512, T, BF16, True, skind)
    GV = DT_(nc, "gv", 512, T, BF16, True, skind)
    GS = DT_(nc, "gs", 512, T, BF16, True, skind)
    GB = DT_(nc, "gbeta", 8, T, F32, False, skind)
    SQ = DT_(nc, "sq", 512, T, BF16, True, skind)
    SK = DT_(nc, "sk", 512, T, BF16, True, skind)
    SV = DT_(nc, "sv", 512, T, BF16, False, skind)
    OMIX = DT_(nc, "omix", 1024, T, BF16, True, skind)
    RQ = DT_(nc, "rq", 1024, T, BF16, True, skind)
    RK = DT_(nc, "rk", 1024, T, BF16, True, skind)
    RV = DT_(nc, "rv", 2048, T, BF16, False, skind)
    RG = DT_(nc, "rg", 2048, T, BF16, True, skind)
    RO = DT_(nc, "ro", 2048, T, BF16, True, skind)

    cst = k.sb([128, NCST], F32, 'cst')
    k.dma(cst.v, CSTD.v)

    def C(n):
        o, w = CST[n]
        return cst[:, o:o + w]
    ident = C('ident')
    ones_f = k.sb([128, 1024], F32, 'ones_f')
    k.memset(ones_f.v, 1.0)
    ones_bf = k.sb([128, 128], BF16, 'ones_bf')
    k.memset(ones_bf.v, 1.0)
    ident_bf = k.sb([128, 128], BF16, 'ident_bf')
    k.copy(ident_bf.v, ident, eng='dve')
    ones64_bf = k.sb([128, 128], BF16, 'ones64_bf')
    k.copy(ones64_bf.v, C('ones64'), eng='dve')
    modc = [k.sb([128, 48], F32, f'modc{l}') for l in range(4)]
    A_m = [k.sb([128, 8], F32, f'Am{l}') for l in range(4)]
    A_f = [k.sb([128, 8], F32, f'Af{l}') for l in range(4)]

    def psum_pool(n, shape=(128, 512), dt=F32, name='ps'):
        def mk(i):
            cm = nc.psum_tensor(f"{name}{i}_{k.sb_names}", list(shape), dt)
            k.sb_names += 1
            t = k.stacks[-1].enter_context(cm)
            return Buf(t.ap(), f"{name}{i}", 'ps')
        return Rot(mk, n)

    def sb_pool(n, shape, dt, name):
        return Rot(lambda i: k.sb(shape, dt, f"{name}{i}"), n)

    act_rr = [0]

    def phase_cast():
        k.phase_begin()
        CW = 2048
        fp = sb_pool(3, [128, CW], F32, 'cf')
        bp = sb_pool(3, [128, CW], BF16, 'cb')
        engs = ['pool', 'act', 'dve']
        cnt = 0
        jobs = []
        for i in range(2):
            if n_layers > 2 * i:
                jobs += [(HWI[i], HWIb[i]), (HWO[i], HWOb[i])]
            if n_layers > 2 * i + 1:
                jobs += [(RWI[i], RWIb[i]), (RWO[i], RWOb[i])]
        for l in range(n_layers):
            jobs += [(FWI[l], FWIb[l]), (FWO[l], FWOb[l])]
        for src, dst in jobs:
            if len(dst.ap.shape) == 4:
                for r in range(22):
                    f = fp.next()
                    b = bp.next()
                    k.dma(f[:, 0:D], src[r * 128:(r + 1) * 128, :])
                    k.copy(b[:, 0:D], f[:, 0:D], eng=engs[cnt % 3])
                    cnt += 1
                    k.dma(View(dst, dst.ap[:, :, r, :].rearrange("c p n -> p c n")),
                          View(b, b.ap[:, 0:D].rearrange("p (c n) -> p c n", c=8)), q='pool')
                continue
            K_, N_ = dst.ap.shape
            for r in range(K_ // 128):
                for c0 in range(0, N_, CW):
                    w = min(CW, N_ - c0)
                    f = fp.next()
                    b = bp.next()
                    k.dma(f[:, 0:w], src[r * 128:(r + 1) * 128, c0:c0 + w])
                    k.copy(b[:, 0:w], f[:, 0:w], eng=engs[cnt % 3])
                    cnt += 1
                    k.dma(dst[r * 128:(r + 1) * 128, c0:c0 + w], b[:, 0:w], q='pool')
        k.phase_end()

    def phase_ada():
        k.phase_begin()
        pp = psum_pool(2)
        ccol = k.sb([128, 8], F32, 'ccol')
        k.dma(ccol.v, C_in.v)
        cact = k.sb([128, 8], F32, 'cact')
        k.act(cact.v, ccol.v, AF.Silu)
        wp = sb_pool(2, [128, 8, 512], F32, 'adaw')
        for l in range(n_layers):
            ps = pp.next()
            for nch in range(12):
                w = wp.next()
                k.dma([w[:, kc, :] for kc in range(8)],
                      [ADAW[l, kc * 128:(kc + 1) * 128, nch * 512:(nch + 1) * 512] for kc in range(8)])
                for jj in range(4):
                    j = nch * 4 + jj
                    for kc in range(8):
                        k.mm(ps[:, j:j + 1], w[:, kc, jj * 128:(jj + 1) * 128], cact[:, kc:kc + 1],
                             start=(kc == 0), stop=(kc == 7))
            bcol = k.sb([128, 48], F32, 'bcol')
            k.dma(bcol.v, ADAB[l])
            k.tt(modc[l].v, ps[:, 0:48], bcol.v, ALU.add)
            for (A, NG, c0) in ((A_m[l], NMIX, 8), (A_f[l], NFFN, 32)):
                g = k.sb([128, 8], F32, 'g')
                k.dma(g.v, NG[l])
                t = k.sb([128, 8], F32, 't')
                k.ts(t.v, modc[l][:, c0:c0 + 8], 1.0, ALU.add)
                k.tt(A.v, t.v, g.v, ALU.mult)
        k.phase_end()

    def make_norm(pp):
        sqp = sb_pool(2, [128, 512], BF16, 'nsq')
        rp = sb_pool(2, [128, 512], F32, 'nr')
        tp = sb_pool(2, [128, 512], F32, 'nt')

        def norm_tile(xt, A, sh, hT):
            ps = pp.next()
            for kc in range(8):
                sq = sqp.next()
                k.act(sq.v, xt[:, kc, :], AF.Square)
                k.mm(ps.v, ones_bf.v, sq.v, start=(kc == 0), stop=(kc == 7))
            r1 = rp.next()
            k.act(r1.v, ps.v, AF.Ln, scale=1.0 / D, bias=EPS)
            rstd = rp.next()
            k.act(rstd.v, r1.v, AF.Exp, scale=-0.5)
            for kc in range(8):
                t = tp.next()
                k.stt(t.v, xt[:, kc, :], A[:, kc:kc + 1], rstd.v, ALU.mult, ALU.mult)
                k.act(hT[:, kc, :], t.v, AF.Identity, bias=sh[:, kc:kc + 1])
        return norm_tile

    def fm_view(dt, ti, r0, nch, p=128):
        b = dt.tile(ti)
        return View(b, b.ap[r0:r0 + nch * p, :].rearrange("(c p) t -> p c t", p=p))

    def tm_view(dt, ti, c0, c1):
        b = dt.tile(ti)
        return View(b, b.ap[:, c0:c1].rearrange("(t p) c -> p t c", p=128))

    def phase_p1_hyb(l, i, xsrc):
        k.phase_begin()
        pp = psum_pool(8)
        W = k.sb([128, 8, HYB_IN], BF16, 'Whyb')
        for kc in range(8):
            k.dma(W[:, kc, :], HWIb[i][kc * 128:(kc + 1) * 128, :])
        conv = k.sb([128, 12, 4], F32, 'conv')
        k.dma(conv.v, CONV[i])
        dtb = k.sb([128, 4], F32, 'dtb')
        k.dma(dtb.v, DTB[i])
        alog = k.sb([128, 4], F32, 'alog')
        k.dma(alog.v, ALOG[i])
        negA = k.sb([128, 4], F32, 'negA')
        k.act(negA.v, alog.v, AF.Exp)
        k.ts(negA.v, negA.v, -1.0, ALU.mult, eng='pool')
        qg = k.sb([128, 1], F32, 'qg')
        k.dma(qg.v, QN[i])
        k.ts(qg.v, qg.v, 0.125, ALU.mult, eng='pool')
        kg = k.sb([128, 1], F32, 'kg')
        k.dma(kg.v, KN[i])
        halo = k.sb([128, 12, 3], F32, 'halo')
        k.memset(halo.v, 0.0)
        xp = sb_pool(2, [128, 8, 512], F32, 'xt')
        hp = sb_pool(2, [128, 8, 512], BF16, 'hT')
        rawp = sb_pool(5, [128, 515], F32, 'raw')
        yp = sb_pool(5, [128, 512], F32, 'y')
        sp_ = sb_pool(5, [128, 512], F32, 's')
        sqp = sb_pool(5, [128, 512], BF16, 'sq1')
        rp = sb_pool(9, [128, 512], F32, 'r')
        obp = sb_pool(8, [128, 512], BF16, 'ob')
        gbp = sb_pool(2, [128, 4, 8], F32, 'gbo')
        smp = sb_pool(6, [128, 4, 4], F32, 'sm')
        norm_tile = make_norm(pp)
        for ti in range(NTT):
            xt = xp.next()
            k.dma(xt.v, fm_view(xsrc, ti, 0, 8))
            hT = hp.next()
            norm_tile(xt, A_m[l], modc[l][:, 0:8], hT)
            for grp in range(3):
                ccs = list(range(4 * grp, 4 * grp + 4))
                raws, ys, ss, obs = {}, {}, {}, {}
                for cc in ccs:
                    ps = pp.next()
                    for kc in range(8):
                        k.mm(ps.v, W[:, kc, cc * 128:(cc + 1) * 128], hT[:, kc, :], start=(kc == 0), stop=(kc == 7))
                    raw = rawp.next()
                    k.copy(raw[:, 0:3], halo[:, cc, :], eng='pool')
                    k.copy(raw[:, 3:515], ps.v, eng='act')
                    k.copy(halo[:, cc, :], raw[:, 512:515], eng='pool')
                    raws[cc] = raw
                for cc in ccs:
                    raw = raws[cc]
                    y = yp.next()
                    k.ts(y.v, raw[:, 0:512], conv[:, cc, 0:1], ALU.mult)
                    for j in range(1, 4):
                        k.stt(y.v, raw[:, j:j + 512], conv[:, cc, j:j + 1], y.v, ALU.mult, ALU.add)
                    ys[cc] = y
                for cc in ccs:
                    s_ = sp_.next()
                    k.act(s_.v, ys[cc].v, AF.Silu)
                    ss[cc] = s_
                if grp < 2:
                    sqs, ps2s, r2s = {}, {}, {}
                    for cc in ccs:
                        sq = sqp.next()
                        k.act(sq.v, ss[cc].v, AF.Square)
                        sqs[cc] = sq
                    for cc in ccs:
                        ps2 = pp.next()
                        k.mm(ps2.v, ones_bf.v, sqs[cc].v)
                        ps2s[cc] = ps2
                    for cc in ccs:
                        r1 = rp.next()
                        k.act(r1.v, ps2s[cc].v, AF.Ln, bias=EPS)
                        r2 = rp.next()
                        k.act(r2.v, r1.v, AF.Exp, scale=-0.5)
                        r2s[cc] = r2
                for cc in ccs:
                    ob = obp.next()
                    hh = cc % 4
                    if grp == 0:
                        k.stt(ob.v, ss[cc].v, 128.0 ** -0.5, r2s[cc].v, ALU.mult, ALU.mult)
                        dst = GQ
                    elif grp == 1:
                        k.tt(ob.v, ss[cc].v, r2s[cc].v, ALU.mult, eng='pool')
                        dst = GK
                    else:
                        k.copy(ob.v, ss[cc].v, eng='pool')
                        dst = GV
                    k.dma(dst.tile(ti)[hh * 128:(hh + 1) * 128, :], ob.v, q='pool')
            pss = []
            for hh in range(4):
                ps = pp.next()
                c0 = 1536 + hh * 128
                for kc in range(8):
                    k.mm(ps.v, W[:, kc, c0:c0 + 128], hT[:, kc, :], start=(kc == 0), stop=(kc == 7))
                pss.append(ps)
            for hh in range(4):
                ob = obp.next()
                k.act(ob.v, pss[hh].v, AF.Silu)
                k.dma(GS.tile(ti)[hh * 128:(hh + 1) * 128, :], ob.v, q='pool')
            for grp in range(2):
                cs4 = list(range(4 * grp, 4 * grp + 4))
                pss, sqs, ps2s, r2s = {}, {}, {}, {}
                for c in cs4:
                    ps = pp.next()
                    c0 = 2056 + c * 128
                    for kc in range(8):
                        k.mm(ps.v, W[:, kc, c0:c0 + 128], hT[:, kc, :], start=(kc == 0), stop=(kc == 7))
                    pss[c] = ps
                for c in cs4:
                    sq = sqp.next()
                    k.act(sq.v, pss[c].v, AF.Square)
                    sqs[c] = sq
                for c in cs4:
                    ps2 = pp.next()
                    k.mm(ps2.v, ones64_bf.v, sqs[c].v)
                    ps2s[c] = ps2
                for c in cs4:
                    r1 = rp.next()
                    k.act(r1.v, ps2s[c].v, AF.Ln, scale=1.0 / 64, bias=EPS)
                    r2 = rp.next()
                    k.act(r2.v, r1.v, AF.Exp, scale=-0.5)
                    r2s[c] = r2
                for c in cs4:
                    ob = obp.next()
                    k.stt(ob.v, pss[c].v, (qg if c < 4 else kg)[:, 0:1], r2s[c].v, ALU.mult, ALU.mult)
                    dst = SQ if c < 4 else SK
                    cc = c % 4
                    k.dma(dst.tile(ti)[cc * 128:(cc + 1) * 128, :], ob.v, q='pool')
            for tb in range(4):
                ps = pp.next()
                for kc in range(8):
                    k.mm(ps.v, hT[:, kc, tb * 128:(tb + 1) * 128], W[:, kc, 3080:3592], start=(kc == 0), stop=(kc == 7))
                ob = obp.next()
                k.copy(ob.v, ps.v, eng='act' if tb % 2 else 'dve')
                k.dma(SV.tile(ti)[tb * 128:(tb + 1) * 128, :], ob.v, q='pool')
            ps = pp.next()
            for tb in range(4):
                for kc in range(8):
                    k.mm(ps[:, tb * 8:(tb + 1) * 8], hT[:, kc, tb * 128:(tb + 1) * 128], W[:, kc, 2048:2056],
                         start=(kc == 0), stop=(kc == 7))
            pv = ps[:, 0:32].re("p (t e) -> p t e", e=8)
            dtb_b = View(dtb, dtb.ap.unsqueeze(1).broadcast_to([128, 4, 4]))
            negA_b = View(negA, negA.ap.unsqueeze(1).broadcast_to([128, 4, 4]))
            z = smp.next()
            k.tt(z.v, pv[:, :, 0:4], dtb_b, ALU.add)
            e1 = smp.next()
            k.act(e1.v, z.v, AF.Exp)
            s1 = smp.next()
            k.act(s1.v, e1.v, AF.Ln, bias=1.0)
            gbo = gbp.next()
            k.tt(gbo[:, :, 0:4], s1.v, negA_b, ALU.mult)
            e2 = smp.next()
            k.act(e2.v, pv[:, :, 4:8], AF.Exp, scale=-1.0)
            d2 = smp.next()
            k.ts(d2.v, e2.v, 1.0, ALU.add)
            k.add('dve', lambda e, o=gbo.ap[:, :, 4:8], i_=d2.ap: e.reciprocal(o, i_), reads=[d2], writes=[gbo])
            k.dma(tm_view(GB, ti, 0, 8), gbo.v, q='pool')
        k.phase_end()

    def phase_gdn(l, i):
        k.phase_begin()
        pp = psum_pool(4)
        pc = psum_pool(2, name='psc')
        ppo = psum_pool(2, name='pso')
        gn = k.sb([128, 1], F32, 'gn')
        k.dma(gn.v, GNORM[i])
        ident4 = k.sb([128, 4, 128], F32, 'ident4')
        for h in range(4):
            k.copy(ident4[:, h, :], ident, eng='pool')
        S = k.sb([128, 4, 128], F32, 'S')
        k.memset(S.v, 0.0)
        mb1x4 = k.sb([128, 4, 128], F32, 'mb1x4')
        mb2x4 = k.sb([128, 4, 128], F32, 'mb2x4')
        for h in range(4):
            k.copy(mb1x4[:, h, :], C('mb1'), eng='pool')
            k.copy(mb2x4[:, h, :], C('mb2'), eng='pool')
        ldp = {n: sb_pool(2, [128, 4, 512], BF16, n) for n in ('kT', 'qT', 'vT', 'gsT')}
        gbp = sb_pool(2, [128, 4, 8], F32, 'gbt')
        osp = sb_pool(2, [128, 4, 512], BF16, 'ost')
        f4 = {n: sb_pool(2, [128, 4, 128], F32, n) for n in
              ('kdec', 'kw', 'vb', 'gbc', 'RAs', 'RBs', 'E', 'ET', 'EQ', 't1', 'A', 'Aqk', 'u', 'wT', 'qd', 'osb', 'o2', 'r1', 'r2')}
        f4r = {n: sb_pool(3, [128, 4, 128], F32, n) for n in ('Sx', 'STx', 'PT')}
        nbf = {n: sb_pool(3, [128, 4, 128], BF16, n + 'b') for n in ('Sx', 'STx', 'PTb')}
        sqp = sb_pool(2, [128, 4, 128], BF16, 'gsq')
        smp = {n: sb_pool(2, [128, w], F32, n) for n, w in (('cs', 16), ('ex', 16), ('ngc', 4), ('kws', 4), ('nb4', 4))}
        vnzp = [sb_pool(2, [128, 4, 128], F32, f'vnz{c}') for c in range(2)]
        for c in range(2):
            for b in vnzp[c].bufs:
                k.memset(b.v, 0.0)

        def f2(b):
            return b.v.re("p h d -> p (h d)")
        tctx = {}

        def tile_ctx(ti):
            if ti not in tctx:
                kT4 = ldp['kT'].next()
                k.dma(kT4.v, fm_view(GK, ti, 0, 4))
                qT4 = ldp['qT'].next()
                k.dma(qT4.v, fm_view(GQ, ti, 0, 4))
                vT4 = ldp['vT'].next()
                k.dma(vT4.v, fm_view(GV, ti, 0, 4))
                gs4 = ldp['gsT'].next()
                k.dma(gs4.v, fm_view(GS, ti, 0, 4))
                gbt = gbp.next()
                k.dma(gbt.v, tm_view(GB, ti, 0, 8))
                tctx[ti] = dict(kT4=kT4, qT4=qT4, vT4=vT4, gs4=gs4, gbt=gbt, ost=osp.next())
            return tctx[ti]

        def prep(ti, tb, out):
            c_ = tile_ctx(ti)
            kT4, qT4, vT4, gs4, gbt, ost = c_['kT4'], c_['qT4'], c_['vT4'], c_['gs4'], c_['gbt'], c_['ost']
            blk = slice(tb * 128, (tb + 1) * 128)
            g4 = gbt[:, tb, 0:4]
            b4 = gbt[:, tb, 4:8]
            cps = pp.next()
            k.mm(cps[:, 0:4], C('tri2'), g4)
            k.mm(cps[:, 4:8], C('triu2'), g4)
            k.mm(cps[:, 8:12], C('cind0'), g4)
            k.mm(cps[:, 12:16], C('cind1'), g4)
            cs = smp['cs'].next()
            k.copy(cs.v, cps[:, 0:16], eng='dve')
            ex = smp['ex'].next()
            k.act(ex.v, cps[:, 0:16], AF.Exp)
            ngc = smp['ngc'].next()
            k.ts(ngc.v, cs[:, 0:4], -1.0, ALU.mult, eng='pool')
            kws = smp['kws'].next()
            k.tt(kws.v, b4, ex[:, 0:4], ALU.mult, eng='pool')
            nb4 = smp['nb4'].next()
            k.ts(nb4.v, b4, -1.0, ALU.mult, eng='pool')
            yield
            trp = pp.next()
            tv = trp.v.bitcast(BF16)
            for h in range(4):
                k.transpose(tv[:, h * 128:(h + 1) * 128], kT4[:, h, blk], ident_bf.v)
            for h in range(4):
                k.transpose(tv[:, 512 + h * 128:512 + (h + 1) * 128], vT4[:, h, blk], ident_bf.v)
            kdec = f4['kdec'].next()
            kw = f4['kw'].next()
            vb = f4['vb'].next()
            for h in range(4):
                k.ts(kdec[:, h, :], tv[:, h * 128:(h + 1) * 128], ex[:, 4 + h:5 + h], ALU.mult)
                k.act(kw[:, h, :], tv[:, h * 128:(h + 1) * 128], AF.Identity, scale=kws[:, h:h + 1])
                k.act(vb[:, h, :], tv[:, 512 + h * 128:512 + (h + 1) * 128], AF.Identity, scale=b4[:, h:h + 1])
            yield
            gbc = f4['gbc'].next()
            for h in range(4):
                k.act(gbc[:, h, :], ones_f[:, 0:128], AF.Identity, scale=g4[:, h:h + 1])
            RC = pp.next()
            for h in range(4):
                hs = slice(h * 128, (h + 1) * 128)
                k.mm(RC[:, hs], gbc[:, h, :], C('tri2'))
            RAs = f4['RAs'].next()
            k.tt(f2(RAs), RC.v, f2(mb1x4), ALU.add)
            RBs = f4['RBs'].next()
            k.tt(f2(RBs), RC.v, f2(mb2x4), ALU.add)
            E = f4['E'].next()
            ET = f4['ET'].next()
            EQ = f4['EQ'].next()
            for h in range(4):
                k.act(E[:, h, :], RAs[:, h, :], AF.Exp, scale=-1.0, bias=cs[:, h:h + 1])
                k.act(ET[:, h, :], RBs[:, h, :], AF.Exp, bias=ngc[:, h:h + 1])
            k.act(f2(EQ), RC.v, AF.Exp)
            yield
            KK = pp.next()
            KQ = pp.next()
            for h in range(4):
                hs = slice(h * 128, (h + 1) * 128)
                k.mm(KK[:, hs], kT4[:, h, blk], kT4[:, h, blk])
                k.mm(KQ[:, hs], kT4[:, h, blk], qT4[:, h, blk])
            t1 = f4['t1'].next()
            k.tt(f2(t1), KK.v, f2(E), ALU.mult)
            A = f4['A'].next()
            for h in range(4):
                k.ts(A[:, h, :], t1[:, h, :], nb4[:, h:h + 1], ALU.mult, eng='dve')
            Aqk = f4['Aqk'].next()
            k.tt(f2(Aqk), KQ.v, f2(ET), ALU.mult)
            yield
            ATp = pp.next()
            for h in range(4):
                k.transpose(ATp[:, h * 128:(h + 1) * 128], A[:, h, :], ident)
            ST_ = f4r['STx'].next()
            k.copy(f2(ST_), ATp.v, eng='act')
            PT = f4r['PT'].next()
            k.tt(f2(PT), ATp.v, f2(ident4), ALU.add)
            S_ = A
            PTb = None
            for lev in range(1, 6):
                lo = lev >= 2
                Sp = pp.next()
                for h in range(4):
                    k.mm(Sp[:, h * 128:(h + 1) * 128], ST_[:, h, :], S_[:, h, :])
                Sn = nbf['Sx'].next()
                k.copy(f2(Sn), Sp.v, eng='act')
                if lev < 5:
                    STp = pp.next()
                    for h in range(4):
                        k.mm(STp[:, h * 128:(h + 1) * 128], S_[:, h, :], ST_[:, h, :])
                    STn = nbf['STx'].next()
                    k.copy(f2(STn), STp.v, eng='dve')
                Pp = pp.next()
                if lev == 1:
                    Sn32 = f4r['Sx'].next()
                    k.copy(f2(Sn32), Sp.v, eng='act')
                    for h in range(4):
                        k.mm(Pp[:, h * 128:(h + 1) * 128], Sn32[:, h, :], PT[:, h, :])
                else:
                    for h in range(4):
                        k.mm(Pp[:, h * 128:(h + 1) * 128], Sn[:, h, :], PTb[:, h, :])
                PTn = f4r['PT'].next()
                k.tt(f2(PTn), Pp.v, f2(PT), ALU.add)
                if lev < 5:
                    PTb = nbf['PTb'].next()
                    k.copy(f2(PTb), f2(PTn), eng='act' if lev % 2 else 'dve')
                PT = PTn
                S_ = Sn
                if lev < 5:
                    ST_ = STn
                yield
            yield
            up = pp.next()
            wp_ = pp.next()
            for h in range(4):
                hs = slice(h * 128, (h + 1) * 128)
                k.mm(up[:, hs], PT[:, h, :], vb[:, h, :])
                k.mm(wp_[:, hs], kw[:, h, :], PT[:, h, :])
            u = f4['u'].next()
            k.copy(f2(u), up.v, eng='act')
            wT = f4['wT'].next()
            k.copy(f2(wT), wp_.v, eng='dve')
            qd = f4['qd'].next()
            k.tt(qd.v, qT4[:, :, blk], EQ.v, ALU.mult, eng='pool')
            out.update(dict(kdec=kdec, wT=wT, u=u, qd=qd, Aqk=Aqk, ex=ex, blk=blk, gs4=gs4, ost=ost))
            yield

        def chain(ti, tb, o, pump):
            kdec, wT, u, qd, Aqk, ex, blk, gs4, ost = (o[x] for x in ('kdec', 'wT', 'u', 'qd', 'Aqk', 'ex', 'blk', 'gs4', 'ost'))
            oTp = ppo.next()
            for c in range(2):
                cs_ = slice(c * 64, (c + 1) * 64)
                vnz = vnzp[c].next()
                vnp = pc.next()
                for h in range(4):
                    k.mm(vnp[:, h * 128:(h + 1) * 128], wT[:, h, :], S[:, h, :])
                k.tt(vnz[cs_, :, :].re("p h d -> p (h d)"), u[cs_, :, :].re("p h d -> p (h d)"), vnp[cs_, :], ALU.subtract)
                pump(2)
                for h in range(4):
                    o_ = oTp[:, h * 128 + c * 64:h * 128 + (c + 1) * 64]
                    k.mm(o_, S[:, h, :], qd[:, h, cs_], start=True, stop=False)
                    k.mm(o_, vnz[:, h, :], Aqk[:, h, cs_], start=False, stop=True)
                dSp = pc.next()
                for h in range(4):
                    k.mm(dSp[:, h * 128:(h + 1) * 128], kdec[:, h, :], vnz[:, h, :])
                for h in range(4):
                    k.stt(S[:, h, :], S[:, h, :], ex[:, 8 + 4 * c + h:9 + 4 * c + h], dSp[:, h * 128:(h + 1) * 128],
                          ALU.mult, ALU.add)
                pump(3)
            osb = f4['osb'].next()
            k.copy(f2(osb), oTp.v, eng='dve')
            sq = sqp.next()
            k.act(f2(sq), oTp.v, AF.Square)
            ssp = pc.next()
            for h in range(4):
                k.mm(ssp[:, h * 128:(h + 1) * 128], ones_bf.v, sq[:, h, :])
            r1 = f4['r1'].next()
            k.act(f2(r1), ssp.v, AF.Ln, scale=1.0 / 128, bias=EPS)
            r2 = f4['r2'].next()
            k.act(f2(r2), f2(r1), AF.Exp, scale=-0.5)
            o2 = f4['o2'].next()
            k.tt(f2(o2), f2(osb), f2(r2), ALU.mult)
            k.stt(ost[:, :, blk], o2.v, gn[:, 0:1], gs4[:, :, blk], ALU.mult, ALU.mult)
            if tb == 3:
                k.dma(fm_view(OMIX, ti, 0, 4), ost.v, q='pool')

        blocks = [(ti, tb) for ti in range(NTT) for tb in range(4)]
        outs = [dict() for _ in blocks]
        g0 = prep(blocks[0][0], blocks[0][1], outs[0])
        for _ in g0:
            pass
        for bi, (ti, tb) in enumerate(blocks):
            nxt = prep(blocks[bi + 1][0], blocks[bi + 1][1], outs[bi + 1]) if bi + 1 < len(blocks) else None

            def pump(n, nxt=nxt):
                if nxt is None:
                    return
                for _ in range(n):
                    try:
                        next(nxt)
                    except StopIteration:
                        return
            chain(ti, tb, outs[bi], pump)
            if nxt is not None:
                for _ in nxt:
                    pass
            outs[bi].clear()
        k.phase_end()

    def phase_sb(l, i):
        k.phase_begin()
        Wk = SB_W
        zp = psum_pool(2, (128, Wk), F32, 'z')
        atp = psum_pool(2, (128, Wk), BF16, 'aT')
        op_ = psum_pool(2, (128, 512), F32, 'oT')
        qp = sb_pool(2, [64, T], BF16, 'qh')
        kp = sb_pool(2, [64, T], BF16, 'kh')
        vp = sb_pool(2, [128, NT, 64], BF16, 'vh')
        ohp = sb_pool(2, [64, T], BF16, 'oh')
        ep = sb_pool(4, [128, Wk], F32, 'e')
        spp = sb_pool(3, [128, Wk], F32, 'sp')
        gp = sb_pool(3, [128, Wk + 1], F32, 'G')
        for b_ in gp.bufs:
            k.memset(b_[:, 0:1], 0.0)
        pp_ = sb_pool(2, [128, Wk], BF16, 'p')
        ap_ = sb_pool(4, [128, Wk], BF16, 'a')
        atsp = sb_pool(3, [128, Wk], BF16, 'aTs')
        bp = sb_pool(10, [128, 1], F32, 'bias')
        mkc = k.sb([64, 3, 128], BF16, 'mkc')
        k.copy(mkc[:, 0, :], C('identS')[0:64, :], eng='dve')
        k.copy(mkc[:, 1, :], C('mk1')[0:64, :], eng='dve')
        k.copy(mkc[:, 2, :], C('mk2')[0:64, :], eng='dve')
        tiles = []
        for h in range(8):
            for qb in range(NT):
                t0 = qb * 128
                nkt = (t0 + 128 + Wk - 1) // Wk
                for kt in reversed(range(nkt)):
                    tiles.append(dict(h=h, qb=qb, t0=t0, k0=kt * Wk, w=min(Wk, t0 + 128 - kt * Wk),
                                      diag=(kt == nkt - 1), lastq=(kt == 0), idx=len(tiles)))
        heads = {}

        def load_head(h):
            if h in heads or h >= 8:
                return
            qh = qp.next()
            k.dma(qh.v, SQ.whole(SQ.full[h * 64:(h + 1) * 64, :]), reads=SQ.tiles)
            kh = kp.next()
            k.dma(kh.v, SK.whole(SK.full[h * 64:(h + 1) * 64, :]), reads=SK.tiles)
            vh = vp.next()
            vstep = min(8, NT)
            for n0 in range(0, NT, vstep):
                k.dma(vh[:, n0:n0 + vstep, :],
                      SV.whole(SV.full[n0 * 128:(n0 + vstep) * 128, h * 64:(h + 1) * 64].rearrange("(n p) d -> p n d", p=128)),
                      reads=SV.tiles)
            heads[h] = dict(qh=qh, kh=kh, vh=vh, oh=ohp.next())

        def S1(t):
            h = t['h']
            if h not in heads:
                load_head(h)
            hd = heads[h]
            w, k0, t0 = t['w'], t['k0'], t['t0']
            z = zp.next()
            wm = w - 128 if t['diag'] else w
            for c0 in range(0, wm, 512):
                cw = min(512, wm - c0)
                k.mm(z[:, c0:c0 + cw], hd['qh'][:, t0:t0 + 128], hd['kh'][:, k0 + c0:k0 + c0 + cw])
            if t['diag']:
                k.mm(z[:, wm:w], hd['qh'][:, t0:t0 + 128], hd['kh'][:, k0 + wm:k0 + w], start=True, stop=False)
                k.mm(z[:, wm:w], ident_bf[0:64, :], mkc[:, 1, :], start=False, stop=False)
                k.mm(z[:, wm:w], mkc[:, 0, :], mkc[:, 2, :], start=False, stop=True)
            e = ep.next()
            k.act(e[:, 0:w], z[:, 0:w], AF.Exp)
            sp = spp.next()
            k.act(sp[:, 0:w], e[:, 0:w], AF.Ln, bias=1.0)
            t['e'] = e
            t['sp'] = sp

        def S2(t):
            w = t['w']
            G = gp.next()
            k.scan(G[:, 1:w + 1], ones_f[:, 0:w], t['sp'][:, 0:w], 0.0, ALU.mult, ALU.add)
            t['G'] = G

        def S3(t):
            w = t['w']
            G = t['G']
            bias = bp.next()
            if t['diag']:
                k.act(bias.v, G[:, w:w + 1], AF.Identity, scale=-1.0)
            else:
                k.act(bias.v, G[:, w:w + 1], AF.Identity, scale=-1.0, bias=tiles[t['idx'] - 1]['bias'][:, 0:1])
            t['bias'] = bias
            p = pp_.next()
            k.act(p[:, 0:w], G[:, 0:w], AF.Exp, bias=bias[:, 0:1])
            a = ap_.next()
            k.tt(a[:, 0:w], t['e'][:, 0:w], p[:, 0:w], ALU.mult, eng='pool')
            t['a'] = a

        def S4(t):
            w = t['w']
            aT = atp.next()
            for sb in range(w // 128):
                k.transpose(aT[:, sb * 128:(sb + 1) * 128], t['a'][:, sb * 128:(sb + 1) * 128], ident_bf.v)
            aTs = atsp.next()
            k.copy(aTs[:, 0:w], aT[:, 0:w], eng='dve')
            t['aTs'] = aTs

        def S5(t):
            w, k0, t0 = t['w'], t['k0'], t['t0']
            hd = heads[t['h']]
            if t['diag'] and t['qb'] == 0:
                load_head(t['h'] + 1)
            if t['diag']:
                t['oT'] = op_.next()
            else:
                t['oT'] = tiles[t['idx'] - 1]['oT']
            oT = t['oT']
            nsb = w // 128
            for sb in range(nsb):
                k.mm(oT[0:64, 0:128], hd['vh'][:, k0 // 128 + sb, :], t['aTs'][:, sb * 128:(sb + 1) * 128],
                     start=(t['diag'] and sb == 0), stop=(t['lastq'] and sb == nsb - 1))
            if t['lastq']:
                k.copy(hd['oh'][:, t0:t0 + 128], oT[0:64, 0:128], eng='act')
                if t['qb'] == NT - 1:
                    h = t['h']
                    k.dma(OMIX.whole(OMIX.full[512 + h * 64:512 + (h + 1) * 64, :]), hd['oh'].v, q='sp', writes=OMIX.tiles)
            for key in ('e', 'sp', 'G', 'a', 'aTs'):
                t.pop(key, None)

        n = len(tiles)
        for s_ in range(n + 6):
            if 0 <= s_ - 4 < n:
                S4(tiles[s_ - 4])
            if 0 <= s_ - 2 < n:
                S3(tiles[s_ - 2])
            if 0 <= s_ - 1 < n:
                S2(tiles[s_ - 1])
            if s_ < n:
                S1(tiles[s_])
            if 0 <= s_ - 5 < n:
                S5(tiles[s_ - 5])
        k.phase_end()

    def phase_p1_ret(l, i, xsrc):
        k.phase_begin()
        pp = psum_pool(8)
        W = k.sb([128, 8, RET_IN], BF16, 'Wret')
        for kc in range(8):
            k.dma(W[:, kc, :], RWIb[i][kc * 128:(kc + 1) * 128, :])
        xp = sb_pool(1, [128, 8, 512], F32, 'xt')
        hp = sb_pool(1, [128, 8, 512], BF16, 'hT')
        csp = sb_pool(2, [128, 2, 512], F32, 'cs')
        tp = sb_pool(4, [128, 512], F32, 'rt')
        qst = sb_pool(2, [128, 8, 512], BF16, 'qst')
        kst = qst
        gst = sb_pool(1, [128, 8, 512], BF16, 'gst')
        vst = sb_pool(1, [128, 2, 2048], BF16, 'vst')
        norm_tile = make_norm(pp)
        for ti in range(NTT):
            xt = xp.next()
            k.dma(xt.v, fm_view(xsrc, ti, 0, 8))
            cs = csp.next()
            k.dma(cs.v, fm_view(ROPE, ti, 0, 2))
            hT = hp.next()
            norm_tile(xt, A_m[l], modc[l][:, 0:8], hT)
            cos = cs[:, 0, :]
            sin = cs[:, 1, :]
            for which in range(2):
                st = (qst if which == 0 else kst).next()
                for h in range(4):
                    c0 = which * 1024 + h * 256
                    p1 = pp.next()
                    p2 = pp.next()
                    for kc in range(8):
                        k.mm(p1.v, W[:, kc, c0:c0 + 128], hT[:, kc, :], start=(kc == 0), stop=(kc == 7))
                    for kc in range(8):
                        k.mm(p2.v, W[:, kc, c0 + 128:c0 + 256], hT[:, kc, :], start=(kc == 0), stop=(kc == 7))
                    sc = 1.0 if which == 0 else 1.0 / 16.0
                    t1 = tp.next()
                    t2 = tp.next()
                    t3 = tp.next()
                    t4 = tp.next()
                    k.stt(t1.v, p1.v, sc, cos, ALU.mult, ALU.mult)
                    k.stt(t2.v, p2.v, sc, sin, ALU.mult, ALU.mult)
                    k.stt(t3.v, p1.v, sc, sin, ALU.mult, ALU.mult)
                    k.stt(t4.v, p2.v, sc, cos, ALU.mult, ALU.mult)
                    k.tt(st[:, 2 * h, :], t1.v, t2.v, ALU.subtract, eng='pool')
                    k.tt(st[:, 2 * h + 1, :], t3.v, t4.v, ALU.add, eng='pool')
                k.dma(fm_view(RQ if which == 0 else RK, ti, 0, 8), st.v, q='pool')
            for half in range(2):
                g_ = gst.next()
                for cl in range(8):
                    c = half * 8 + cl
                    ps = pp.next()
                    c0 = 4096 + c * 128
                    for kc in range(8):
                        k.mm(ps.v, W[:, kc, c0:c0 + 128], hT[:, kc, :], start=(kc == 0), stop=(kc == 7))
                    k.act(g_[:, cl, :], ps.v, AF.Silu)
                k.dma(fm_view(RG, ti, half * 1024, 8), g_.v, q='pool')
            for half in range(2):
                v_ = vst.next()
                for tl in range(2):
                    tb = half * 2 + tl
                    for nb in range(4):
                        ps = pp.next()
                        c0 = 2048 + nb * 512
                        for kc in range(8):
                            k.mm(ps.v, hT[:, kc, tb * 128:(tb + 1) * 128], W[:, kc, c0:c0 + 512], start=(kc == 0), stop=(kc == 7))
                        k.copy(v_[:, tl, nb * 512:(nb + 1) * 512], ps.v, eng='dve' if nb % 2 else 'act')
                bt = RV.tile(ti)
                k.dma(View(bt, bt.ap[half * 256:(half + 1) * 256, :].rearrange("(t p) c -> p t c", p=128)), v_.v, q='pool')
        k.phase_end()

    def phase_ret(l, i):
        k.phase_begin()
        pt4p = psum_pool(1, (128, 512), F32, 'pt4')
        trpp = psum_pool(1, (128, 512), F32, 'trp')
        otp = psum_pool(4, (128, 512), F32, 'oTp')
        pp2 = psum_pool(2, (128, 512), F32, 'dS')
        Sst = [k.sb([128, 1024], F32, f'S{h}') for h in range(4)]
        Sb = [k.sb([128, 2, 512], BF16, f'Sb{h}') for h in range(4)]
        for h in range(4):
            k.memset(Sst[h].v, 0.0)
            k.memset(Sb[h].v, 0.0, eng='dve')
        qp = sb_pool(2, [128, 8, 512], BF16, 'q8')
        kp = sb_pool(2, [128, 8, 512], BF16, 'k8')
        vp = sb_pool(2, [128, 4, 2048], BF16, 'v4')
        gp = sb_pool(1, [128, 16, 512], BF16, 'g16')
        osp = sb_pool(2, [128, 16, 512], BF16, 'o16')
        P4p = sb_pool(2, [128, 4, 128], BF16, 'P4')
        kd4p = sb_pool(2, [128, 4, 256], BF16, 'kd4')
        qd4p = sb_pool(2, [128, 8, 128], BF16, 'qd4')
        sqp = sb_pool(2, [128, 4, 512], BF16, 'rsq')
        rp = sb_pool(4, [128, 512], F32, 'rr')
        o2p = sb_pool(4, [128, 4, 128], F32, 'o2')
        kds = C('kds')
        for ti in range(NTT):
            q8 = qp.next()
            k.dma(q8.v, fm_view(RQ, ti, 0, 8))
            k8 = kp.next()
            k.dma(k8.v, fm_view(RK, ti, 0, 8))
            v4 = vp.next()
            k.dma(v4.v, tm_view(RV, ti, 0, 2048))
            g16 = gp.next()
            k.dma(g16.v, fm_view(RG, ti, 0, 16))
            ost = osp.next()
            for tb in range(4):
                blk = slice(tb * 128, (tb + 1) * 128)
                PT4 = pt4p.next()
                trp = trpp.next()
                tv = trp.v.bitcast(BF16)
                for h in range(4):
                    hs = slice(h * 128, (h + 1) * 128)
                    k.mm(PT4[:, hs], k8[:, 2 * h, blk], q8[:, 2 * h, blk], start=True, stop=False)
                    k.mm(PT4[:, hs], k8[:, 2 * h + 1, blk], q8[:, 2 * h + 1, blk], start=False, stop=True)
                for h in range(4):
                    for d in range(2):
                        k.transpose(tv[:, h * 256 + d * 128:h * 256 + (d + 1) * 128], k8[:, 2 * h + d, blk], ident_bf.v)
                P4 = P4p.next()
                kd4 = kd4p.next()
                qd4 = qd4p.next()
                for h in range(4):
                    k.tt(P4[:, h, :], PT4[:, h * 128:(h + 1) * 128], C(f'DT{h}'), ALU.mult)
                for h in range(4):
                    k.ts(kd4[:, h, :], tv[:, h * 256:(h + 1) * 256], kds[:, h:h + 1], ALU.mult)
                for h in range(4):
                    for d in range(2):
                        k.tt(qd4[:, 2 * h + d, :], q8[:, 2 * h + d, blk], C(f'QD{h}'), ALU.mult, eng='pool')
                oTps = []
                for h in range(4):
                    oTp = otp.next()
                    for dvc in range(4):
                        o_ = oTp[:, dvc * 128:(dvc + 1) * 128]
                        k.mm(o_, v4[:, tb, h * 512 + dvc * 128:h * 512 + (dvc + 1) * 128], P4[:, h, :], start=True, stop=False)
                        k.mm(o_, Sb[h][:, 0, dvc * 128:(dvc + 1) * 128], qd4[:, 2 * h, :], start=False, stop=False)
                        k.mm(o_, Sb[h][:, 1, dvc * 128:(dvc + 1) * 128], qd4[:, 2 * h + 1, :], start=False, stop=True)
                    oTps.append(oTp)
                for h in range(4):
                    for d in range(2):
                        dSp = pp2.next()
                        k.mm(dSp.v, kd4[:, h, d * 128:(d + 1) * 128], v4[:, tb, h * 512:(h + 1) * 512])
                        k.stt(Sst[h][:, d * 512:(d + 1) * 512], Sst[h][:, d * 512:(d + 1) * 512],
                              float(np.exp(RET_LG[h] * 128.0)), dSp.v, ALU.mult, ALU.add)
                    k.copy(Sb[h].v.re("p a b -> p (a b)"), Sst[h].v, eng='act')
                sq = sqp.next()
                for h in range(4):
                    k.act(sq[:, h, :], oTps[h].v, AF.Square)
                ssp = pt4p.next()
                for h in range(4):
                    for dvc in range(4):
                        k.mm(ssp[:, h * 128:(h + 1) * 128], ones_bf.v, sq[:, h, dvc * 128:(dvc + 1) * 128], start=(dvc == 0), stop=(dvc == 3))
                r1 = rp.next()
                k.act(r1.v, ssp.v, AF.Ln, scale=1.0 / 512, bias=EPS)
                r2 = rp.next()
                k.act(r2.v, r1.v, AF.Exp, scale=-0.5)
                for h in range(4):
                    o2 = o2p.next()
                    r2b = View(r2, r2.ap[:, h * 128:(h + 1) * 128].unsqueeze(1).broadcast_to([128, 4, 128]))
                    k.tt(o2.v, oTps[h].v.re("p (c t) -> p c t", c=4), r2b, ALU.mult)
                    k.tt(ost[:, 4 * h:4 * h + 4, blk], o2.v, g16[:, 4 * h:4 * h + 4, blk], ALU.mult, eng='pool')
            k.dma(fm_view(RO, ti, 0, 16), ost.v, q='pool')
        k.phase_end()

    def phase_out(l, i, hyb, xsrc):
        k.phase_begin()
        pp = psum_pool(4)
        if hyb:
            wA = k.sb([128, 8, D], BF16, 'wA')
            for c in range(0, 8, 4):
                k.dma(wA[:, c:c + 4, :], View(HWOb[i], HWOb[i].ap[c * 128:(c + 4) * 128, :].rearrange("(c p) n -> p c n", p=128)))
            oap = sb_pool(2, [128, 8, 512], BF16, 'oa')
        else:
            wR = k.sb([128, 16, D], BF16, 'wR')
            for c in range(0, 16, 4):
                k.dma(wR[:, c:c + 4, :], View(RWOb[i], RWOb[i].ap[c * 128:(c + 4) * 128, :].rearrange("(c p) n -> p c n", p=128)))
            oap = sb_pool(2, [128, 16, 512], BF16, 'oa')
        xp = sb_pool(2, [128, 8, 512], F32, 'xt')
        gt = modc[l][:, 16:24]
        for ti in range(NTT):
            xt = xp.next()
            k.dma(xt.v, fm_view(xsrc, ti, 0, 8))
            oa = oap.next()
            if hyb:
                k.dma(oa.v, fm_view(OMIX, ti, 0, 8))
            else:
                k.dma(oa.v, fm_view(RO, ti, 0, 16))
            for dc in range(8):
                ps = pp.next()
                ds_ = slice(dc * 128, (dc + 1) * 128)
                if hyb:
                    for c in range(8):
                        k.mm(ps.v, wA[:, c, ds_], oa[:, c, :], start=(c == 0), stop=(c == 7))
                else:
                    for c in range(16):
                        k.mm(ps.v, wR[:, c, ds_], oa[:, c, :], start=(c == 0), stop=(c == 15))
                k.stt(xt[:, dc, :], ps.v, gt[:, dc:dc + 1], xt[:, dc, :], ALU.mult, ALU.add)
            k.dma(fm_view(XT, ti, 0, 8), xt.v, q='pool')
        k.phase_end()

    def phase_ffn(l, xdst):
        k.phase_begin()
        pp = psum_pool(8)
        W1 = k.sb([128, 8, 2 * FFN_H], BF16, 'W1')
        for kc in range(8):
            k.dma(W1[:, kc, :], FWIb[l][kc * 128:(kc + 1) * 128, :])
        w2p = sb_pool(2, [128, 22, 128], BF16, 'w2')
        xp = sb_pool(2, [128, 8, 512], F32, 'xt')
        hp = sb_pool(2, [128, 8, 512], BF16, 'hT')
        ap_ = sb_pool(1, [128, 22, 512], BF16, 'actT')
        sp_ = sb_pool(3, [128, 512], F32, 'sl')
        norm_tile = make_norm(pp)
        gt = modc[l][:, 40:48]
        for ti in range(NTT):
            xt = xp.next()
            k.dma(xt.v, fm_view(XT, ti, 0, 8))
            hT = hp.next()
            norm_tile(xt, A_f[l], modc[l][:, 24:32], hT)
            aT = ap_.next()
            for j in range(22):
                pg = pp.next()
                pu = pp.next()
                for kc in range(8):
                    k.mm(pg.v, W1[:, kc, j * 128:(j + 1) * 128], hT[:, kc, :], start=(kc == 0), stop=(kc == 7))
                for kc in range(8):
                    k.mm(pu.v, W1[:, kc, FFN_H + j * 128:FFN_H + (j + 1) * 128], hT[:, kc, :], start=(kc == 0), stop=(kc == 7))
                s = sp_.next()
                k.act(s.v, pg.v, AF.Silu)
                k.tt(aT[:, j, :], s.v, pu.v, ALU.mult)
            for dc in range(8):
                w2 = w2p.next()
                k.dma(w2.v, View(FWOb[l], FWOb[l].ap[dc]))
                ps = pp.next()
                for j in range(22):
                    k.mm(ps.v, w2[:, j, :], aT[:, j, :], start=(j == 0), stop=(j == 21))
                k.stt(xt[:, dc, :], ps.v, gt[:, dc:dc + 1], xt[:, dc, :], ALU.mult, ALU.add)
            k.dma(fm_view(xdst, ti, 0, 8), xt.v, q='pool')
        k.phase_end()

    if want('cast'):
        phase_cast()
    if want('ada'):
        phase_ada()
    for l in range(n_layers):
        i = l // 2
        xsrc = XT_in if l == 0 else XT
        xdst = OUT if l == n_layers - 1 else XT
        if l % 2 == 0:
            if want(f'p1_{l}'):
                phase_p1_hyb(l, i, xsrc)
            if want(f'gdn_{l}'):
                phase_gdn(l, i)
            if want(f'sb_{l}'):
                phase_sb(l, i)
            if want(f'out_{l}'):
                phase_out(l, i, True, xsrc)
        else:
            if want(f'p1_{l}'):
                phase_p1_ret(l, i, xsrc)
            if want(f'ret_{l}'):
                phase_ret(l, i)
            if want(f'out_{l}'):
                phase_out(l, i, False, xsrc)
        if want(f'ffn_{l}'):
            phase_ffn(l, xdst)
    k.barrier()
    k.finalize()
    return nc, k


def host_inputs(b, T, x, c, ada_w, ada_b, norm_mix, norm_ffn, hyb_w_in, hyb_conv, gdn_a_log, gdn_dt_bias,
                gdn_norm, sb_q_norm, sb_k_norm, hyb_w_out, ret_w_in, ret_w_out, ffn_w_in, ffn_w_out, shared):
    f = np.float32
    m = dict(shared)
    m["xT"] = np.ascontiguousarray(x[b, :T].T)
    m["c_col"] = np.ascontiguousarray(c[b].reshape(8, 128).T)
    return m


def host_shared(T, ada_w, ada_b, norm_mix, norm_ffn, hyb_w_in, hyb_conv, gdn_a_log, gdn_dt_bias,
                gdn_norm, sb_q_norm, sb_k_norm, hyb_w_out, ret_w_in, ret_w_out, ffn_w_in, ffn_w_out):
    ca = np.ascontiguousarray
    m = {}
    m["ada_w"] = ca(ada_w)
    m["ada_b_col"] = ca(ada_b.reshape(4, 48, 128).transpose(0, 2, 1))
    m["norm_mix_col"] = ca(norm_mix.reshape(4, 8, 128).transpose(0, 2, 1))
    m["norm_ffn_col"] = ca(norm_ffn.reshape(4, 8, 128).transpose(0, 2, 1))
    m["hyb_w_in"] = ca(hyb_w_in)
    m["conv_col"] = ca(hyb_conv.reshape(2, 4, 12, 128).transpose(0, 3, 2, 1))
    m["a_log_b"] = ca(np.broadcast_to(gdn_a_log[:, None, :], (2, 128, 4)))
    m["dt_bias_b"] = ca(np.broadcast_to(gdn_dt_bias[:, None, :], (2, 128, 4)))
    m["gdn_norm_col"] = ca(gdn_norm.reshape(2, 128, 1))
    m["sb_q_norm_col"] = ca(np.concatenate([sb_q_norm, sb_q_norm], axis=1).reshape(2, 128, 1))
    m["sb_k_norm_col"] = ca(np.concatenate([sb_k_norm, sb_k_norm], axis=1).reshape(2, 128, 1))
    m["hyb_w_out"] = ca(hyb_w_out)
    m["ret_w_in"] = ca(ret_w_in)
    m["ret_w_out"] = ca(ret_w_out)
    m["ffn_w_in"] = ca(ffn_w_in)
    m["ffn_w_out"] = ca(ffn_w_out)
    m["cst"] = make_consts()
    m["rope"] = ca(make_rope(T).reshape(256, T))
    return m


_CACHE = {}


def kernel(x, c, ada_w, ada_b, norm_mix, norm_ffn, hyb_w_in, hyb_conv, gdn_a_log, gdn_dt_bias,
           gdn_norm, sb_q_norm, sb_k_norm, hyb_w_out, ret_w_in, ret_w_out, ffn_w_in, ffn_w_out):
    args = [np.asarray(a, dtype=np.float32) for a in
            (x, c, ada_w, ada_b, norm_mix, norm_ffn, hyb_w_in, hyb_conv, gdn_a_log, gdn_dt_bias,
             gdn_norm, sb_q_norm, sb_k_norm, hyb_w_out, ret_w_in, ret_w_out, ffn_w_in, ffn_w_out)]
    x, c = args[0], args[1]
    B, T, _ = x.shape
    shared = host_shared(T, *args[2:])
    in_maps = [host_inputs(b, T, *args, shared) for b in range(B)]
    if T not in _CACHE:
        _CACHE[T] = build(T)[0]
    nc = _CACHE[T]
    res = run_bass_kernel_spmd(nc, in_maps, core_ids=list(range(B)))
    out = np.stack([np.asarray(r["outT"]).T for r in res.results], axis=0)
    return np.ascontiguousarray(out.astype(np.float32))
```
